# Optimizing a Trainium2 kernel written in Bass

```python
import jax, jax.numpy as jnp
from jax import lax
import numpy as np

D_MODEL = 1024
BATCH = 8
SEQ = 4096
DEPTH = 1

CHUNK = 64
Q_BLOCK = 128
MLA_HEADS = 8
QK_NOPE_DIM = 64
QK_ROPE_DIM = 32
V_HEAD_DIM = 64
Q_LORA_RANK = 384
KV_LORA_RANK = 256
MLA_WIDTH = MLA_HEADS * V_HEAD_DIM
ROPE_THETA = 10000.0
POOL_WINDOWS = (2, 4, 8, 16)
POOL_GROUPS = len(POOL_WINDOWS)
POOL_WIDTH = D_MODEL // 2
POOL_GROUP_DIM = POOL_WIDTH // POOL_GROUPS
N_BRANCHES = 2
EPS = 1e-6

IN_SIZES = (Q_LORA_RANK, KV_LORA_RANK, QK_ROPE_DIM, MLA_WIDTH, POOL_WIDTH, POOL_WIDTH, N_BRANCHES * D_MODEL)
IN_SPLITS = tuple(int(s) for s in np.cumsum(IN_SIZES)[:-1])
IN_TOTAL = int(sum(IN_SIZES))

kernel_name = "hybrid_mla_pool_gated_block"


def rms_norm(x, g):
    xf = x.astype(jnp.float32)
    y = xf * lax.rsqrt(jnp.mean(xf * xf, axis=-1, keepdims=True) + EPS)
    return (y * g.astype(jnp.float32)).astype(x.dtype)


def rope_tables(seq):
    half = QK_ROPE_DIM // 2
    inv_freq = ROPE_THETA ** (-jnp.arange(half, dtype=jnp.float32) / half)
    ang = jnp.arange(seq, dtype=jnp.float32)[:, None] * inv_freq[None, :]
    return jnp.cos(ang), jnp.sin(ang)


def apply_rope(x, cos, sin):
    xf = x.astype(jnp.float32)
    x1, x2 = jnp.split(xf, 2, axis=-1)
    out = jnp.concatenate([x1 * cos - x2 * sin, x1 * sin + x2 * cos], axis=-1)
    return out.astype(x.dtype)


def mla_attention(zq, zkv, zkr, q_norm, w_uq, kv_norm, w_ukv):
    b, s, _ = zq.shape
    cos, sin = rope_tables(s)
    c_q = rms_norm(zq, q_norm)
    q = jnp.einsum('bsr,rhd->bshd', c_q, w_uq)
    q_nope = q[..., :QK_NOPE_DIM]
    q_rope = apply_rope(q[..., QK_NOPE_DIM:], cos[None, :, None, :], sin[None, :, None, :])
    c_kv = rms_norm(zkv, kv_norm)
    kv = jnp.einsum('bsr,rhd->bshd', c_kv, w_ukv)
    k_nope = kv[..., :QK_NOPE_DIM]
    v = kv[..., QK_NOPE_DIM:]
    k_rope = apply_rope(zkr, cos[None], sin[None])
    scale = (QK_NOPE_DIM + QK_ROPE_DIM) ** -0.5
    key_chunk = jnp.arange(s) // CHUNK
    n_blocks = s // Q_BLOCK

    def one_block(i):
        start = i * Q_BLOCK
        qn = lax.dynamic_slice_in_dim(q_nope, start, Q_BLOCK, axis=1)
        qr = lax.dynamic_slice_in_dim(q_rope, start, Q_BLOCK, axis=1)
        sc = (jnp.einsum('bqhd,bkhd->bhqk', qn, k_nope).astype(jnp.float32)
              + jnp.einsum('bqhr,bkr->bhqk', qr, k_rope).astype(jnp.float32)) * scale
        q_chunk = (start + jnp.arange(Q_BLOCK)) // CHUNK
        mask = key_chunk[None, :] <= q_chunk[:, None]
        sc = jnp.where(mask[None, None], sc, -jnp.inf)
        p = jax.nn.softmax(sc, axis=-1).astype(v.dtype)
        return jnp.einsum('bhqk,bkhd->bqhd', p, v)

    o = lax.map(one_block, jnp.arange(n_blocks))
    o = jnp.transpose(o, (1, 0, 2, 3, 4)).reshape(b, s, MLA_WIDTH)
    return o


def multiscale_pool(u, pool_w, pool_scale):
    b, s, _ = u.shape
    uf = u.astype(jnp.float32)
    csum = jnp.cumsum(uf, axis=1)
    t = jnp.arange(s)
    outs = []
    for gi, w in enumerate(POOL_WINDOWS):
        sl = slice(gi * POOL_GROUP_DIM, (gi + 1) * POOL_GROUP_DIM)
        cg = csum[..., sl]
        shifted = jnp.pad(cg, ((0, 0), (w, 0), (0, 0)))[:, :s]
        count = jnp.minimum(t + 1, w).astype(jnp.float32)[None, :, None]
        d = (cg - shifted) / count - uf[..., sl]
        outs.append(jnp.einsum('bsc,cd->bsd', d, pool_w[gi].astype(jnp.float32)))
    y = jnp.concatenate(outs, axis=-1) * pool_scale.astype(jnp.float32)
    return y.astype(u.dtype)


def setup_inputs(seed: int = 0) -> dict:
    key = jax.random.key(seed)
    ks = jax.random.split(key, 14)
    f = jnp.float32
    def nrm(k, shape, fan_in):
        return jax.random.normal(k, shape, f) * (fan_in ** -0.5)
    return {
        "x": jax.random.normal(ks[0], (BATCH, SEQ, D_MODEL), f),
        "norm_in": 1.0 + 0.02 * jax.random.normal(ks[1], (D_MODEL,), f),
        "w_in": nrm(ks[2], (D_MODEL, IN_TOTAL), D_MODEL),
        "q_norm": 1.0 + 0.02 * jax.random.normal(ks[3], (Q_LORA_RANK,), f),
        "w_uq": nrm(ks[4], (Q_LORA_RANK, MLA_HEADS, QK_NOPE_DIM + QK_ROPE_DIM), Q_LORA_RANK),
        "kv_norm": 1.0 + 0.02 * jax.random.normal(ks[5], (KV_LORA_RANK,), f),
        "w_ukv": nrm(ks[6], (KV_LORA_RANK, MLA_HEADS, QK_NOPE_DIM + V_HEAD_DIM), KV_LORA_RANK),
        "pool_w": nrm(ks[7], (POOL_GROUPS, POOL_GROUP_DIM, POOL_GROUP_DIM), POOL_GROUP_DIM),
        "pool_scale": 1.0 + 0.02 * jax.random.normal(ks[8], (POOL_WIDTH,), f),
        "w_branch_attn": nrm(ks[9], (MLA_WIDTH, D_MODEL), MLA_WIDTH),
        "w_branch_pool": nrm(ks[10], (POOL_WIDTH, D_MODEL), POOL_WIDTH),
        "w_out": nrm(ks[11], (D_MODEL, D_MODEL), D_MODEL),
        "norm_final": 1.0 + 0.02 * jax.random.normal(ks[12], (D_MODEL,), f),
    }


def reference(x, norm_in, w_in, q_norm, w_uq, kv_norm, w_ukv, pool_w, pool_scale,
              w_branch_attn, w_branch_pool, w_out, norm_final):
    h = x
    for _ in range(DEPTH):
        hn = rms_norm(h, norm_in)
        z = jnp.einsum('bsd,de->bse', hn, w_in)
        zq, zkv, zkr, g_attn, u_pool, g_pool, g_merge = jnp.split(z, IN_SPLITS, axis=-1)
        y_attn = mla_attention(zq, zkv, zkr, q_norm, w_uq, kv_norm, w_ukv) * jax.nn.silu(g_attn)
        y_pool = multiscale_pool(u_pool, pool_w, pool_scale) * jax.nn.silu(g_pool)
        a = jnp.einsum('bsc,cd->bsd', y_attn, w_branch_attn)
        p = jnp.einsum('bsc,cd->bsd', y_pool, w_branch_pool)
        gate_a, gate_p = jnp.split(jax.nn.sigmoid(g_merge.astype(jnp.float32)).astype(h.dtype), 2, axis=-1)
        merged = gate_a * a + gate_p * p
        h = h + jnp.einsum('bsd,de->bse', merged, w_out)
    return rms_norm(h, norm_final)
```

```python
import numpy as np
import concourse.bass as bass
import concourse.mybir as mybir
from concourse.bass_utils import run_bass_kernel_spmd
from contextlib import ExitStack

F32 = mybir.dt.float32
BF16 = mybir.dt.bfloat16
I32 = mybir.dt.int32
ALU = mybir.AluOpType
AF = mybir.ActivationFunctionType

D = 1024
H = 8
NOPE = 64
ROPE = 32
QKD = 96
VD = 64
QL = 384
KVL = 256
NFRONT = 2208
OFF_ZQ, OFF_ZKV, OFF_ZKR, OFF_GA, OFF_UP, OFF_GP, OFF_GM = 0, 384, 640, 672, 1184, 1696, 2208
IN_TOTAL = 4256
G = 512
EPS = 1e-6
POOL_W = (2, 4, 8, 16)
MAGIC = 0x5F3759DF
SM_SCALE = float(QKD ** -0.5)
FULL_S = 4096
NCORES = 8


class T:
    __slots__ = ("name", "w", "r", "excl")

    def __init__(self, name, excl=False):
        self.name = name
        self.w = None
        self.r = {}
        self.excl = excl


class Prog:
    COMPUTE = ("pe", "act", "dve", "pool")

    def __init__(self):
        self.ins = []
        self.bar_gen = 0
        self.bar_deps = set()
        self.eng_gen = {}
        self.last = {}
        self.dmas_since_bar = []
        self.groups = {}

    def op(self, eng, fn, reads=(), writes=(), dma=None, group=False):
        i = len(self.ins)
        deps = set()
        for t in reads:
            if t.w is not None:
                deps.add(t.w)
            if t.excl:
                for k, rid in t.r.items():
                    if k != eng:
                        deps.add(rid)
        for t in writes:
            if t.w is not None:
                deps.add(t.w)
            for rid in t.r.values():
                deps.add(rid)
        if self.eng_gen.get(eng, 0) < self.bar_gen:
            deps |= self.bar_deps
            self.eng_gen[eng] = self.bar_gen
        deps.discard(i)
        if dma is not None:
            deps = {d for d in deps if self.ins[d]["dma"] != dma}
        self.ins.append(dict(eng=eng, fn=fn, deps=deps, dma=dma, signal=dma is not None))
        if dma is not None:
            self.groups[dma] = group
            self.dmas_since_bar.append(i)
        else:
            self.last[eng] = i
        rkey = eng if dma is None else ("dma", i)
        for t in reads:
            t.r[rkey] = i
        for t in writes:
            t.w = i
            t.r = {}
        return i

    def barrier(self):
        self.bar_gen += 1
        self.bar_deps = set(self.last.values()) | set(self.dmas_since_bar)
        self.dmas_since_bar = []

    def resolve(self):
        ins = self.ins
        for it in ins:
            for d in it["deps"]:
                dd = ins[d]
                if dd["dma"] is None and dd["eng"] == it["eng"] == "pe" and it["dma"] is None:
                    continue
                dd["signal"] = True
        cnt = {}
        for it in ins:
            if it["dma"] is not None:
                k = ("dma", it["dma"])
                cnt[k] = cnt.get(k, 0) + 16
                it["sig"] = (k, cnt[k])
            elif it["signal"]:
                k = ("eng", it["eng"])
                cnt[k] = cnt.get(k, 0) + 1
                it["sig"] = (k, cnt[k])
        totals = dict(cnt)
        waited = {}
        for it in ins:
            need = {}
            for d in it["deps"]:
                dd = ins[d]
                if dd["dma"] is None and dd["eng"] == it["eng"] == "pe" and it["dma"] is None:
                    continue
                k, v = dd["sig"]
                if k[0] == "dma" and self.groups[k[1]]:
                    v = totals[k]
                if v > need.get(k, 0):
                    need[k] = v
            w = waited.setdefault(it["eng"], {})
            out = []
            for k, v in need.items():
                if w.get(k, 0) < v:
                    w[k] = v
                    out.append((k, v))
            it["waits"] = out
        return totals


def build_nc(S, stop=3):
    NG = S // G
    NT = S // 128
    nc = bass.Bass("TRN2", target_bir_lowering=False)

    def din(name, shape, dt=F32):
        return nc.dram_tensor(name, list(shape), dt, kind="ExternalInput").ap()

    def dscr(name, shape, dt=BF16):
        return nc.dram_tensor(name, list(shape), dt, kind="Internal").ap()

    x_d = din("x", [S, D])
    norm_in_d = din("norm_in", [D])
    w_in_d = din("w_in", [D, IN_TOTAL])
    q_norm_d = din("q_norm", [QL])
    w_uq_d = din("w_uq", [QL, H, QKD])
    kv_norm_d = din("kv_norm", [KVL])
    w_ukv_d = din("w_ukv", [KVL, H, 128])
    pool_w_d = din("pool_w", [4, 128, 128])
    pool_scale_d = din("pool_scale", [512])
    w_ba_d = din("w_branch_attn", [512, D])
    w_bp_d = din("w_branch_pool", [512, D])
    w_out_d = din("w_out", [D, D])
    norm_final_d = din("norm_final", [D])
    ident_d = din("c_ident", [128, 128])
    cos_d = din("c_cos", [32, S])
    sin_d = din("c_sin", [32, S])
    pcnt_d = din("c_pcnt", [4, 16])
    out_d = nc.dram_tensor("out", [S, D], F32, kind="ExternalOutput").ap()

    qt_d = dscr("s_qt", [H, QKD, S])
    kt_d = dscr("s_kt", [H, QKD, S])
    v_d = dscr("s_v", [S, 512])
    sg_d = dscr("s_sg", [512, S])
    yp_d = dscr("s_yp", [512, S])
    o_d = dscr("s_o", [512, S], F32)
    l_d = dscr("s_l", [H, S], F32)

    ARENA = 207 * 1024
    arena = nc.alloc_sbuf_tensor("arena", [128, ARENA // 2], BF16)
    arena32 = arena.bitcast(F32)
    psum = nc.alloc_psum_tensor("psum", [128, 4096], F32)
    psum16 = psum.bitcast(BF16)

    class Bump:
        def __init__(self, lo, hi):
            self.lo, self.hi, self.off = lo, hi, lo

        def reset(self):
            self.off = self.lo

        def alloc(self, dt, shape):
            sz = 4 if dt in (F32, I32) else 2
            n = int(np.prod(shape))
            nb = (n * sz + 31) // 32 * 32
            assert self.off + nb <= self.hi, ("arena overflow", self.off, nb, self.hi)
            o = self.off
            self.off += nb
            if sz == 4:
                ap = arena32[:, o // 4:o // 4 + n]
            else:
                ap = arena[:, o // 2:o // 2 + n]
            if len(shape) == 2:
                ap = ap.rearrange("p (a b) -> p a b", b=shape[1])
            elif len(shape) == 3:
                ap = ap.rearrange("p (a b c) -> p a b c", b=shape[1], c=shape[2])
            return ap

    WREG = 68 * 1024
    wmem = Bump(0, WREG)
    cmem = Bump(WREG, WREG + 10 * 1024)
    work = Bump(WREG + 10 * 1024, ARENA)

    def bank(b, n=512, parts=128, off=0):
        return psum[0:parts, b * 512 + off:b * 512 + off + n]

    PB = [T("psum%d" % b, excl=True) for b in range(8)]
    prog = Prog()
    pb_rr = [0]

    def next_bank():
        b = pb_rr[0]
        pb_rr[0] = (b + 1) % 8
        return b

    def dma(q, out, in_, key, reads=(), writes=(), group=False):
        prog.op(q, lambda e, o=out, i=in_: e.dma_start(out=o, in_=i), reads=reads, writes=writes,
                dma=key, group=group)

    def mm(out, lhsT, rhs, start, stop, reads, writes):
        prog.op("pe", lambda e, o=out, l=lhsT, r=rhs, s=start, p=stop: e.matmul(o, lhsT=l, rhs=r, start=s, stop=p),
                reads=reads, writes=writes)

    def act(out, in_, func, reads, writes, scale=1.0, accum=None):
        if accum is None:
            prog.op("act", lambda e, o=out, i=in_, f=func, s=scale: e.activation(out=o, in_=i, func=f, scale=s),
                    reads=reads, writes=writes)
        else:
            prog.op("act", lambda e, o=out, i=in_, f=func, s=scale, a=accum: e.activation(out=o, in_=i, func=f, scale=s, accum_out=a),
                    reads=reads, writes=writes)

    def tt(eng, out, in0, in1, op, reads, writes):
        prog.op(eng, lambda e, o=out, a=in0, b=in1, p=op: e.tensor_tensor(out=o, in0=a, in1=b, op=p),
                reads=reads, writes=writes)

    def ts(eng, out, in0, s1, op0, reads, writes, s2=None, op1=None):
        if op1 is None:
            prog.op(eng, lambda e, o=out, a=in0, x=s1, p=op0: e.tensor_scalar(out=o, in0=a, scalar1=x, scalar2=None, op0=p),
                    reads=reads, writes=writes)
        else:
            prog.op(eng, lambda e, o=out, a=in0, x=s1, y=s2, p=op0, q=op1: e.tensor_scalar(out=o, in0=a, scalar1=x, scalar2=y, op0=p, op1=q),
                    reads=reads, writes=writes)

    def stt(out, in0, scalar, in1, op0, op1, reads, writes):
        prog.op("dve", lambda e, o=out, a=in0, s=scalar, b=in1, p=op0, q=op1: e.scalar_tensor_tensor(out=o, in0=a, scalar=s, in1=b, op0=p, op1=q),
                reads=reads, writes=writes)

    def cp(eng, out, in_, reads, writes):
        if eng == "act":
            prog.op("act", lambda e, o=out, i=in_: e.copy(out=o, in_=i), reads=reads, writes=writes)
        else:
            prog.op(eng, lambda e, o=out, i=in_: e.tensor_copy(out=o, in_=i), reads=reads, writes=writes)

    def memset(eng, ap, val, writes):
        prog.op(eng, lambda e, a=ap, v=val: e.memset(a, v), writes=writes)

    def rsqrt_newton(v_ap, y_ap, t_ap, Tv, Ty, Tt, iters):
        ts("dve", t_ap.bitcast(I32), v_ap.bitcast(I32), 1, ALU.logical_shift_right, [Tv], [Tt])
        ts("dve", y_ap.bitcast(I32), t_ap.bitcast(I32), -1, ALU.mult, [Tt], [Ty], s2=MAGIC, op1=ALU.add)
        for _ in range(iters):
            stt(t_ap, y_ap, -0.5, y_ap, ALU.mult, ALU.mult, [Ty], [Tt])
            tt("dve", t_ap, t_ap, v_ap, ALU.mult, [Tt, Tv], [Tt])
            stt(y_ap, t_ap, 1.5, y_ap, ALU.add, ALU.mult, [Tt, Ty], [Ty])

    ident = cmem.alloc(BF16, [128])
    Tident = T("ident")
    gcol = cmem.alloc(F32, [8])
    Tgin = T("gcol")
    gfin = cmem.alloc(F32, [D])
    Tgfin = T("gfin")
    qn = cmem.alloc(F32, [3])
    kvn = cmem.alloc(F32, [2])
    psc = cmem.alloc(F32, [4])
    Tvec = T("vecs")
    rstd_all = cmem.alloc(F32, [NT])
    Trstd = [T("rstd%d" % g) for g in range(NG)]
    ones_bf = cmem.alloc(BF16, [128])
    half_f = cmem.alloc(F32, [64])
    Tones = T("ones")
    pcnt = cmem.alloc(F32, [4, 16])
    Tpcnt = T("pcnt")

    dma("pool", ident, ident_d, "wI", writes=[Tident], group=True)
    for k in range(8):
        dma("sp", gcol[:, k:k + 1], norm_in_d[k * 128:(k + 1) * 128].rearrange("(p o) -> p o", o=1), "c0", writes=[Tgin], group=True)
    dma("sp", gfin, norm_final_d.partition_broadcast(128), "c0", writes=[Tgfin], group=True)
    for c in range(3):
        dma("sp", qn[:, c:c + 1], q_norm_d[c * 128:(c + 1) * 128].rearrange("(p o) -> p o", o=1), "c0", writes=[Tvec], group=True)
    for c in range(2):
        dma("sp", kvn[:, c:c + 1], kv_norm_d[c * 128:(c + 1) * 128].rearrange("(p o) -> p o", o=1), "c0", writes=[Tvec], group=True)
    for c in range(4):
        dma("sp", psc[:, c:c + 1], pool_scale_d[c * 128:(c + 1) * 128].rearrange("(p o) -> p o", o=1), "c0", writes=[Tvec], group=True)
    dma("sp", pcnt.rearrange("p a b -> p (a b)"), pcnt_d.rearrange("a b -> (a b)").partition_broadcast(128), "c0",
        writes=[Tpcnt], group=True)
    ts("dve", gcol, gcol, 32.0, ALU.mult, [Tgin], [Tgin])
    ts("dve", gfin, gfin, 32.0, ALU.mult, [Tgfin], [Tgfin])
    ts("dve", qn, qn, float(np.sqrt(QL)), ALU.mult, [Tvec], [Tvec])
    ts("dve", kvn, kvn, 16.0, ALU.mult, [Tvec], [Tvec])
    for gi, w in enumerate(POOL_W):
        ts("dve", psc[:, gi:gi + 1], psc[:, gi:gi + 1], 0.5 / w, ALU.mult, [Tvec], [Tvec])
    memset("dve", ones_bf, 1.0, [Tones])
    memset("dve", half_f, 0.5, [Tones])

    w_f = wmem.alloc(BF16, [8, NFRONT])
    w_krs = wmem.alloc(BF16, [8, 96])
    w_q = wmem.alloc(BF16, [3, H, QKD])
    w_qs = wmem.alloc(BF16, [3, H, QKD])
    w_kv = wmem.alloc(BF16, [2, 1024])
    w_pl = wmem.alloc(BF16, [4, 128])
    TW1 = T("w1")
    TWA = T("wA")
    TWB = T("wB")
    memset("pool", w_krs[:, :, 0:64], 0.0, [TWA])
    memset("pool", w_qs[:, :, :, 0:64], 0.0, [TW1])
    w_in_v = w_in_d.rearrange("(k p) c -> p k c", p=128)
    for k in range(8):
        dma("pool", w_f[:, k, 0:OFF_GA], w_in_v[:, k, 0:OFF_GA], "wA", writes=[TWA], group=True)
    dma("pool", w_krs[:, :, 64:80], w_in_v[:, :, OFF_ZKR + 16:OFF_ZKR + 32], "wA", writes=[TWA], group=True)
    dma("pool", w_krs[:, :, 80:96], w_in_v[:, :, OFF_ZKR:OFF_ZKR + 16], "wA", writes=[TWA], group=True)
    for k in range(8):
        dma("pool", w_f[:, k, OFF_GA:NFRONT], w_in_v[:, k, OFF_GA:NFRONT], "wB", writes=[TWB], group=True)
    w_uq_v = w_uq_d.rearrange("(k p) h c -> p k h c", p=128)
    for k in range(3):
        dma("pool", w_q[:, k, :, :], w_uq_v[:, k, :, :], "w1", writes=[TW1], group=True)
        dma("pool", w_qs[:, k, :, 64:80], w_uq_v[:, k, :, 80:96], "w1", writes=[TW1], group=True)
        dma("pool", w_qs[:, k, :, 80:96], w_uq_v[:, k, :, 64:80], "w1", writes=[TW1], group=True)
    w_ukv_v = w_ukv_d.rearrange("(k p) h c -> p k (h c)", p=128)
    for k in range(2):
        dma("pool", w_kv[:, k, :], w_ukv_v[:, k, :], "w1", writes=[TW1], group=True)
    dma("pool", w_pl, pool_w_d.rearrange("g c d -> c g d"), "w1", writes=[TW1], group=True)
    for k in range(8):
        ts("dve", w_f[:, k, 0:OFF_GA], w_f[:, k, 0:OFF_GA], gcol[:, k:k + 1], ALU.mult, [TWA, Tgin], [TWA])
        ts("dve", w_krs[:, k, 64:96], w_krs[:, k, 64:96], gcol[:, k:k + 1], ALU.mult, [TWA, Tgin], [TWA])

    def fold_wB():
        for k in range(8):
            ts("dve", w_f[:, k, OFF_GA:NFRONT], w_f[:, k, OFF_GA:NFRONT], gcol[:, k:k + 1], ALU.mult, [TWB, Tgin], [TWB])
    xt = [wmem.alloc(F32, [D]) for _ in range(4)]

    work.reset()
    NXS = 4
    Txt = [T("xt%d" % i) for i in range(NXS)]
    junk = [work.alloc(BF16, [D]) for _ in range(2)]
    Tjunk = [T("junk%d" % i) for i in range(2)]
    ssx = work.alloc(F32, [4]); Tssx = T("ssx")
    nty = work.alloc(F32, [4]); Tnty = T("nty")
    ntt = work.alloc(F32, [4]); Tntt = T("ntt")
    hn = [work.alloc(BF16, [D]) for _ in range(2)]
    Thn = [T("hn%d" % i) for i in range(2)]
    hnTb = [work.alloc(BF16, [8, G]) for _ in range(2)]
    ThnTb = [[T("hnT%d_%d" % (s_, i)) for i in range(4)] for s_ in range(2)]
    zs = work.alloc(F32, [5, G]); Tzs = [T("zs%d" % i) for i in range(5)]
    sq = work.alloc(BF16, [5, G]); Tsq = [T("sq%d" % i) for i in range(5)]
    ssn = work.alloc(F32, [2 * G]); Tssn = T("ssn")
    nry = work.alloc(F32, [2 * G]); Tnry = T("nry")
    nrt = work.alloc(F32, [2 * G]); Tnrt = T("nrt")
    cTb = [work.alloc(BF16, [5, G]) for _ in range(2)]
    TcTb = [[T("cT%d_%d" % (s_, i)) for i in range(5)] for s_ in range(2)]
    cs = [work.alloc(F32, [G]) for _ in range(2)]
    sn = [work.alloc(F32, [G]) for _ in range(2)]
    Tcs = [T("cs%d" % i) for i in range(2)]
    r1 = [work.alloc(BF16, [G]) for _ in range(2)]
    r2 = [work.alloc(BF16, [G]) for _ in range(2)]
    Tr1 = [T("r1_%d" % i) for i in range(2)]
    Tr2 = [T("r2_%d" % i) for i in range(2)]
    wu = work.alloc(BF16, [4, G]); Twu = [T("wu%d" % c) for c in range(4)]
    pfx = work.alloc(F32, [16]); Tpfx = T("pfx")
    qst = work.alloc(BF16, [H, G]); Tqst = [T("qst%d" % h) for h in range(H)]
    kst = work.alloc(BF16, [H, G]); Tkst = [T("kst%d" % h) for h in range(H)]; Tkr = T("kstrope")
    vst = work.alloc(BF16, [4, 512]); Tvst = [T("vst%d" % i) for i in range(4)]
    _th = work.alloc(BF16, [2, G]); _Tth = [T("th%d" % i) for i in range(2)]

    class _Rot:
        def __init__(self, a, n): self.a, self.n = a, n
        def __getitem__(self, i): return self.a[i % self.n]
    Tth = _Rot(_Tth, 2)

    class _RotAP:
        def __getitem__(self, key):
            p, c, f = key
            return _th[p, c % 2, f]
    th = _RotAP()
    sgst = work.alloc(BF16, [4, G]); Tsgst = [T("sgst%d" % i) for i in range(4)]
    sgp = work.alloc(BF16, [4, G]); Tsgp = [T("sgp%d" % i) for i in range(4)]
    ub = work.alloc(F32, [4, 16 + G])
    Tub = [T("ub_%d" % c) for c in range(4)]
    Tubh = [T("ubh_%d" % c) for c in range(4)]
    ubs = work.alloc(F32, [4, 16]); Tubs = [T("ubs%d" % c) for c in range(4)]
    pa = work.alloc(F32, [16 + G]); Tpa = T("pa")
    pbuf = work.alloc(F32, [16 + G]); Tpb = T("pbuf")
    dT = work.alloc(BF16, [4, G]); TdT = [T("dT%d" % i) for i in range(4)]
    ypst = work.alloc(BF16, [4, G]); Typst = [T("ypst%d" % i) for i in range(4)]

    Tqt_d = [[T("qtd%d_%d" % (h, g)) for g in range(NG)] for h in range(H)]
    Tkt_d = [[T("ktd%d_%d" % (h, g)) for g in range(NG)] for h in range(H)]
    Tv_d = [T("vd%d" % g) for g in range(NG)]
    Tsg_d = [T("sgd%d" % g) for g in range(NG)]
    Typ_d = [T("ypd%d" % g) for g in range(NG)]
    To_d = [[T("od%d_%d" % (h, g)) for g in range(NG)] for h in range(H)]
    Tl_d = [[T("ld%d_%d" % (h, g)) for g in range(NG)] for h in range(H)]

    for c in range(4):
        memset("pool", ubs[:, c, :], 0.0, [Tubs[c]])

    def load_x(t):
        xs = t % 4
        dma("sp", xt[xs], x_d[t * 128:(t + 1) * 128, :], "x%d" % xs, writes=[Txt[xs]])

    cur_g = [0]

    def zchunk(col0, ncols, b, lhs_src=None):
        hnT, ThnT = hnTb[cur_g[0] % 2], ThnTb[cur_g[0] % 2]
        for k in range(8):
            lhsT = w_f[:, k, col0:col0 + ncols] if lhs_src is None else lhs_src[:, k, :]
            mm(bank(b, G, ncols), lhsT, hnT[:, k, :], k == 0, k == 7,
               [TWA if (lhs_src is not None or col0 < OFF_GA) else TWB] + ThnT, [PB[b]])

    def P1a(g):
        for j in range(4):
            act(junk[j % 2], xt[j], AF.Square, [Txt[j]], [Tjunk[j % 2], Tssx], accum=ssx[:, j:j + 1])
        ts("dve", ssx, ssx, float(D * EPS), ALU.add, [Tssx], [Tssx])
        rsqrt_newton(ssx, nty, ntt, Tssx, Tnty, Tntt, 3)
        cp("dve", rstd_all[:, 4 * g:4 * g + 4], nty, [Tnty], [Trstd[g]])

    def P1b_scale(g, j):
        t = 4 * g + j
        hs = j % 2
        prog.op("act", lambda e, o=hn[hs], i=xt[j], sc=rstd_all[:, t:t + 1]: e.activation(out=o, in_=i, func=AF.Copy, scale=sc),
                reads=[Txt[j], Trstd[g]], writes=[Thn[hs]])
        if t + 4 < NT:
            load_x(t + 4)

    def P1b_tr(g, j):
        hs = j % 2
        hnT, ThnT = hnTb[g % 2], ThnTb[g % 2]
        b = next_bank()
        for k in range(8):
            prog.op("pe", lambda e, o=psum16[:, b * 1024 + k * 128:b * 1024 + (k + 1) * 128],
                    i=hn[hs][:, k * 128:(k + 1) * 128]: e.transpose(out=o, in_=i, identity=ident),
                    reads=[Thn[hs], Tident], writes=[PB[b]])
        cp("act", hnT[:, :, j * 128:(j + 1) * 128],
           psum16[:, b * 1024:(b + 1) * 1024].rearrange("p (k c) -> p k c", c=128),
           [PB[b]], [ThnT[j]])

    def P1b_steps(g):
        P1b_scale(g, 0)
        yield
        for j in range(4):
            if j + 1 < 4:
                P1b_scale(g, j + 1)
            P1b_tr(g, j)
            yield

    def drain(gen):
        if gen is not None:
            for _ in gen:
                pass

    def step(gen):
        if gen is not None:
            next(gen, None)

    def P2(g):
        cur_g[0] = g
        cT, TcT = cTb[g % 2], TcTb[g % 2]
        for c in range(5):
            b = next_bank()
            zchunk(c * 128, 128, b)
            act(sq[:, c, :], bank(b), AF.Square, [PB[b]], [Tsq[c]])
            cp("act", zs[:, c, :], bank(b), [PB[b]], [Tzs[c]])
        bq = next_bank()
        for c in range(3):
            mm(bank(bq), ones_bf, sq[:, c, :], c == 0, c == 2, [Tones, Tsq[c]], [PB[bq]])
        bk = next_bank()
        for c in range(2):
            mm(bank(bk), ones_bf, sq[:, 3 + c, :], c == 0, c == 1, [Tones, Tsq[3 + c]], [PB[bk]])
        ts("dve", ssn[:, 0:G], bank(bq), float(QL * EPS), ALU.add, [PB[bq]], [Tssn])
        ts("dve", ssn[:, G:2 * G], bank(bk), float(KVL * EPS), ALU.add, [PB[bk]], [Tssn])

    def P2b(g):
        cT, TcT = cTb[g % 2], TcTb[g % 2]
        rsqrt_newton(ssn, nry, nrt, Tssn, Tnry, Tnrt, 2)
        for c in range(3):
            stt(cT[:, c, :], zs[:, c, :], qn[:, c:c + 1], nry[:, 0:G], ALU.mult, ALU.mult,
                [Tzs[c], Tvec, Tnry], [TcT[c]])
        for c in range(2):
            stt(cT[:, 3 + c, :], zs[:, 3 + c, :], kvn[:, c:c + 1], nry[:, G:2 * G], ALU.mult, ALU.mult,
                [Tzs[3 + c], Tvec, Tnry], [TcT[3 + c]])

    def P3(g):
        cT, TcT = cTb[g % 2], TcTb[g % 2]
        cslot = g % 2
        for h in range(H):
            ba = next_bank()
            for k in range(3):
                mm(bank(ba, G, 96), w_q[:, k, h, :], cT[:, k, :], k == 0, k == 2, [TW1, TcT[k]], [PB[ba]])
            bb = next_bank()
            for k in range(3):
                mm(bank(bb, G, 96), w_qs[:, k, h, :], cT[:, k, :], k == 0, k == 2, [TW1, TcT[k]], [PB[bb]])
            cp("act", qst[0:64, h, :], psum[0:64, ba * 512:ba * 512 + G], [PB[ba]], [Tqst[h]])
            rs = h % 2
            tt("dve", r1[rs][64:96, :], psum[64:96, ba * 512:ba * 512 + G], cs[cslot][64:96, :], ALU.mult, [PB[ba], Tcs[cslot]], [Tr1[rs]])
            tt("dve", r2[rs][64:96, :], psum[64:96, bb * 512:bb * 512 + G], sn[cslot][64:96, :], ALU.mult, [PB[bb], Tcs[cslot]], [Tr2[rs]])
            tt("pool", qst[64:96, h, :], r1[rs][64:96, :], r2[rs][64:96, :], ALU.add, [Tr1[rs], Tr2[rs]], [Tqst[h]])
        dma("sp", qt_d[:, :, g * G:(g + 1) * G].rearrange("h r t -> r h t"), qst[0:96, :, :], "qst",
            reads=Tqst, writes=[Tqt_d[h][g] for h in range(H)])
        for h in range(H):
            b = next_bank()
            for k in range(2):
                mm(bank(b, G, 64), w_kv[:, k, h * 128:h * 128 + 64], cT[:, 3 + k, :], k == 0, k == 1,
                   [TW1, TcT[3 + k]], [PB[b]])
            cp("act", kst[0:64, h, :], psum[0:64, b * 512:b * 512 + G], [PB[b]], [Tkst[h]])
        dma("sp", kt_d[:, :, g * G:(g + 1) * G].rearrange("h r t -> r h t"), kst[0:96, :, :], "kst",
            reads=Tkst, writes=[Tkt_d[h][g] for h in range(H)])
        wv = w_kv.rearrange("p k (h c) -> p k h c", c=128)
        for j in range(4):
            b = next_bank()
            for k in range(2):
                mm(bank(b).rearrange("p (h c) -> p h c", c=64), cT[:, 3 + k, j * 128:(j + 1) * 128],
                   wv[:, k, :, 64:128], k == 0, k == 1, [TW1, TcT[3 + k]], [PB[b]])
            cp("act", vst[:, j, :], bank(b), [PB[b]], [Tvst[j]])
        dma("sp", v_d[g * G:(g + 1) * G, :].rearrange("(j p) c -> p j c", p=128), vst, "vst",
            reads=Tvst, writes=[Tv_d[g]])

    def P4a(g, nxt=None):
        cur_g[0] = g
        cslot = g % 2
        b1 = next_bank()
        zchunk(OFF_ZKR - 64, 96, b1)
        b2 = next_bank()
        zchunk(0, 96, b2, lhs_src=w_krs)
        tt("dve", r1[1][64:96, :], psum[64:96, b1 * 512:b1 * 512 + G], cs[cslot][64:96, :], ALU.mult, [PB[b1], Tcs[cslot]], [Tr1[1]])
        tt("dve", r2[1][64:96, :], psum[64:96, b2 * 512:b2 * 512 + G], sn[cslot][64:96, :], ALU.mult, [PB[b2], Tcs[cslot]], [Tr2[1]])
        tt("pool", kst[64:96, :, :], r1[1][64:96, :].unsqueeze(1).broadcast_to([32, H, G]),
           r2[1][64:96, :].unsqueeze(1).broadcast_to([32, H, G]), ALU.add, [Tr1[1], Tr2[1]], Tkst)
        step(nxt)
        for c in range(4):
            b = next_bank()
            zchunk(OFF_GA + c * 128, 128, b)
            act(th[:, c, :], bank(b), AF.Tanh, [PB[b]], [Tth[c]], scale=0.5)
            stt(sgst[:, c, :], th[:, c, :], 1.0, bank(b), ALU.add, ALU.mult, [Tth[c], PB[b]], [Tsgst[c]])
            if c % 2 == 1:
                step(nxt)
        dma("sp", sg_d[:, g * G:(g + 1) * G].rearrange("(c p) t -> p c t", p=128), sgst, "sgst",
            reads=Tsgst, writes=[Tsg_d[g]])

    def P4b(g, nxt=None):
        cur_g[0] = g
        for c in range(4):
            cp("pool", ub[:, c, 0:16], ubs[:, c, :], [Tubs[c]], [Tubh[c]])
            b = next_bank()
            zchunk(OFF_UP + c * 128, 128, b)
            cp("act", ub[:, c, 16:16 + G], bank(b), [PB[b]], [Tub[c]])
            act(wu[:, c, :], bank(b), AF.Copy, [PB[b]], [Twu[c]], scale=float(POOL_W[c]))
            if c % 2 == 1:
                step(nxt)
        for c in range(4):
            b = next_bank()
            zchunk(OFF_GP + c * 128, 128, b)
            act(th[:, 4 + c, :], bank(b), AF.Tanh, [PB[b]], [Tth[4 + c]], scale=0.5)
            stt(sgp[:, c, :], th[:, 4 + c, :], 1.0, bank(b), ALU.add, ALU.mult, [Tth[4 + c], PB[b]], [Tsgp[c]])

    def P4p(g):
        W = 16 + G
        for c in range(4):
            src, Tsrc = ub[:, c, :], Tub[c]
            cur, Tcur = src, Tsrc
            lo = 0
            dsts = [(pa, Tpa), (pbuf, Tpb)]
            for lvl in range(c + 1):
                sh = 1 << lvl
                dst, Tdst = dsts[lvl % 2]
                nlo = lo + sh
                tt("pool", dst[:, nlo:W], cur[:, nlo:W], cur[:, nlo - sh:W - sh], ALU.add,
                   [Tcur] + ([Tubh[c]] if lvl == 0 else []), [Tdst])
                cur, Tcur, lo = dst, Tdst, nlo
            tt("pool", dT[:, c, :], cur[:, 16:W], wu[:, c, :], ALU.subtract, [Tcur, Twu[c]], [TdT[c]])
            if g == 0:
                tt("pool", pfx, cur[:, 16:32], pcnt[:, c, :], ALU.mult, [Tcur, Tpcnt], [Tpfx])
                tt("pool", dT[:, c, 0:16], pfx, wu[:, c, 0:16], ALU.subtract, [Tpfx, Twu[c]], [TdT[c]])
            cp("pool", ubs[:, c, :], src[:, G:G + 16], [Tsrc], [Tubs[c]])

    def P4c(g):
        for c in range(4):
            b = next_bank()
            mm(bank(b), w_pl[:, c, :], dT[:, c, :], True, True, [TW1, TdT[c]], [PB[b]])
            stt(ypst[:, c, :], bank(b), psc[:, c:c + 1], sgp[:, c, :], ALU.mult, ALU.mult,
                [PB[b], Tvec, Tsgp[c]], [Typst[c]])
        dma("sp", yp_d[:, g * G:(g + 1) * G].rearrange("(c p) t -> p c t", p=128), ypst, "ypst",
            reads=Typst, writes=[Typ_d[g]])

    for t in range(min(4, NT)):
        load_x(t)
    for g in range(NG):
        cslot = g % 2
        dma("sp", cs[cslot][64:96, :], cos_d[:, g * G:(g + 1) * G], "cs%d" % cslot, writes=[Tcs[cslot]])
        dma("sp", sn[cslot][64:96, :], sin_d[:, g * G:(g + 1) * G], "cs%d" % cslot, writes=[Tcs[cslot]])
        if g == 0:
            P1a(0)
            drain(P1b_steps(0))
        P2(g)
        if g == 0:
            fold_wB()
        if g > 0:
            P3(g - 1)
            P4c(g - 1)
        nxt = None
        if g + 1 < NG:
            P1a(g + 1)
            nxt = P1b_steps(g + 1)
        P4a(g, nxt)
        P2b(g)
        P4b(g, nxt)
        P4p(g)
        drain(nxt)
    P3(NG - 1)
    P4c(NG - 1)

    if stop >= 2:
        prog.barrier()
        wmem.reset()
        w_m = wmem.alloc(BF16, [8, 2048])
        w_ba = wmem.alloc(BF16, [4, D])
        w_bp = wmem.alloc(BF16, [4, D])
        w_o = wmem.alloc(BF16, [8, D])
        TW3 = T("w3")
        work.reset()
        KTb = [work.alloc(BF16, [S]) for _ in range(2)]
        TKT = [T("KT%d" % i) for i in range(2)]
        VAb = [work.alloc(BF16, [NT, 65]) for _ in range(2)]
        TVA = [T("VA%d" % i) for i in range(2)]
        QTb = [work.alloc(BF16, [G]) for _ in range(2)]
        TQT = [T("QT%d" % i) for i in range(2)]
        NPT = 3
        PT = [work.alloc(BF16, [2 * G]) for _ in range(NPT)]
        TPT = [T("PT%d" % i) for i in range(NPT)]
        ost = [work.alloc(F32, [G]) for _ in range(2)]
        Tost = [T("ost%d" % i) for i in range(2)]
        for i in range(2):
            memset("pool", VAb[i][:, :, 64:65], 1.0, [TVA[i]])

        import os as _os4
        NDUMMY = int(_os4.environ.get("KDUMMY", "0"))
        SU = [(0, 1), (2, 3), (4, 5)]
        FILL_N = int(_os4.environ.get("KFILLN", "128"))
        su_rr = [0]
        pt_rr = [0]

        def load_head(h, s):
            dma("pool", KTb[s][0:96, :], kt_d[h, :, :], "KT%d" % s,
                reads=[Tkt_d[h][g] for g in range(NG)], writes=[TKT[s]])
            dma("pool", VAb[s][:, :, 0:64], v_d[:, h * 64:(h + 1) * 64].rearrange("(t p) c -> p t c", p=128),
                "VA%d" % s, reads=Tv_d, writes=[TVA[s]])

        def load_q(h, g, s):
            dma("sp", QTb[s][0:96, :], qt_d[h, :, g * G:(g + 1) * G], "QT%d" % s,
                reads=[Tqt_d[h][g]], writes=[TQT[s]])

        hq = [(h, g) for h in range(H) for g in range(NG)]
        allu = []
        for idx, (h, g) in enumerate(hq):
            units = []
            for p in range(2 * g):
                units.append(([2 * p, 2 * p + 1], [(0, 0), (G, 0)], False))
            units.append(([4 * g, 4 * g + 1], [(0, 0), (G, 128)], True))
            units.append(([4 * g + 2, 4 * g + 3], [(256, 256), (G, 384)], True))
            for ui, (kts, q0, pvq) in enumerate(units):
                allu.append((idx, kts, q0, ui == 0, ui == len(units) - 1, pvq))
        def emit_scores(i):
            idx, kts, q0, first, last, pvq = allu[i]
            h, g = hq[idx]
            hs, qs = h % 2, idx % 2
            if first:
                if idx >= 2:
                    w3_issue(1 if NG * H >= 40 else 3)
                if idx >= 1 and idx + 1 < len(hq):
                    load_q(hq[idx + 1][0], hq[idx + 1][1], 1 - qs)
            sb = SU[su_rr[0]]
            su_rr[0] = (su_rr[0] + 1) % len(SU)
            for k_, kt in enumerate(kts):
                b = sb[k_]
                rc, qk = q0[k_]
                mm(psum[:, sb[0] * 512 + rc:sb[0] * 512 + rc + G - qk], KTb[hs][0:96, kt * 128:(kt + 1) * 128],
                   QTb[qs][0:96, qk:G], True, True, [TKT[hs], TQT[qs]], [PB[b]])
            return sb

        def emit_exp_pv(i, sb):
            idx, kts, q0, first, last, pvq = allu[i]
            h, g = hq[idx]
            hs, qs = h % 2, idx % 2
            ob_ = 6 + idx % 2
            if first and g == 0 and h + 1 < H:
                load_head(h + 1, 1 - hs)
            pi = pt_rr[0]
            pt_rr[0] = (pi + 1) % NPT
            lo = q0[0][0]
            hi = q0[-1][0] + G - q0[-1][1]
            act(PT[pi][:, lo:hi], psum[:, sb[0] * 512 + lo:sb[0] * 512 + hi], AF.Exp,
                [PB[sb[0]], PB[sb[1]]], [TPT[pi]], scale=SM_SCALE)
            if pvq:
                for k_ in range(2):
                    memset("dve", PT[pi][64:128, q0[k_][0]:q0[k_][0] + 64], 0.0, [TPT[pi]])
            for k_, kt in enumerate(kts):
                rc, qk = q0[k_]
                mm(psum[0:65, ob_ * 512 + qk:ob_ * 512 + G], VAb[hs][:, kt, :], PT[pi][:, rc:rc + G - qk],
                   first and k_ == 0, last and k_ == len(kts) - 1, [TVA[hs], TPT[pi]], [PB[ob_]])
            for _ in range(NDUMMY):
                mm(bank(7, FILL_N), KTb[hs][0:96, 0:128], QTb[qs][0:96, 0:FILL_N], True, True, [TKT[hs], TQT[qs]], [PB[7]])
            if last:
                os_ = idx % 2
                cp("dve", ost[os_][0:65, :], psum[0:65, ob_ * 512:ob_ * 512 + G], [PB[ob_]], [Tost[os_]])
                dma("sp", o_d[h * 64:(h + 1) * 64, g * G:(g + 1) * G], ost[os_][0:64, :], "ost%d" % os_,
                    reads=[Tost[os_]], writes=[To_d[h][g]])
                dma("sp", l_d[h:h + 1, g * G:(g + 1) * G], ost[os_][64:65, :], "ost%d" % os_,
                    reads=[Tost[os_]], writes=[Tl_d[h][g]])
                if fold_todo and not w3_jobs and idx >= FOLD0:
                    k = fold_todo.pop(0)
                    ts("dve", w_m[:, k, :], w_m[:, k, :], gcol[:, k:k + 1], ALU.mult, [TW3, Tgin], [TW3])

        load_head(0, 0)
        load_q(0, 0, 0)
        if len(hq) > 1:
            load_q(hq[1][0], hq[1][1], 1)
        w3_jobs = []
        for k in range(8):
            w3_jobs.append((w_m[:, k, :], w_in_v[:, k, OFF_GM:OFF_GM + 2048]))
        w3_jobs.append((w_ba, w_ba_d.rearrange("(k p) c -> p k c", p=128)))
        w3_jobs.append((w_bp, w_bp_d.rearrange("(k p) c -> p k c", p=128)))
        for k in range(8):
            w3_jobs.append((w_o[:, k, :], w_out_d.rearrange("(k p) c -> p k c", p=128)[:, k, :]))

        def w3_issue(n):
            for _ in range(n):
                if w3_jobs:
                    o_, i_ = w3_jobs.pop(0)
                    dma("pool", o_, i_, "w3", writes=[TW3], group=True)

        LA = 2
        fold_todo = list(range(8))
        FOLD0 = (len(hq) * 5) // 8
        pendq = [emit_scores(i) for i in range(min(LA, len(allu)))]
        for i in range(len(allu)):
            if i + LA < len(allu):
                pendq.append(emit_scores(i + LA))
            emit_exp_pv(i, pendq.pop(0))
        w3_issue(len(w3_jobs))
        for k in fold_todo:
            ts("dve", w_m[:, k, :], w_m[:, k, :], gcol[:, k:k + 1], ALU.mult, [TW3, Tgin], [TW3])

    if stop >= 3:
        prog.barrier()
        work.reset()
        NXS3 = 5
        xt = [work.alloc(F32, [D]) for _ in range(NXS3)]
        Txt = [T("x3t%d" % i) for i in range(NXS3)]
        hn = [work.alloc(BF16, [D]) for _ in range(2)]
        Thn = [T("h3n%d" % i) for i in range(2)]
        hnT = work.alloc(BF16, [8, G]); ThnT = [T("h3nT%d" % i) for i in range(4)]
        tg = work.alloc(BF16, [16, G]); Ttg = [T("tg%d" % i) for i in range(16)]
        oin = [work.alloc(F32, [4, G]) for _ in range(2)]; Toin = [T("oin%d" % i) for i in range(2)]
        lin = work.alloc(F32, [4, G]); Tlin = [T("lin%d" % i) for i in range(4)]
        sgin = [work.alloc(BF16, [4, G]) for _ in range(2)]; Tsgin = [T("sgin%d" % i) for i in range(2)]
        ypin = [work.alloc(BF16, [4, G]) for _ in range(2)]; Typin = [T("ypin%d" % i) for i in range(2)]
        ya = work.alloc(BF16, [4, G]); Tya = [T("ya%d" % i) for i in range(4)]
        m1 = [work.alloc(F32, [G]) for _ in range(2)]; Tm1 = [T("m1_%d" % i) for i in range(2)]
        m2 = [work.alloc(F32, [G]) for _ in range(2)]; Tm2 = [T("m2_%d" % i) for i in range(2)]
        mg = work.alloc(BF16, [8, G]); Tmg = [T("mg%d" % i) for i in range(8)]
        hh = [work.alloc(F32, [D]) for _ in range(4)]; Thh = [T("hh%d" % i) for i in range(4)]
        junk3 = [work.alloc(BF16, [D])] * 4; Tjunk3 = [T("j3")] * 4
        ss3 = work.alloc(F32, [4]); Tss3 = T("ss3")
        ny3 = work.alloc(F32, [4]); Tny3 = T("ny3")
        nt3 = work.alloc(F32, [4]); Tnt3 = T("nt3")

        import os as _os2
        if _os2.environ.get("KDBG"):
            print("DBG phase3 work used", work.off - work.lo, "of", work.hi - work.lo)

        def load_g3(g, s):
            dma("sp", oin[s], o_d[:, g * G:(g + 1) * G].rearrange("(c p) t -> p c t", p=128), "oin%d" % s,
                reads=[To_d[h][g] for h in range(H)], writes=[Toin[s]])
            dma("sp", sgin[s], sg_d[:, g * G:(g + 1) * G].rearrange("(c p) t -> p c t", p=128), "sgin%d" % s,
                reads=[Tsg_d[g]], writes=[Tsgin[s]])
            dma("sp", ypin[s], yp_d[:, g * G:(g + 1) * G].rearrange("(c p) t -> p c t", p=128), "ypin%d" % s,
                reads=[Typ_d[g]], writes=[Typin[s]])

        def load_x3(t):
            xs = t % NXS3
            dma("sp", xt[xs], x_d[t * 128:(t + 1) * 128, :], "x3_%d" % xs, writes=[Txt[xs]])

        def load_lin(g):
            for hh_ in range(H):
                dma("sp", lin[(hh_ % 2) * 64:(hh_ % 2) * 64 + 64, hh_ // 2, :],
                    l_d[hh_, g * G:(g + 1) * G].partition_broadcast(64), "lin%d" % (hh_ // 2),
                    reads=[Tl_d[hh_][g]], writes=[Tlin[hh_ // 2]])

        def stageA3(g, j):
            t = 4 * g + j
            xs = t % NXS3
            ts("dve", hn[j % 2], xt[xs], rstd_all[:, t:t + 1], ALU.mult, [Txt[xs], Trstd[g]], [Thn[j % 2]])
            b = next_bank()
            for k in range(8):
                prog.op("pe", lambda e, o=psum16[:, b * 1024 + k * 128:b * 1024 + (k + 1) * 128],
                        i=hn[j % 2][:, k * 128:(k + 1) * 128]: e.transpose(out=o, in_=i, identity=ident),
                        reads=[Thn[j % 2], Tident], writes=[PB[b]])
            cp("act", hnT[:, :, j * 128:(j + 1) * 128],
               psum16[:, b * 1024:(b + 1) * 1024].rearrange("p (k c) -> p k c", c=128),
               [PB[b]], [ThnT[j]])

        load_g3(0, 0)
        for t in range(min(NXS3, NT)):
            load_x3(t)
        for g in range(NG):
            s = g % 2
            if g + 1 < NG:
                load_g3(g + 1, 1 - s)
            if g == 0:
                for j in range(4):
                    stageA3(0, j)
            for c in range(16):
                b = next_bank()
                for k in range(8):
                    mm(bank(b), w_m[:, k, c * 128:(c + 1) * 128], hnT[:, k, :], k == 0, k == 7, [TW3] + ThnT, [PB[b]])
                act(tg[:, c, :], bank(b), AF.Tanh, [PB[b]], [Ttg[c]], scale=0.5)
            if g == 0:
                load_lin(0)
            for c in range(4):
                tt("pool", oin[s][:, c, :], oin[s][:, c, :], sgin[s][:, c, :], ALU.mult, [Toin[s], Tsgin[s]], [Toin[s]])
                prog.op("dve", lambda e, o=lin[:, c, :]: e.reciprocal(out=o, in_=o), reads=[Tlin[c]], writes=[Tlin[c]])
                stt(ya[:, c, :], oin[s][:, c, :], 0.5, lin[:, c, :], ALU.mult, ALU.mult, [Toin[s], Tlin[c]], [Tya[c]])
            if g + 1 < NG:
                load_lin(g + 1)
            for dc in range(8):
                ba = next_bank()
                for k in range(4):
                    mm(bank(ba), w_ba[:, k, dc * 128:(dc + 1) * 128], ya[:, k, :], k == 0, k == 3, [TW3, Tya[k]], [PB[ba]])
                bp = next_bank()
                for k in range(4):
                    mm(bank(bp), w_bp[:, k, dc * 128:(dc + 1) * 128], ypin[s][:, k, :], k == 0, k == 3, [TW3, Typin[s]], [PB[bp]])
                ms = dc % 2
                stt(m1[ms], tg[:, dc, :], 1.0, bank(ba), ALU.add, ALU.mult, [Ttg[dc], PB[ba]], [Tm1[ms]])
                stt(m2[ms], tg[:, 8 + dc, :], 1.0, bank(bp), ALU.add, ALU.mult, [Ttg[8 + dc], PB[bp]], [Tm2[ms]])
                tt("pool", mg[:, dc, :], m1[ms], m2[ms], ALU.add, [Tm1[ms], Tm2[ms]], [Tmg[dc]])
            for j in range(4):
                t = 4 * g + j
                xs = t % NXS3
                hs = j
                b0 = next_bank()
                b1 = next_bank()
                for half, b in ((0, b0), (1, b1)):
                    for k in range(8):
                        mm(bank(b), mg[:, k, j * 128:(j + 1) * 128], w_o[:, k, half * 512:(half + 1) * 512],
                           k == 0, k == 7, [TW3, Tmg[k]], [PB[b]])
                if g + 1 < NG and j >= 1:
                    stageA3(g + 1, j - 1)
                for half, b in ((0, b0), (1, b1)):
                    stt(hh[hs][:, half * 512:(half + 1) * 512], bank(b), 0.5, xt[xs][:, half * 512:(half + 1) * 512],
                        ALU.mult, ALU.add, [PB[b], Txt[xs]], [Thh[hs]])
                if t + NXS3 < NT:
                    load_x3(t + NXS3)
                act(junk3[hs], hh[hs], AF.Square, [Thh[hs]], [Tjunk3[hs], Tss3], accum=ss3[:, j:j + 1])
                tt("pool", hh[hs], hh[hs], gfin, ALU.mult, [Thh[hs], Tgfin], [Thh[hs]])
            if g + 1 < NG:
                stageA3(g + 1, 3)
            ts("dve", ss3, ss3, float(D * EPS), ALU.add, [Tss3], [Tss3])
            rsqrt_newton(ss3, ny3, nt3, Tss3, Tny3, Tnt3, 3)
            for j in range(4):
                t = 4 * g + j
                prog.op("act", lambda e, o=hh[j], sc=ny3[:, j:j + 1]: e.activation(out=o, in_=o, func=AF.Copy, scale=sc),
                        reads=[Thh[j], Tny3], writes=[Thh[j]])
                dma("sp", out_d[t * 128:(t + 1) * 128, :], hh[j], "ob%d" % j, reads=[Thh[j]], writes=[T("outd")])

    import os as _os
    _cut = int(_os.environ.get("KCUT", "0"))
    if _cut:
        prog.ins = prog.ins[:_cut]
    prog.ins.append(dict(eng="sp", fn=None, deps=set(range(len(prog.ins))), dma=None, signal=False))

    totals = prog.resolve()
    with ExitStack() as es:
        sems = {}
        for k in totals:
            sems[k] = es.enter_context(nc.semaphore("s_%s_%s" % (k[0], k[1])))
        es.enter_context(nc.allow_low_precision("bf16 matmul operands, fp32 accumulation (problem tolerance calibrated for bf16)"))
        block = es.enter_context(nc.Block())

        def run(engname):
            def f(e):
                for it in prog.ins:
                    if it["eng"] != engname:
                        continue
                    for k, v in it["waits"]:
                        e.wait_ge(sems[k], v)
                    if it["fn"] is None:
                        continue
                    ins = it["fn"](e)
                    if it["signal"]:
                        k, v = it["sig"]
                        ins.then_inc(sems[k], 16 if k[0] == "dma" else 1)
            return f

        block.sync(run("sp"))
        block.tensor(run("pe"))
        block.scalar(run("act"))
        block.vector(run("dve"))
        block.gpsimd(run("pool"))
    return nc


def host_consts(S):
    half = ROPE // 2
    inv_freq = (np.float32(10000.0) ** (-np.arange(half, dtype=np.float32) / np.float32(half))).astype(np.float32)
    ang = (np.arange(S, dtype=np.float32)[:, None] * inv_freq[None, :]).astype(np.float32)
    cos = np.cos(ang.astype(np.float64)).astype(np.float32).T
    sin = np.sin(ang.astype(np.float64)).astype(np.float32).T
    cosT = np.concatenate([cos, cos], axis=0)
    sinT = np.concatenate([-sin, sin], axis=0)
    pcnt = np.zeros((4, 16), np.float32)
    for gi, w in enumerate(POOL_W):
        pcnt[gi] = float(w) / np.minimum(np.arange(16) + 1, w)
    return {
        "c_ident": np.eye(128, dtype=np.float32),
        "c_cos": np.ascontiguousarray(cosT),
        "c_sin": np.ascontiguousarray(sinT),
        "c_pcnt": pcnt,
    }


_NC_CACHE = {}


def kernel(x, norm_in, w_in, q_norm, w_uq, kv_norm, w_ukv, pool_w, pool_scale,
           w_branch_attn, w_branch_pool, w_out, norm_final):
    x = np.asarray(x, dtype=np.float32)
    B, S, _ = x.shape
    if S not in _NC_CACHE:
        _NC_CACHE[S] = build_nc(S)
    nc = _NC_CACHE[S]
    consts = host_consts(S)
    shared = {
        "norm_in": np.asarray(norm_in, np.float32), "w_in": np.asarray(w_in, np.float32),
        "q_norm": np.asarray(q_norm, np.float32), "w_uq": np.asarray(w_uq, np.float32),
        "kv_norm": np.asarray(kv_norm, np.float32), "w_ukv": np.asarray(w_ukv, np.float32),
        "pool_w": np.asarray(pool_w, np.float32), "pool_scale": np.asarray(pool_scale, np.float32),
        "w_branch_attn": np.asarray(w_branch_attn, np.float32),
        "w_branch_pool": np.asarray(w_branch_pool, np.float32),
        "w_out": np.asarray(w_out, np.float32), "norm_final": np.asarray(norm_final, np.float32),
    }
    shared.update(consts)
    in_maps = []
    for b in range(B):
        m = dict(shared)
        m["x"] = np.ascontiguousarray(x[b])
        in_maps.append(m)
    res = run_bass_kernel_spmd(nc, in_maps, core_ids=list(range(B)))
    return np.stack([np.asarray(r["out"], dtype=np.float32) for r in res.results], axis=0)
```

```python
import numpy as np
import concourse.bass as bass
import concourse.mybir as mybir
from concourse.bass_utils import run_bass_kernel_spmd
from contextlib import ExitStack

F32 = mybir.dt.float32
BF16 = mybir.dt.bfloat16
I32 = mybir.dt.int32
ALU = mybir.AluOpType
AF = mybir.ActivationFunctionType

D = 1024
H = 8
NOPE = 64
ROPE = 32
QKD = 96
VD = 64
QL = 384
KVL = 256
NFRONT = 2208
OFF_ZQ, OFF_ZKV, OFF_ZKR, OFF_GA, OFF_UP, OFF_GP, OFF_GM = 0, 384, 640, 672, 1184, 1696, 2208
IN_TOTAL = 4256
G = 512
EPS = 1e-6
POOL_W = (2, 4, 8, 16)
MAGIC = 0x5F3759DF
SM_SCALE = float(QKD ** -0.5)
FULL_S = 4096
NCORES = 8


class T:
    __slots__ = ("name", "w", "r", "excl")

    def __init__(self, name, excl=False):
        self.name = name
        self.w = None
        self.r = {}
        self.excl = excl


class Prog:
    COMPUTE = ("pe", "act", "dve", "pool")

    def __init__(self):
        self.ins = []
        self.bar_gen = 0
        self.bar_deps = set()
        self.eng_gen = {}
        self.last = {}
        self.dmas_since_bar = []
        self.groups = {}

    def op(self, eng, fn, reads=(), writes=(), dma=None, group=False):
        i = len(self.ins)
        deps = set()
        for t in reads:
            if t.w is not None:
                deps.add(t.w)
            if t.excl:
                for k, rid in t.r.items():
                    if k != eng:
                        deps.add(rid)
        for t in writes:
            if t.w is not None:
                deps.add(t.w)
            for rid in t.r.values():
                deps.add(rid)
        if self.eng_gen.get(eng, 0) < self.bar_gen:
            deps |= self.bar_deps
            self.eng_gen[eng] = self.bar_gen
        deps.discard(i)
        if dma is not None:
            deps = {d for d in deps if self.ins[d]["dma"] != dma}
        self.ins.append(dict(eng=eng, fn=fn, deps=deps, dma=dma, signal=dma is not None))
        if dma is not None:
            self.groups[dma] = group
            self.dmas_since_bar.append(i)
        else:
            self.last[eng] = i
        rkey = eng if dma is None else ("dma", i)
        for t in reads:
            t.r[rkey] = i
        for t in writes:
            t.w = i
            t.r = {}
        return i

    def barrier(self):
        self.bar_gen += 1
        self.bar_deps = set(self.last.values()) | set(self.dmas_since_bar)
        self.dmas_since_bar = []

    def resolve(self):
        ins = self.ins
        for it in ins:
            for d in it["deps"]:
                dd = ins[d]
                if dd["dma"] is None and dd["eng"] == it["eng"] == "pe" and it["dma"] is None:
                    continue
                dd["signal"] = True
        cnt = {}
        for it in ins:
            if it["dma"] is not None:
                k = ("dma", it["dma"])
                cnt[k] = cnt.get(k, 0) + 16
                it["sig"] = (k, cnt[k])
            elif it["signal"]:
                k = ("eng", it["eng"])
                cnt[k] = cnt.get(k, 0) + 1
                it["sig"] = (k, cnt[k])
        totals = dict(cnt)
        waited = {}
        for it in ins:
            need = {}
            for d in it["deps"]:
                dd = ins[d]
                if dd["dma"] is None and dd["eng"] == it["eng"] == "pe" and it["dma"] is None:
                    continue
                k, v = dd["sig"]
                if k[0] == "dma" and self.groups[k[1]]:
                    v = totals[k]
                if v > need.get(k, 0):
                    need[k] = v
            w = waited.setdefault(it["eng"], {})
            out = []
            for k, v in need.items():
                if w.get(k, 0) < v:
                    w[k] = v
                    out.append((k, v))
            it["waits"] = out
        return totals


def build_nc(S, stop=3):
    NG = S // G
    NT = S // 128
    nc = bass.Bass("TRN2", target_bir_lowering=False)

    def din(name, shape, dt=F32):
        return nc.dram_tensor(name, list(shape), dt, kind="ExternalInput").ap()

    def dscr(name, shape, dt=BF16):
        return nc.dram_tensor(name, list(shape), dt, kind="Internal").ap()

    x_d = din("x", [S, D])
    norm_in_d = din("norm_in", [D])
    w_in_d = din("w_in", [D, IN_TOTAL])
    q_norm_d = din("q_norm", [QL])
    w_uq_d = din("w_uq", [QL, H, QKD])
    kv_norm_d = din("kv_norm", [KVL])
    w_ukv_d = din("w_ukv", [KVL, H, 128])
    pool_w_d = din("pool_w", [4, 128, 128])
    pool_scale_d = din("pool_scale", [512])
    w_ba_d = din("w_branch_attn", [512, D])
    w_bp_d = din("w_branch_pool", [512, D])
    w_out_d = din("w_out", [D, D])
    norm_final_d = din("norm_final", [D])
    ident_d = din("c_ident", [128, 128])
    cos_d = din("c_cos", [32, S])
    sin_d = din("c_sin", [32, S])
    pcnt_d = din("c_pcnt", [4, 16])
    out_d = nc.dram_tensor("out", [S, D], F32, kind="ExternalOutput").ap()

    qt_d = dscr("s_qt", [H, QKD, S])
    kt_d = dscr("s_kt", [H, QKD, S])
    v_d = dscr("s_v", [S, 512])
    sg_d = dscr("s_sg", [512, S])
    yp_d = dscr("s_yp", [512, S])
    o_d = dscr("s_o", [512, S], F32)
    l_d = dscr("s_l", [H, S], F32)

    ARENA = 207 * 1024
    arena = nc.alloc_sbuf_tensor("arena", [128, ARENA // 2], BF16)
    arena32 = arena.bitcast(F32)
    psum = nc.alloc_psum_tensor("psum", [128, 4096], F32)
    psum16 = psum.bitcast(BF16)

    class Bump:
        def __init__(self, lo, hi):
            self.lo, self.hi, self.off = lo, hi, lo

        def reset(self):
            self.off = self.lo

        def alloc(self, dt, shape):
            sz = 4 if dt in (F32, I32) else 2
            n = int(np.prod(shape))
            nb = (n * sz + 31) // 32 * 32
            assert self.off + nb <= self.hi, ("arena overflow", self.off, nb, self.hi)
            o = self.off
            self.off += nb
            if sz == 4:
                ap = arena32[:, o // 4:o // 4 + n]
            else:
                ap = arena[:, o // 2:o // 2 + n]
            if len(shape) == 2:
                ap = ap.rearrange("p (a b) -> p a b", b=shape[1])
            elif len(shape) == 3:
                ap = ap.rearrange("p (a b c) -> p a b c", b=shape[1], c=shape[2])
            return ap

    WREG = 68 * 1024
    wmem = Bump(0, WREG)
    cmem = Bump(WREG, WREG + 10 * 1024)
    work = Bump(WREG + 10 * 1024, ARENA)

    def bank(b, n=512, parts=128, off=0):
        return psum[0:parts, b * 512 + off:b * 512 + off + n]

    PB = [T("psum%d" % b, excl=True) for b in range(8)]
    prog = Prog()
    pb_rr = [0]

    def next_bank():
        b = pb_rr[0]
        pb_rr[0] = (b + 1) % 8
        return b

    def dma(q, out, in_, key, reads=(), writes=(), group=False):
        prog.op(q, lambda e, o=out, i=in_: e.dma_start(out=o, in_=i), reads=reads, writes=writes,
                dma=key, group=group)

    def mm(out, lhsT, rhs, start, stop, reads, writes):
        prog.op("pe", lambda e, o=out, l=lhsT, r=rhs, s=start, p=stop: e.matmul(o, lhsT=l, rhs=r, start=s, stop=p),
                reads=reads, writes=writes)

    def act(out, in_, func, reads, writes, scale=1.0, accum=None):
        if accum is None:
            prog.op("act", lambda e, o=out, i=in_, f=func, s=scale: e.activation(out=o, in_=i, func=f, scale=s),
                    reads=reads, writes=writes)
        else:
            prog.op("act", lambda e, o=out, i=in_, f=func, s=scale, a=accum: e.activation(out=o, in_=i, func=f, scale=s, accum_out=a),
                    reads=reads, writes=writes)

    def tt(eng, out, in0, in1, op, reads, writes):
        prog.op(eng, lambda e, o=out, a=in0, b=in1, p=op: e.tensor_tensor(out=o, in0=a, in1=b, op=p),
                reads=reads, writes=writes)

    def ts(eng, out, in0, s1, op0, reads, writes, s2=None, op1=None):
        if op1 is None:
            prog.op(eng, lambda e, o=out, a=in0, x=s1, p=op0: e.tensor_scalar(out=o, in0=a, scalar1=x, scalar2=None, op0=p),
                    reads=reads, writes=writes)
        else:
            prog.op(eng, lambda e, o=out, a=in0, x=s1, y=s2, p=op0, q=op1: e.tensor_scalar(out=o, in0=a, scalar1=x, scalar2=y, op0=p, op1=q),
                    reads=reads, writes=writes)

    def stt(out, in0, scalar, in1, op0, op1, reads, writes):
        prog.op("dve", lambda e, o=out, a=in0, s=scalar, b=in1, p=op0, q=op1: e.scalar_tensor_tensor(out=o, in0=a, scalar=s, in1=b, op0=p, op1=q),
                reads=reads, writes=writes)

    def cp(eng, out, in_, reads, writes):
        if eng == "act":
            prog.op("act", lambda e, o=out, i=in_: e.copy(out=o, in_=i), reads=reads, writes=writes)
        else:
            prog.op(eng, lambda e, o=out, i=in_: e.tensor_copy(out=o, in_=i), reads=reads, writes=writes)

    def memset(eng, ap, val, writes):
        prog.op(eng, lambda e, a=ap, v=val: e.memset(a, v), writes=writes)

    def rsqrt_newton(v_ap, y_ap, t_ap, Tv, Ty, Tt, iters):
        ts("dve", t_ap.bitcast(I32), v_ap.bitcast(I32), 1, ALU.logical_shift_right, [Tv], [Tt])
        ts("dve", y_ap.bitcast(I32), t_ap.bitcast(I32), -1, ALU.mult, [Tt], [Ty], s2=MAGIC, op1=ALU.add)
        for _ in range(iters):
            stt(t_ap, y_ap, -0.5, y_ap, ALU.mult, ALU.mult, [Ty], [Tt])
            tt("dve", t_ap, t_ap, v_ap, ALU.mult, [Tt, Tv], [Tt])
            stt(y_ap, t_ap, 1.5, y_ap, ALU.add, ALU.mult, [Tt, Ty], [Ty])

    ident = cmem.alloc(BF16, [128])
    Tident = T("ident")
    gcol = cmem.alloc(F32, [8])
    Tgin = T("gcol")
    gfin = cmem.alloc(F32, [D])
    Tgfin = T("gfin")
    qn = cmem.alloc(F32, [3])
    kvn = cmem.alloc(F32, [2])
    psc = cmem.alloc(F32, [4])
    Tvec = T("vecs")
    rstd_all = cmem.alloc(F32, [NT])
    Trstd = [T("rstd%d" % g) for g in range(NG)]
    ones_bf = cmem.alloc(BF16, [128])
    half_f = cmem.alloc(F32, [64])
    Tones = T("ones")
    pcnt = cmem.alloc(F32, [4, 16])
    Tpcnt = T("pcnt")

    dma("pool", ident, ident_d, "wI", writes=[Tident], group=True)
    for k in range(8):
        dma("sp", gcol[:, k:k + 1], norm_in_d[k * 128:(k + 1) * 128].rearrange("(p o) -> p o", o=1), "c0", writes=[Tgin], group=True)
    dma("sp", gfin, norm_final_d.partition_broadcast(128), "c0", writes=[Tgfin], group=True)
    for c in range(3):
        dma("sp", qn[:, c:c + 1], q_norm_d[c * 128:(c + 1) * 128].rearrange("(p o) -> p o", o=1), "c0", writes=[Tvec], group=True)
    for c in range(2):
        dma("sp", kvn[:, c:c + 1], kv_norm_d[c * 128:(c + 1) * 128].rearrange("(p o) -> p o", o=1), "c0", writes=[Tvec], group=True)
    for c in range(4):
        dma("sp", psc[:, c:c + 1], pool_scale_d[c * 128:(c + 1) * 128].rearrange("(p o) -> p o", o=1), "c0", writes=[Tvec], group=True)
    dma("sp", pcnt.rearrange("p a b -> p (a b)"), pcnt_d.rearrange("a b -> (a b)").partition_broadcast(128), "c0",
        writes=[Tpcnt], group=True)
    ts("dve", gcol, gcol, 32.0, ALU.mult, [Tgin], [Tgin])
    ts("dve", gfin, gfin, 32.0, ALU.mult, [Tgfin], [Tgfin])
    ts("dve", qn, qn, float(np.sqrt(QL)), ALU.mult, [Tvec], [Tvec])
    ts("dve", kvn, kvn, 16.0, ALU.mult, [Tvec], [Tvec])
    for gi, w in enumerate(POOL_W):
        ts("dve", psc[:, gi:gi + 1], psc[:, gi:gi + 1], 0.5 / w, ALU.mult, [Tvec], [Tvec])
    memset("dve", ones_bf, 1.0, [Tones])
    memset("dve", half_f, 0.5, [Tones])

    w_f = wmem.alloc(BF16, [8, NFRONT])
    w_krs = wmem.alloc(BF16, [8, 96])
    w_q = wmem.alloc(BF16, [3, H, QKD])
    w_qs = wmem.alloc(BF16, [3, H, QKD])
    w_kv = wmem.alloc(BF16, [2, 1024])
    w_pl = wmem.alloc(BF16, [4, 128])
    TW1 = T("w1")
    TWA = T("wA")
    TWB = T("wB")
    memset("pool", w_krs[:, :, 0:64], 0.0, [TWA])
    memset("pool", w_qs[:, :, :, 0:64], 0.0, [TW1])
    w_in_v = w_in_d.rearrange("(k p) c -> p k c", p=128)
    for k in range(8):
        dma("pool", w_f[:, k, 0:OFF_GA], w_in_v[:, k, 0:OFF_GA], "wA", writes=[TWA], group=True)
    dma("pool", w_krs[:, :, 64:80], w_in_v[:, :, OFF_ZKR + 16:OFF_ZKR + 32], "wA", writes=[TWA], group=True)
    dma("pool", w_krs[:, :, 80:96], w_in_v[:, :, OFF_ZKR:OFF_ZKR + 16], "wA", writes=[TWA], group=True)
    for k in range(8):
        dma("pool", w_f[:, k, OFF_GA:NFRONT], w_in_v[:, k, OFF_GA:NFRONT], "wB", writes=[TWB], group=True)
    w_uq_v = w_uq_d.rearrange("(k p) h c -> p k h c", p=128)
    for k in range(3):
        dma("pool", w_q[:, k, :, :], w_uq_v[:, k, :, :], "w1", writes=[TW1], group=True)
        dma("pool", w_qs[:, k, :, 64:80], w_uq_v[:, k, :, 80:96], "w1", writes=[TW1], group=True)
        dma("pool", w_qs[:, k, :, 80:96], w_uq_v[:, k, :, 64:80], "w1", writes=[TW1], group=True)
    w_ukv_v = w_ukv_d.rearrange("(k p) h c -> p k (h c)", p=128)
    for k in range(2):
        dma("pool", w_kv[:, k, :], w_ukv_v[:, k, :], "w1", writes=[TW1], group=True)
    dma("pool", w_pl, pool_w_d.rearrange("g c d -> c g d"), "w1", writes=[TW1], group=True)
    for k in range(8):
        ts("dve", w_f[:, k, 0:OFF_GA], w_f[:, k, 0:OFF_GA], gcol[:, k:k + 1], ALU.mult, [TWA, Tgin], [TWA])
        ts("dve", w_krs[:, k, 64:96], w_krs[:, k, 64:96], gcol[:, k:k + 1], ALU.mult, [TWA, Tgin], [TWA])

    def fold_wB():
        for k in range(8):
            ts("dve", w_f[:, k, OFF_GA:NFRONT], w_f[:, k, OFF_GA:NFRONT], gcol[:, k:k + 1], ALU.mult, [TWB, Tgin], [TWB])
    xt = [wmem.alloc(F32, [D]) for _ in range(4)]

    work.reset()
    NXS = 4
    Txt = [T("xt%d" % i) for i in range(NXS)]
    junk = [work.alloc(BF16, [D]) for _ in range(2)]
    Tjunk = [T("junk%d" % i) for i in range(2)]
    ssx = work.alloc(F32, [4]); Tssx = T("ssx")
    nty = work.alloc(F32, [4]); Tnty = T("nty")
    ntt = work.alloc(F32, [4]); Tntt = T("ntt")
    hn = [work.alloc(BF16, [D]) for _ in range(2)]
    Thn = [T("hn%d" % i) for i in range(2)]
    hnTb = [work.alloc(BF16, [8, G]) for _ in range(2)]
    ThnTb = [[T("hnT%d_%d" % (s_, i)) for i in range(4)] for s_ in range(2)]
    zs = work.alloc(F32, [5, G]); Tzs = [T("zs%d" % i) for i in range(5)]
    sq = work.alloc(BF16, [5, G]); Tsq = [T("sq%d" % i) for i in range(5)]
    ssn = work.alloc(F32, [2 * G]); Tssn = T("ssn")
    nry = work.alloc(F32, [2 * G]); Tnry = T("nry")
    nrt = work.alloc(F32, [2 * G]); Tnrt = T("nrt")
    cTb = [work.alloc(BF16, [5, G]) for _ in range(2)]
    TcTb = [[T("cT%d_%d" % (s_, i)) for i in range(5)] for s_ in range(2)]
    cs = [work.alloc(F32, [G]) for _ in range(2)]
    sn = [work.alloc(F32, [G]) for _ in range(2)]
    Tcs = [T("cs%d" % i) for i in range(2)]
    r1 = [work.alloc(BF16, [G]) for _ in range(2)]
    r2 = [work.alloc(BF16, [G]) for _ in range(2)]
    Tr1 = [T("r1_%d" % i) for i in range(2)]
    Tr2 = [T("r2_%d" % i) for i in range(2)]
    wu = work.alloc(BF16, [4, G]); Twu = [T("wu%d" % c) for c in range(4)]
    pfx = work.alloc(F32, [16]); Tpfx = T("pfx")
    qst = work.alloc(BF16, [H, G]); Tqst = [T("qst%d" % h) for h in range(H)]
    kst = work.alloc(BF16, [H, G]); Tkst = [T("kst%d" % h) for h in range(H)]; Tkr = T("kstrope")
    vst = work.alloc(BF16, [4, 512]); Tvst = [T("vst%d" % i) for i in range(4)]
    _th = work.alloc(BF16, [2, G]); _Tth = [T("th%d" % i) for i in range(2)]

    class _Rot:
        def __init__(self, a, n): self.a, self.n = a, n
        def __getitem__(self, i): return self.a[i % self.n]
    Tth = _Rot(_Tth, 2)

    class _RotAP:
        def __getitem__(self, key):
            p, c, f = key
            return _th[p, c % 2, f]
    th = _RotAP()
    sgst = work.alloc(BF16, [4, G]); Tsgst = [T("sgst%d" % i) for i in range(4)]
    sgp = work.alloc(BF16, [4, G]); Tsgp = [T("sgp%d" % i) for i in range(4)]
    ub = work.alloc(F32, [4, 16 + G])
    Tub = [T("ub_%d" % c) for c in range(4)]
    Tubh = [T("ubh_%d" % c) for c in range(4)]
    ubs = work.alloc(F32, [4, 16]); Tubs = [T("ubs%d" % c) for c in range(4)]
    pa = work.alloc(F32, [16 + G]); Tpa = T("pa")
    pbuf = work.alloc(F32, [16 + G]); Tpb = T("pbuf")
    dT = work.alloc(BF16, [4, G]); TdT = [T("dT%d" % i) for i in range(4)]
    ypst = work.alloc(BF16, [4, G]); Typst = [T("ypst%d" % i) for i in range(4)]

    Tqt_d = [[T("qtd%d_%d" % (h, g)) for g in range(NG)] for h in range(H)]
    Tkt_d = [[T("ktd%d_%d" % (h, g)) for g in range(NG)] for h in range(H)]
    Tv_d = [T("vd%d" % g) for g in range(NG)]
    Tsg_d = [T("sgd%d" % g) for g in range(NG)]
    Typ_d = [T("ypd%d" % g) for g in range(NG)]
    To_d = [[T("od%d_%d" % (h, g)) for g in range(NG)] for h in range(H)]
    Tl_d = [[T("ld%d_%d" % (h, g)) for g in range(NG)] for h in range(H)]

    for c in range(4):
        memset("pool", ubs[:, c, :], 0.0, [Tubs[c]])

    def load_x(t):
        xs = t % 4
        dma("sp", xt[xs], x_d[t * 128:(t + 1) * 128, :], "x%d" % xs, writes=[Txt[xs]])

    cur_g = [0]

    def zchunk(col0, ncols, b, lhs_src=None):
        hnT, ThnT = hnTb[cur_g[0] % 2], ThnTb[cur_g[0] % 2]
        for k in range(8):
            lhsT = w_f[:, k, col0:col0 + ncols] if lhs_src is None else lhs_src[:, k, :]
            mm(bank(b, G, ncols), lhsT, hnT[:, k, :], k == 0, k == 7,
               [TWA if (lhs_src is not None or col0 < OFF_GA) else TWB] + ThnT, [PB[b]])

    def P1a(g):
        for j in range(4):
            act(junk[j % 2], xt[j], AF.Square, [Txt[j]], [Tjunk[j % 2], Tssx], accum=ssx[:, j:j + 1])
        ts("dve", ssx, ssx, float(D * EPS), ALU.add, [Tssx], [Tssx])
        rsqrt_newton(ssx, nty, ntt, Tssx, Tnty, Tntt, 3)
        cp("dve", rstd_all[:, 4 * g:4 * g + 4], nty, [Tnty], [Trstd[g]])

    def P1b_scale(g, j):
        t = 4 * g + j
        hs = j % 2
        prog.op("act", lambda e, o=hn[hs], i=xt[j], sc=rstd_all[:, t:t + 1]: e.activation(out=o, in_=i, func=AF.Copy, scale=sc),
                reads=[Txt[j], Trstd[g]], writes=[Thn[hs]])
        if t + 4 < NT:
            load_x(t + 4)

    def P1b_tr(g, j):
        hs = j % 2
        hnT, ThnT = hnTb[g % 2], ThnTb[g % 2]
        b = next_bank()
        for k in range(8):
            prog.op("pe", lambda e, o=psum16[:, b * 1024 + k * 128:b * 1024 + (k + 1) * 128],
                    i=hn[hs][:, k * 128:(k + 1) * 128]: e.transpose(out=o, in_=i, identity=ident),
                    reads=[Thn[hs], Tident], writes=[PB[b]])
        cp("act", hnT[:, :, j * 128:(j + 1) * 128],
           psum16[:, b * 1024:(b + 1) * 1024].rearrange("p (k c) -> p k c", c=128),
           [PB[b]], [ThnT[j]])

    def P1b_steps(g):
        P1b_scale(g, 0)
        yield
        for j in range(4):
            if j + 1 < 4:
                P1b_scale(g, j + 1)
            P1b_tr(g, j)
            yield

    def drain(gen):
        if gen is not None:
            for _ in gen:
                pass

    def step(gen):
        if gen is not None:
            next(gen, None)

    def P2(g):
        cur_g[0] = g
        cT, TcT = cTb[g % 2], TcTb[g % 2]
        for c in range(5):
            b = next_bank()
            zchunk(c * 128, 128, b)
            act(sq[:, c, :], bank(b), AF.Square, [PB[b]], [Tsq[c]])
            cp("act", zs[:, c, :], bank(b), [PB[b]], [Tzs[c]])
        bq = next_bank()
        for c in range(3):
            mm(bank(bq), ones_bf, sq[:, c, :], c == 0, c == 2, [Tones, Tsq[c]], [PB[bq]])
        bk = next_bank()
        for c in range(2):
            mm(bank(bk), ones_bf, sq[:, 3 + c, :], c == 0, c == 1, [Tones, Tsq[3 + c]], [PB[bk]])
        ts("dve", ssn[:, 0:G], bank(bq), float(QL * EPS), ALU.add, [PB[bq]], [Tssn])
        ts("dve", ssn[:, G:2 * G], bank(bk), float(KVL * EPS), ALU.add, [PB[bk]], [Tssn])

    def P2b(g):
        cT, TcT = cTb[g % 2], TcTb[g % 2]
        rsqrt_newton(ssn, nry, nrt, Tssn, Tnry, Tnrt, 2)
        for c in range(3):
            stt(cT[:, c, :], zs[:, c, :], qn[:, c:c + 1], nry[:, 0:G], ALU.mult, ALU.mult,
                [Tzs[c], Tvec, Tnry], [TcT[c]])
        for c in range(2):
            stt(cT[:, 3 + c, :], zs[:, 3 + c, :], kvn[:, c:c + 1], nry[:, G:2 * G], ALU.mult, ALU.mult,
                [Tzs[3 + c], Tvec, Tnry], [TcT[3 + c]])

    def P3(g):
        cT, TcT = cTb[g % 2], TcTb[g % 2]
        cslot = g % 2
        for h in range(H):
            ba = next_bank()
            for k in range(3):
                mm(bank(ba, G, 96), w_q[:, k, h, :], cT[:, k, :], k == 0, k == 2, [TW1, TcT[k]], [PB[ba]])
            bb = next_bank()
            for k in range(3):
                mm(bank(bb, G, 96), w_qs[:, k, h, :], cT[:, k, :], k == 0, k == 2, [TW1, TcT[k]], [PB[bb]])
            cp("act", qst[0:64, h, :], psum[0:64, ba * 512:ba * 512 + G], [PB[ba]], [Tqst[h]])
            rs = h % 2
            tt("dve", r1[rs][64:96, :], psum[64:96, ba * 512:ba * 512 + G], cs[cslot][64:96, :], ALU.mult, [PB[ba], Tcs[cslot]], [Tr1[rs]])
            tt("dve", r2[rs][64:96, :], psum[64:96, bb * 512:bb * 512 + G], sn[cslot][64:96, :], ALU.mult, [PB[bb], Tcs[cslot]], [Tr2[rs]])
            tt("pool", qst[64:96, h, :], r1[rs][64:96, :], r2[rs][64:96, :], ALU.add, [Tr1[rs], Tr2[rs]], [Tqst[h]])
        dma("sp", qt_d[:, :, g * G:(g + 1) * G].rearrange("h r t -> r h t"), qst[0:96, :, :], "qst",
            reads=Tqst, writes=[Tqt_d[h][g] for h in range(H)])
        for h in range(H):
            b = next_bank()
            for k in range(2):
                mm(bank(b, G, 64), w_kv[:, k, h * 128:h * 128 + 64], cT[:, 3 + k, :], k == 0, k == 1,
                   [TW1, TcT[3 + k]], [PB[b]])
            cp("act", kst[0:64, h, :], psum[0:64, b * 512:b * 512 + G], [PB[b]], [Tkst[h]])
        dma("sp", kt_d[:, :, g * G:(g + 1) * G].rearrange("h r t -> r h t"), kst[0:96, :, :], "kst",
            reads=Tkst, writes=[Tkt_d[h][g] for h in range(H)])
        wv = w_kv.rearrange("p k (h c) -> p k h c", c=128)
        for j in range(4):
            b = next_bank()
            for k in range(2):
                mm(bank(b).rearrange("p (h c) -> p h c", c=64), cT[:, 3 + k, j * 128:(j + 1) * 128],
                   wv[:, k, :, 64:128], k == 0, k == 1, [TW1, TcT[3 + k]], [PB[b]])
            cp("act", vst[:, j, :], bank(b), [PB[b]], [Tvst[j]])
        dma("sp", v_d[g * G:(g + 1) * G, :].rearrange("(j p) c -> p j c", p=128), vst, "vst",
            reads=Tvst, writes=[Tv_d[g]])

    def P4a(g, nxt=None):
        cur_g[0] = g
        cslot = g % 2
        b1 = next_bank()
        zchunk(OFF_ZKR - 64, 96, b1)
        b2 = next_bank()
        zchunk(0, 96, b2, lhs_src=w_krs)
        tt("dve", r1[1][64:96, :], psum[64:96, b1 * 512:b1 * 512 + G], cs[cslot][64:96, :], ALU.mult, [PB[b1], Tcs[cslot]], [Tr1[1]])
        tt("dve", r2[1][64:96, :], psum[64:96, b2 * 512:b2 * 512 + G], sn[cslot][64:96, :], ALU.mult, [PB[b2], Tcs[cslot]], [Tr2[1]])
        tt("pool", kst[64:96, :, :], r1[1][64:96, :].unsqueeze(1).broadcast_to([32, H, G]),
           r2[1][64:96, :].unsqueeze(1).broadcast_to([32, H, G]), ALU.add, [Tr1[1], Tr2[1]], Tkst)
        step(nxt)
        for c in range(4):
            b = next_bank()
            zchunk(OFF_GA + c * 128, 128, b)
            act(th[:, c, :], bank(b), AF.Tanh, [PB[b]], [Tth[c]], scale=0.5)
            stt(sgst[:, c, :], th[:, c, :], 1.0, bank(b), ALU.add, ALU.mult, [Tth[c], PB[b]], [Tsgst[c]])
            if c % 2 == 1:
                step(nxt)
        dma("sp", sg_d[:, g * G:(g + 1) * G].rearrange("(c p) t -> p c t", p=128), sgst, "sgst",
            reads=Tsgst, writes=[Tsg_d[g]])

    def P4b(g, nxt=None):
        cur_g[0] = g
        for c in range(4):
            cp("pool", ub[:, c, 0:16], ubs[:, c, :], [Tubs[c]], [Tubh[c]])
            b = next_bank()
            zchunk(OFF_UP + c * 128, 128, b)
            cp("act", ub[:, c, 16:16 + G], bank(b), [PB[b]], [Tub[c]])
            act(wu[:, c, :], bank(b), AF.Copy, [PB[b]], [Twu[c]], scale=float(POOL_W[c]))
            if c % 2 == 1:
                step(nxt)
        for c in range(4):
            b = next_bank()
            zchunk(OFF_GP + c * 128, 128, b)
            act(th[:, 4 + c, :], bank(b), AF.Tanh, [PB[b]], [Tth[4 + c]], scale=0.5)
            stt(sgp[:, c, :], th[:, 4 + c, :], 1.0, bank(b), ALU.add, ALU.mult, [Tth[4 + c], PB[b]], [Tsgp[c]])

    def P4p(g):
        W = 16 + G
        for c in range(4):
            src, Tsrc = ub[:, c, :], Tub[c]
            cur, Tcur = src, Tsrc
            lo = 0
            dsts = [(pa, Tpa), (pbuf, Tpb)]
            for lvl in range(c + 1):
                sh = 1 << lvl
                dst, Tdst = dsts[lvl % 2]
                nlo = lo + sh
                tt("pool", dst[:, nlo:W], cur[:, nlo:W], cur[:, nlo - sh:W - sh], ALU.add,
                   [Tcur] + ([Tubh[c]] if lvl == 0 else []), [Tdst])
                cur, Tcur, lo = dst, Tdst, nlo
            tt("pool", dT[:, c, :], cur[:, 16:W], wu[:, c, :], ALU.subtract, [Tcur, Twu[c]], [TdT[c]])
            if g == 0:
                tt("pool", pfx, cur[:, 16:32], pcnt[:, c, :], ALU.mult, [Tcur, Tpcnt], [Tpfx])
                tt("pool", dT[:, c, 0:16], pfx, wu[:, c, 0:16], ALU.subtract, [Tpfx, Twu[c]], [TdT[c]])
            cp("pool", ubs[:, c, :], src[:, G:G + 16], [Tsrc], [Tubs[c]])

    def P4c(g):
        for c in range(4):
            b = next_bank()
            mm(bank(b), w_pl[:, c, :], dT[:, c, :], True, True, [TW1, TdT[c]], [PB[b]])
            stt(ypst[:, c, :], bank(b), psc[:, c:c + 1], sgp[:, c, :], ALU.mult, ALU.mult,
                [PB[b], Tvec, Tsgp[c]], [Typst[c]])
        dma("sp", yp_d[:, g * G:(g + 1) * G].rearrange("(c p) t -> p c t", p=128), ypst, "ypst",
            reads=Typst, writes=[Typ_d[g]])

    for t in range(min(4, NT)):
        load_x(t)
    for g in range(NG):
        cslot = g % 2
        dma("sp", cs[cslot][64:96, :], cos_d[:, g * G:(g + 1) * G], "cs%d" % cslot, writes=[Tcs[cslot]])
        dma("sp", sn[cslot][64:96, :], sin_d[:, g * G:(g + 1) * G], "cs%d" % cslot, writes=[Tcs[cslot]])
        if g == 0:
            P1a(0)
            drain(P1b_steps(0))
        P2(g)
        if g == 0:
            fold_wB()
        if g > 0:
            P3(g - 1)
            P4c(g - 1)
        nxt = None
        if g + 1 < NG:
            P1a(g + 1)
            nxt = P1b_steps(g + 1)
        P4a(g, nxt)
        P2b(g)
        P4b(g, nxt)
        P4p(g)
        drain(nxt)
    P3(NG - 1)
    P4c(NG - 1)

    if stop >= 2:
        prog.barrier()
        wmem.reset()
        w_m = wmem.alloc(BF16, [8, 2048])
        w_ba = wmem.alloc(BF16, [4, D])
        w_bp = wmem.alloc(BF16, [4, D])
        w_o = wmem.alloc(BF16, [8, D])
        TW3 = T("w3")
        work.reset()
        KTb = [work.alloc(BF16, [S]) for _ in range(2)]
        TKT = [T("KT%d" % i) for i in range(2)]
        VAb = [work.alloc(BF16, [NT, 65]) for _ in range(2)]
        TVA = [T("VA%d" % i) for i in range(2)]
        QTb = [work.alloc(BF16, [G]) for _ in range(3)]
        TQT = [T("QT%d" % i) for i in range(3)]
        NPT = 3
        PT = [work.alloc(BF16, [2 * G]) for _ in range(NPT)]
        TPT = [T("PT%d" % i) for i in range(NPT)]
        ost = [work.alloc(F32, [G]) for _ in range(2)]
        Tost = [T("ost%d" % i) for i in range(2)]
        for i in range(2):
            memset("pool", VAb[i][:, :, 64:65], 1.0, [TVA[i]])

        import os as _os4
        NDUMMY = int(_os4.environ.get("KDUMMY", "0"))
        SU = [(0, 1), (2, 3), (4, 5)]
        FILL_N = int(_os4.environ.get("KFILLN", "128"))
        su_rr = [0]
        pt_rr = [0]

        def load_head(h, s):
            dma("pool", KTb[s][0:96, :], kt_d[h, :, :], "KT%d" % s,
                reads=[Tkt_d[h][g] for g in range(NG)], writes=[TKT[s]])
            dma("pool", VAb[s][:, :, 0:64], v_d[:, h * 64:(h + 1) * 64].rearrange("(t p) c -> p t c", p=128),
                "VA%d" % s, reads=Tv_d, writes=[TVA[s]])

        def load_q(h, g, s):
            dma("sp", QTb[s][0:96, :], qt_d[h, :, g * G:(g + 1) * G], "QT%d" % s,
                reads=[Tqt_d[h][g]], writes=[TQT[s]])

        hq = [(h, g) for h in range(H) for g in range(NG)]
        allu = []
        for idx, (h, g) in enumerate(hq):
            units = []
            for p in range(2 * g):
                units.append(([2 * p, 2 * p + 1], [(0, 0), (G, 0)], False))
            units.append(([4 * g, 4 * g + 1], [(0, 0), (G, 128)], True))
            units.append(([4 * g + 2, 4 * g + 3], [(256, 256), (G, 384)], True))
            for ui, (kts, q0, pvq) in enumerate(units):
                allu.append((idx, kts, q0, ui == 0, ui == len(units) - 1, pvq))
        def emit_scores(i):
            idx, kts, q0, first, last, pvq = allu[i]
            h, g = hq[idx]
            hs, qs = h % 2, idx % 3
            if first:
                if idx >= 2:
                    w3_issue(1 if NG * H >= 40 else 3)
                if idx >= 1 and idx + 2 < len(hq):
                    load_q(hq[idx + 2][0], hq[idx + 2][1], (idx + 2) % 3)
            sb = SU[su_rr[0]]
            su_rr[0] = (su_rr[0] + 1) % len(SU)
            for k_, kt in enumerate(kts):
                b = sb[k_]
                rc, qk = q0[k_]
                mm(psum[:, sb[0] * 512 + rc:sb[0] * 512 + rc + G - qk], KTb[hs][0:96, kt * 128:(kt + 1) * 128],
                   QTb[qs][0:96, qk:G], True, True, [TKT[hs], TQT[qs]], [PB[b]])
            return sb

        def emit_exp_pv(i, sb):
            idx, kts, q0, first, last, pvq = allu[i]
            h, g = hq[idx]
            hs, qs = h % 2, idx % 3
            ob_ = 6 + idx % 2
            if first and g == 0 and h + 1 < H:
                load_head(h + 1, 1 - hs)
            pi = pt_rr[0]
            pt_rr[0] = (pi + 1) % NPT
            lo = q0[0][0]
            hi = q0[-1][0] + G - q0[-1][1]
            act(PT[pi][:, lo:hi], psum[:, sb[0] * 512 + lo:sb[0] * 512 + hi], AF.Exp,
                [PB[sb[0]], PB[sb[1]]], [TPT[pi]], scale=SM_SCALE)
            if pvq:
                for k_ in range(2):
                    memset("dve", PT[pi][64:128, q0[k_][0]:q0[k_][0] + 64], 0.0, [TPT[pi]])
            for k_, kt in enumerate(kts):
                rc, qk = q0[k_]
                mm(psum[0:65, ob_ * 512 + qk:ob_ * 512 + G], VAb[hs][:, kt, :], PT[pi][:, rc:rc + G - qk],
                   first and k_ == 0, last and k_ == len(kts) - 1, [TVA[hs], TPT[pi]], [PB[ob_]])
            for _ in range(NDUMMY):
                mm(bank(7, FILL_N), KTb[hs][0:96, 0:128], QTb[qs][0:96, 0:FILL_N], True, True, [TKT[hs], TQT[qs]], [PB[7]])
            if last:
                os_ = idx % 2
                cp("dve", ost[os_][0:65, :], psum[0:65, ob_ * 512:ob_ * 512 + G], [PB[ob_]], [Tost[os_]])
                dma("sp", o_d[h * 64:(h + 1) * 64, g * G:(g + 1) * G], ost[os_][0:64, :], "ost%d" % os_,
                    reads=[Tost[os_]], writes=[To_d[h][g]])
                dma("sp", l_d[h:h + 1, g * G:(g + 1) * G], ost[os_][64:65, :], "ost%d" % os_,
                    reads=[Tost[os_]], writes=[Tl_d[h][g]])
                if fold_todo and not w3_jobs and idx >= FOLD0:
                    k = fold_todo.pop(0)
                    ts("dve", w_m[:, k, :], w_m[:, k, :], gcol[:, k:k + 1], ALU.mult, [TW3, Tgin], [TW3])

        load_head(0, 0)
        load_q(0, 0, 0)
        for i_ in range(1, min(3, len(hq))):
            load_q(hq[i_][0], hq[i_][1], i_)
        w3_jobs = []
        for k in range(8):
            w3_jobs.append((w_m[:, k, :], w_in_v[:, k, OFF_GM:OFF_GM + 2048]))
        w3_jobs.append((w_ba, w_ba_d.rearrange("(k p) c -> p k c", p=128)))
        w3_jobs.append((w_bp, w_bp_d.rearrange("(k p) c -> p k c", p=128)))
        for k in range(8):
            w3_jobs.append((w_o[:, k, :], w_out_d.rearrange("(k p) c -> p k c", p=128)[:, k, :]))

        def w3_issue(n):
            for _ in range(n):
                if w3_jobs:
                    o_, i_ = w3_jobs.pop(0)
                    dma("pool", o_, i_, "w3", writes=[TW3], group=True)

        LA = 2
        fold_todo = list(range(8))
        FOLD0 = (len(hq) * 5) // 8
        pendq = [emit_scores(i) for i in range(min(LA, len(allu)))]
        for i in range(len(allu)):
            if i + LA < len(allu):
                pendq.append(emit_scores(i + LA))
            emit_exp_pv(i, pendq.pop(0))
        w3_issue(len(w3_jobs))
        for k in fold_todo:
            ts("dve", w_m[:, k, :], w_m[:, k, :], gcol[:, k:k + 1], ALU.mult, [TW3, Tgin], [TW3])

    if stop >= 3:
        prog.barrier()
        work.reset()
        NXS3 = 5
        xt = [work.alloc(F32, [D]) for _ in range(NXS3)]
        Txt = [T("x3t%d" % i) for i in range(NXS3)]
        hn = [work.alloc(BF16, [D]) for _ in range(2)]
        Thn = [T("h3n%d" % i) for i in range(2)]
        hnT = work.alloc(BF16, [8, G]); ThnT = [T("h3nT%d" % i) for i in range(4)]
        tg = work.alloc(BF16, [16, G]); Ttg = [T("tg%d" % i) for i in range(16)]
        oin = [work.alloc(F32, [4, G]) for _ in range(2)]; Toin = [T("oin%d" % i) for i in range(2)]
        lin = work.alloc(F32, [4, G]); Tlin = [T("lin%d" % i) for i in range(4)]
        sgin = [work.alloc(BF16, [4, G]) for _ in range(2)]; Tsgin = [T("sgin%d" % i) for i in range(2)]
        ypin = [work.alloc(BF16, [4, G]) for _ in range(2)]; Typin = [T("ypin%d" % i) for i in range(2)]
        ya = work.alloc(BF16, [4, G]); Tya = [T("ya%d" % i) for i in range(4)]
        m1 = [work.alloc(F32, [G]) for _ in range(2)]; Tm1 = [T("m1_%d" % i) for i in range(2)]
        m2 = [work.alloc(F32, [G]) for _ in range(2)]; Tm2 = [T("m2_%d" % i) for i in range(2)]
        mg = work.alloc(BF16, [8, G]); Tmg = [T("mg%d" % i) for i in range(8)]
        hh = [work.alloc(F32, [D]) for _ in range(4)]; Thh = [T("hh%d" % i) for i in range(4)]
        junk3 = [work.alloc(BF16, [D])] * 4; Tjunk3 = [T("j3")] * 4
        ss3 = work.alloc(F32, [4]); Tss3 = T("ss3")
        ny3 = work.alloc(F32, [4]); Tny3 = T("ny3")
        nt3 = work.alloc(F32, [4]); Tnt3 = T("nt3")

        import os as _os2
        if _os2.environ.get("KDBG"):
            print("DBG phase3 work used", work.off - work.lo, "of", work.hi - work.lo)

        def load_g3(g, s):
            dma("sp", oin[s], o_d[:, g * G:(g + 1) * G].rearrange("(c p) t -> p c t", p=128), "oin%d" % s,
                reads=[To_d[h][g] for h in range(H)], writes=[Toin[s]])
            dma("sp", sgin[s], sg_d[:, g * G:(g + 1) * G].rearrange("(c p) t -> p c t", p=128), "sgin%d" % s,
                reads=[Tsg_d[g]], writes=[Tsgin[s]])
            dma("sp", ypin[s], yp_d[:, g * G:(g + 1) * G].rearrange("(c p) t -> p c t", p=128), "ypin%d" % s,
                reads=[Typ_d[g]], writes=[Typin[s]])

        def load_x3(t):
            xs = t % NXS3
            dma("sp", xt[xs], x_d[t * 128:(t + 1) * 128, :], "x3_%d" % xs, writes=[Txt[xs]])

        def load_lin(g):
            for hh_ in range(H):
                dma("sp", lin[(hh_ % 2) * 64:(hh_ % 2) * 64 + 64, hh_ // 2, :],
                    l_d[hh_, g * G:(g + 1) * G].partition_broadcast(64), "lin%d" % (hh_ // 2),
                    reads=[Tl_d[hh_][g]], writes=[Tlin[hh_ // 2]])

        def stageA3(g, j):
            t = 4 * g + j
            xs = t % NXS3
            ts("dve", hn[j % 2], xt[xs], rstd_all[:, t:t + 1], ALU.mult, [Txt[xs], Trstd[g]], [Thn[j % 2]])
            b = next_bank()
            for k in range(8):
                prog.op("pe", lambda e, o=psum16[:, b * 1024 + k * 128:b * 1024 + (k + 1) * 128],
                        i=hn[j % 2][:, k * 128:(k + 1) * 128]: e.transpose(out=o, in_=i, identity=ident),
                        reads=[Thn[j % 2], Tident], writes=[PB[b]])
            cp("act", hnT[:, :, j * 128:(j + 1) * 128],
               psum16[:, b * 1024:(b + 1) * 1024].rearrange("p (k c) -> p k c", c=128),
               [PB[b]], [ThnT[j]])

        load_g3(0, 0)
        for t in range(min(NXS3, NT)):
            load_x3(t)
        for g in range(NG):
            s = g % 2
            if g + 1 < NG:
                load_g3(g + 1, 1 - s)
            if g == 0:
                for j in range(4):
                    stageA3(0, j)
            for c in range(16):
                b = next_bank()
                for k in range(8):
                    mm(bank(b), w_m[:, k, c * 128:(c + 1) * 128], hnT[:, k, :], k == 0, k == 7, [TW3] + ThnT, [PB[b]])
                act(tg[:, c, :], bank(b), AF.Tanh, [PB[b]], [Ttg[c]], scale=0.5)
            if g == 0:
                load_lin(0)
            for c in range(4):
                tt("pool", oin[s][:, c, :], oin[s][:, c, :], sgin[s][:, c, :], ALU.mult, [Toin[s], Tsgin[s]], [Toin[s]])
                prog.op("dve", lambda e, o=lin[:, c, :]: e.reciprocal(out=o, in_=o), reads=[Tlin[c]], writes=[Tlin[c]])
                stt(ya[:, c, :], oin[s][:, c, :], 0.5, lin[:, c, :], ALU.mult, ALU.mult, [Toin[s], Tlin[c]], [Tya[c]])
            if g + 1 < NG:
                load_lin(g + 1)
            for dc in range(8):
                ba = next_bank()
                for k in range(4):
                    mm(bank(ba), w_ba[:, k, dc * 128:(dc + 1) * 128], ya[:, k, :], k == 0, k == 3, [TW3, Tya[k]], [PB[ba]])
                bp = next_bank()
                for k in range(4):
                    mm(bank(bp), w_bp[:, k, dc * 128:(dc + 1) * 128], ypin[s][:, k, :], k == 0, k == 3, [TW3, Typin[s]], [PB[bp]])
                ms = dc % 2
                stt(m1[ms], tg[:, dc, :], 1.0, bank(ba), ALU.add, ALU.mult, [Ttg[dc], PB[ba]], [Tm1[ms]])
                stt(m2[ms], tg[:, 8 + dc, :], 1.0, bank(bp), ALU.add, ALU.mult, [Ttg[8 + dc], PB[bp]], [Tm2[ms]])
                tt("pool", mg[:, dc, :], m1[ms], m2[ms], ALU.add, [Tm1[ms], Tm2[ms]], [Tmg[dc]])
            for j in range(4):
                t = 4 * g + j
                xs = t % NXS3
                hs = j
                b0 = next_bank()
                b1 = next_bank()
                for half, b in ((0, b0), (1, b1)):
                    for k in range(8):
                        mm(bank(b), mg[:, k, j * 128:(j + 1) * 128], w_o[:, k, half * 512:(half + 1) * 512],
                           k == 0, k == 7, [TW3, Tmg[k]], [PB[b]])
                if g + 1 < NG and j >= 1:
                    stageA3(g + 1, j - 1)
                for half, b in ((0, b0), (1, b1)):
                    stt(hh[hs][:, half * 512:(half + 1) * 512], bank(b), 0.5, xt[xs][:, half * 512:(half + 1) * 512],
                        ALU.mult, ALU.add, [PB[b], Txt[xs]], [Thh[hs]])
                if t + NXS3 < NT:
                    load_x3(t + NXS3)
                act(junk3[hs], hh[hs], AF.Square, [Thh[hs]], [Tjunk3[hs], Tss3], accum=ss3[:, j:j + 1])
                tt("pool", hh[hs], hh[hs], gfin, ALU.mult, [Thh[hs], Tgfin], [Thh[hs]])
            if g + 1 < NG:
                stageA3(g + 1, 3)
            ts("dve", ss3, ss3, float(D * EPS), ALU.add, [Tss3], [Tss3])
            rsqrt_newton(ss3, ny3, nt3, Tss3, Tny3, Tnt3, 3)
            for j in range(4):
                t = 4 * g + j
                prog.op("act", lambda e, o=hh[j], sc=ny3[:, j:j + 1]: e.activation(out=o, in_=o, func=AF.Copy, scale=sc),
                        reads=[Thh[j], Tny3], writes=[Thh[j]])
                dma("sp", out_d[t * 128:(t + 1) * 128, :], hh[j], "ob%d" % j, reads=[Thh[j]], writes=[T("outd")])

    import os as _os
    _cut = int(_os.environ.get("KCUT", "0"))
    if _cut:
        prog.ins = prog.ins[:_cut]
    prog.ins.append(dict(eng="sp", fn=None, deps=set(range(len(prog.ins))), dma=None, signal=False))

    totals = prog.resolve()
    with ExitStack() as es:
        sems = {}
        for k in totals:
            sems[k] = es.enter_context(nc.semaphore("s_%s_%s" % (k[0], k[1])))
        es.enter_context(nc.allow_low_precision("bf16 matmul operands, fp32 accumulation (problem tolerance calibrated for bf16)"))
        block = es.enter_context(nc.Block())

        def run(engname):
            def f(e):
                for it in prog.ins:
                    if it["eng"] != engname:
                        continue
                    for k, v in it["waits"]:
                        e.wait_ge(sems[k], v)
                    if it["fn"] is None:
                        continue
                    ins = it["fn"](e)
                    if it["signal"]:
                        k, v = it["sig"]
                        ins.then_inc(sems[k], 16 if k[0] == "dma" else 1)
            return f

        block.sync(run("sp"))
        block.tensor(run("pe"))
        block.scalar(run("act"))
        block.vector(run("dve"))
        block.gpsimd(run("pool"))
    return nc


def host_consts(S):
    half = ROPE // 2
    inv_freq = (np.float32(10000.0) ** (-np.arange(half, dtype=np.float32) / np.float32(half))).astype(np.float32)
    ang = (np.arange(S, dtype=np.float32)[:, None] * inv_freq[None, :]).astype(np.float32)
    cos = np.cos(ang.astype(np.float64)).astype(np.float32).T
    sin = np.sin(ang.astype(np.float64)).astype(np.float32).T
    cosT = np.concatenate([cos, cos], axis=0)
    sinT = np.concatenate([-sin, sin], axis=0)
    pcnt = np.zeros((4, 16), np.float32)
    for gi, w in enumerate(POOL_W):
        pcnt[gi] = float(w) / np.minimum(np.arange(16) + 1, w)
    return {
        "c_ident": np.eye(128, dtype=np.float32),
        "c_cos": np.ascontiguousarray(cosT),
        "c_sin": np.ascontiguousarray(sinT),
        "c_pcnt": pcnt,
    }


_NC_CACHE = {}


def kernel(x, norm_in, w_in, q_norm, w_uq, kv_norm, w_ukv, pool_w, pool_scale,
           w_branch_attn, w_branch_pool, w_out, norm_final):
    x = np.asarray(x, dtype=np.float32)
    B, S, _ = x.shape
    if S not in _NC_CACHE:
        _NC_CACHE[S] = build_nc(S)
    nc = _NC_CACHE[S]
    consts = host_consts(S)
    shared = {
        "norm_in": np.asarray(norm_in, np.float32), "w_in": np.asarray(w_in, np.float32),
        "q_norm": np.asarray(q_norm, np.float32), "w_uq": np.asarray(w_uq, np.float32),
        "kv_norm": np.asarray(kv_norm, np.float32), "w_ukv": np.asarray(w_ukv, np.float32),
        "pool_w": np.asarray(pool_w, np.float32), "pool_scale": np.asarray(pool_scale, np.float32),
        "w_branch_attn": np.asarray(w_branch_attn, np.float32),
        "w_branch_pool": np.asarray(w_branch_pool, np.float32),
        "w_out": np.asarray(w_out, np.float32), "norm_final": np.asarray(norm_final, np.float32),
    }
    shared.update(consts)
    in_maps = []
    for b in range(B):
        m = dict(shared)
        m["x"] = np.ascontiguousarray(x[b])
        in_maps.append(m)
    res = run_bass_kernel_spmd(nc, in_maps, core_ids=list(range(B)))
    return np.stack([np.asarray(r["out"], dtype=np.float32) for r in res.results], axis=0)
```

```python
import numpy as np
import concourse.bass as bass
import concourse.mybir as mybir
from concourse.bass_utils import run_bass_kernel_spmd
from contextlib import ExitStack

F32 = mybir.dt.float32
BF16 = mybir.dt.bfloat16
I32 = mybir.dt.int32
ALU = mybir.AluOpType
AF = mybir.ActivationFunctionType

D = 1024
H = 8
NOPE = 64
ROPE = 32
QKD = 96
VD = 64
QL = 384
KVL = 256
NFRONT = 2208
OFF_ZQ, OFF_ZKV, OFF_ZKR, OFF_GA, OFF_UP, OFF_GP, OFF_GM = 0, 384, 640, 672, 1184, 1696, 2208
IN_TOTAL = 4256
G = 512
EPS = 1e-6
POOL_W = (2, 4, 8, 16)
MAGIC = 0x5F3759DF
SM_SCALE = float(QKD ** -0.5)
FULL_S = 4096
NCORES = 8


class T:
    __slots__ = ("name", "w", "r", "excl")

    def __init__(self, name, excl=False):
        self.name = name
        self.w = None
        self.r = {}
        self.excl = excl


class Prog:
    COMPUTE = ("pe", "act", "dve", "pool")

    def __init__(self):
        self.ins = []
        self.bar_gen = 0
        self.bar_deps = set()
        self.eng_gen = {}
        self.last = {}
        self.dmas_since_bar = []
        self.groups = {}

    def op(self, eng, fn, reads=(), writes=(), dma=None, group=False):
        i = len(self.ins)
        deps = set()
        for t in reads:
            if t.w is not None:
                deps.add(t.w)
            if t.excl:
                for k, rid in t.r.items():
                    if k != eng:
                        deps.add(rid)
        for t in writes:
            if t.w is not None:
                deps.add(t.w)
            for rid in t.r.values():
                deps.add(rid)
        if self.eng_gen.get(eng, 0) < self.bar_gen:
            deps |= self.bar_deps
            self.eng_gen[eng] = self.bar_gen
        deps.discard(i)
        if dma is not None:
            deps = {d for d in deps if self.ins[d]["dma"] != dma}
        self.ins.append(dict(eng=eng, fn=fn, deps=deps, dma=dma, signal=dma is not None))
        if dma is not None:
            self.groups[dma] = group
            self.dmas_since_bar.append(i)
        else:
            self.last[eng] = i
        rkey = eng if dma is None else ("dma", i)
        for t in reads:
            t.r[rkey] = i
        for t in writes:
            t.w = i
            t.r = {}
        return i

    def barrier(self):
        self.bar_gen += 1
        self.bar_deps = set(self.last.values()) | set(self.dmas_since_bar)
        self.dmas_since_bar = []

    def resolve(self):
        ins = self.ins
        for it in ins:
            for d in it["deps"]:
                dd = ins[d]
                if dd["dma"] is None and dd["eng"] == it["eng"] == "pe" and it["dma"] is None:
                    continue
                dd["signal"] = True
        cnt = {}
        for it in ins:
            if it["dma"] is not None:
                k = ("dma", it["dma"])
                cnt[k] = cnt.get(k, 0) + 16
                it["sig"] = (k, cnt[k])
            elif it["signal"]:
                k = ("eng", it["eng"])
                cnt[k] = cnt.get(k, 0) + 1
                it["sig"] = (k, cnt[k])
        totals = dict(cnt)
        waited = {}
        for it in ins:
            need = {}
            for d in it["deps"]:
                dd = ins[d]
                if dd["dma"] is None and dd["eng"] == it["eng"] == "pe" and it["dma"] is None:
                    continue
                k, v = dd["sig"]
                if k[0] == "dma" and self.groups[k[1]]:
                    v = totals[k]
                if v > need.get(k, 0):
                    need[k] = v
            w = waited.setdefault(it["eng"], {})
            out = []
            for k, v in need.items():
                if w.get(k, 0) < v:
                    w[k] = v
                    out.append((k, v))
            it["waits"] = out
        return totals


def build_nc(S, stop=3):
    NG = S // G
    NT = S // 128
    nc = bass.Bass("TRN2", target_bir_lowering=False)

    def din(name, shape, dt=F32):
        return nc.dram_tensor(name, list(shape), dt, kind="ExternalInput").ap()

    def dscr(name, shape, dt=BF16):
        return nc.dram_tensor(name, list(shape), dt, kind="Internal").ap()

    x_d = din("x", [S, D])
    norm_in_d = din("norm_in", [D])
    w_in_d = din("w_in", [D, IN_TOTAL])
    q_norm_d = din("q_norm", [QL])
    w_uq_d = din("w_uq", [QL, H, QKD])
    kv_norm_d = din("kv_norm", [KVL])
    w_ukv_d = din("w_ukv", [KVL, H, 128])
    pool_w_d = din("pool_w", [4, 128, 128])
    pool_scale_d = din("pool_scale", [512])
    w_ba_d = din("w_branch_attn", [512, D])
    w_bp_d = din("w_branch_pool", [512, D])
    w_out_d = din("w_out", [D, D])
    norm_final_d = din("norm_final", [D])
    ident_d = din("c_ident", [128, 128])
    cos_d = din("c_cos", [32, S])
    sin_d = din("c_sin", [32, S])
    pcnt_d = din("c_pcnt", [4, 16])
    out_d = nc.dram_tensor("out", [S, D], F32, kind="ExternalOutput").ap()

    qt_d = dscr("s_qt", [H, QKD, S])
    kt_d = dscr("s_kt", [H, QKD, S])
    v_d = dscr("s_v", [S, 512])
    sg_d = dscr("s_sg", [512, S])
    yp_d = dscr("s_yp", [512, S])
    o_d = dscr("s_o", [512, S], F32)
    l_d = dscr("s_l", [H, S], F32)

    ARENA = 207 * 1024
    arena = nc.alloc_sbuf_tensor("arena", [128, ARENA // 2], BF16)
    arena32 = arena.bitcast(F32)
    psum = nc.alloc_psum_tensor("psum", [128, 4096], F32)
    psum16 = psum.bitcast(BF16)

    class Bump:
        def __init__(self, lo, hi):
            self.lo, self.hi, self.off = lo, hi, lo

        def reset(self):
            self.off = self.lo

        def alloc(self, dt, shape):
            sz = 4 if dt in (F32, I32) else 2
            n = int(np.prod(shape))
            nb = (n * sz + 31) // 32 * 32
            assert self.off + nb <= self.hi, ("arena overflow", self.off, nb, self.hi)
            o = self.off
            self.off += nb
            if sz == 4:
                ap = arena32[:, o // 4:o // 4 + n]
            else:
                ap = arena[:, o // 2:o // 2 + n]
            if len(shape) == 2:
                ap = ap.rearrange("p (a b) -> p a b", b=shape[1])
            elif len(shape) == 3:
                ap = ap.rearrange("p (a b c) -> p a b c", b=shape[1], c=shape[2])
            return ap

    WREG = 68 * 1024
    wmem = Bump(0, WREG)
    cmem = Bump(WREG, WREG + 10 * 1024)
    work = Bump(WREG + 10 * 1024, ARENA)

    def bank(b, n=512, parts=128, off=0):
        return psum[0:parts, b * 512 + off:b * 512 + off + n]

    PB = [T("psum%d" % b, excl=True) for b in range(8)]
    prog = Prog()
    pb_rr = [0]

    def next_bank():
        b = pb_rr[0]
        pb_rr[0] = (b + 1) % 8
        return b

    def dma(q, out, in_, key, reads=(), writes=(), group=False):
        prog.op(q, lambda e, o=out, i=in_: e.dma_start(out=o, in_=i), reads=reads, writes=writes,
                dma=key, group=group)

    def mm(out, lhsT, rhs, start, stop, reads, writes):
        prog.op("pe", lambda e, o=out, l=lhsT, r=rhs, s=start, p=stop: e.matmul(o, lhsT=l, rhs=r, start=s, stop=p),
                reads=reads, writes=writes)

    def act(out, in_, func, reads, writes, scale=1.0, accum=None):
        if accum is None:
            prog.op("act", lambda e, o=out, i=in_, f=func, s=scale: e.activation(out=o, in_=i, func=f, scale=s),
                    reads=reads, writes=writes)
        else:
            prog.op("act", lambda e, o=out, i=in_, f=func, s=scale, a=accum: e.activation(out=o, in_=i, func=f, scale=s, accum_out=a),
                    reads=reads, writes=writes)

    def tt(eng, out, in0, in1, op, reads, writes):
        prog.op(eng, lambda e, o=out, a=in0, b=in1, p=op: e.tensor_tensor(out=o, in0=a, in1=b, op=p),
                reads=reads, writes=writes)

    def ts(eng, out, in0, s1, op0, reads, writes, s2=None, op1=None):
        if op1 is None:
            prog.op(eng, lambda e, o=out, a=in0, x=s1, p=op0: e.tensor_scalar(out=o, in0=a, scalar1=x, scalar2=None, op0=p),
                    reads=reads, writes=writes)
        else:
            prog.op(eng, lambda e, o=out, a=in0, x=s1, y=s2, p=op0, q=op1: e.tensor_scalar(out=o, in0=a, scalar1=x, scalar2=y, op0=p, op1=q),
                    reads=reads, writes=writes)

    def stt(out, in0, scalar, in1, op0, op1, reads, writes):
        prog.op("dve", lambda e, o=out, a=in0, s=scalar, b=in1, p=op0, q=op1: e.scalar_tensor_tensor(out=o, in0=a, scalar=s, in1=b, op0=p, op1=q),
                reads=reads, writes=writes)

    def cp(eng, out, in_, reads, writes):
        if eng == "act":
            prog.op("act", lambda e, o=out, i=in_: e.copy(out=o, in_=i), reads=reads, writes=writes)
        else:
            prog.op(eng, lambda e, o=out, i=in_: e.tensor_copy(out=o, in_=i), reads=reads, writes=writes)

    def memset(eng, ap, val, writes):
        prog.op(eng, lambda e, a=ap, v=val: e.memset(a, v), writes=writes)

    def rsqrt_newton(v_ap, y_ap, t_ap, Tv, Ty, Tt, iters):
        ts("dve", t_ap.bitcast(I32), v_ap.bitcast(I32), 1, ALU.logical_shift_right, [Tv], [Tt])
        ts("dve", y_ap.bitcast(I32), t_ap.bitcast(I32), -1, ALU.mult, [Tt], [Ty], s2=MAGIC, op1=ALU.add)
        for _ in range(iters):
            stt(t_ap, y_ap, -0.5, y_ap, ALU.mult, ALU.mult, [Ty], [Tt])
            tt("dve", t_ap, t_ap, v_ap, ALU.mult, [Tt, Tv], [Tt])
            stt(y_ap, t_ap, 1.5, y_ap, ALU.add, ALU.mult, [Tt, Ty], [Ty])

    ident = cmem.alloc(BF16, [128])
    Tident = T("ident")
    gcol = cmem.alloc(F32, [8])
    Tgin = T("gcol")
    gfin = cmem.alloc(F32, [D])
    Tgfin = T("gfin")
    qn = cmem.alloc(F32, [3])
    kvn = cmem.alloc(F32, [2])
    psc = cmem.alloc(F32, [4])
    Tvec = T("vecs")
    rstd_all = cmem.alloc(F32, [NT])
    Trstd = [T("rstd%d" % g) for g in range(NG)]
    ones_bf = cmem.alloc(BF16, [128])
    half_f = cmem.alloc(F32, [64])
    Tones = T("ones")
    pcnt = cmem.alloc(F32, [4, 16])
    Tpcnt = T("pcnt")

    dma("pool", ident, ident_d, "wI", writes=[Tident], group=True)
    for k in range(8):
        dma("sp", gcol[:, k:k + 1], norm_in_d[k * 128:(k + 1) * 128].rearrange("(p o) -> p o", o=1), "c0", writes=[Tgin], group=True)
    dma("sp", gfin, norm_final_d.partition_broadcast(128), "c0", writes=[Tgfin], group=True)
    for c in range(3):
        dma("sp", qn[:, c:c + 1], q_norm_d[c * 128:(c + 1) * 128].rearrange("(p o) -> p o", o=1), "c0", writes=[Tvec], group=True)
    for c in range(2):
        dma("sp", kvn[:, c:c + 1], kv_norm_d[c * 128:(c + 1) * 128].rearrange("(p o) -> p o", o=1), "c0", writes=[Tvec], group=True)
    for c in range(4):
        dma("sp", psc[:, c:c + 1], pool_scale_d[c * 128:(c + 1) * 128].rearrange("(p o) -> p o", o=1), "c0", writes=[Tvec], group=True)
    dma("sp", pcnt.rearrange("p a b -> p (a b)"), pcnt_d.rearrange("a b -> (a b)").partition_broadcast(128), "c0",
        writes=[Tpcnt], group=True)
    ts("dve", gcol, gcol, 32.0, ALU.mult, [Tgin], [Tgin])
    ts("dve", gfin, gfin, 32.0, ALU.mult, [Tgfin], [Tgfin])
    ts("dve", qn, qn, float(np.sqrt(QL)), ALU.mult, [Tvec], [Tvec])
    ts("dve", kvn, kvn, 16.0, ALU.mult, [Tvec], [Tvec])
    for gi, w in enumerate(POOL_W):
        ts("dve", psc[:, gi:gi + 1], psc[:, gi:gi + 1], 0.5 / w, ALU.mult, [Tvec], [Tvec])
    memset("dve", ones_bf, 1.0, [Tones])
    memset("dve", half_f, 0.5, [Tones])

    w_f = wmem.alloc(BF16, [8, NFRONT])
    w_krs = wmem.alloc(BF16, [8, 96])
    w_q = wmem.alloc(BF16, [3, H, QKD])
    w_qs = wmem.alloc(BF16, [3, H, QKD])
    w_kv = wmem.alloc(BF16, [2, 1024])
    w_pl = wmem.alloc(BF16, [4, 128])
    TW1 = T("w1")
    TWA = T("wA")
    TWB = T("wB")
    memset("pool", w_krs[:, :, 0:64], 0.0, [TWA])
    memset("pool", w_qs[:, :, :, 0:64], 0.0, [TW1])
    w_in_v = w_in_d.rearrange("(k p) c -> p k c", p=128)
    for k in range(8):
        dma("pool", w_f[:, k, 0:OFF_GA], w_in_v[:, k, 0:OFF_GA], "wA", writes=[TWA], group=True)
    dma("pool", w_krs[:, :, 64:80], w_in_v[:, :, OFF_ZKR + 16:OFF_ZKR + 32], "wA", writes=[TWA], group=True)
    dma("pool", w_krs[:, :, 80:96], w_in_v[:, :, OFF_ZKR:OFF_ZKR + 16], "wA", writes=[TWA], group=True)
    for k in range(8):
        dma("pool", w_f[:, k, OFF_GA:NFRONT], w_in_v[:, k, OFF_GA:NFRONT], "wB", writes=[TWB], group=True)
    w_uq_v = w_uq_d.rearrange("(k p) h c -> p k h c", p=128)
    for k in range(3):
        dma("pool", w_q[:, k, :, :], w_uq_v[:, k, :, :], "w1", writes=[TW1], group=True)
        dma("pool", w_qs[:, k, :, 64:80], w_uq_v[:, k, :, 80:96], "w1", writes=[TW1], group=True)
        dma("pool", w_qs[:, k, :, 80:96], w_uq_v[:, k, :, 64:80], "w1", writes=[TW1], group=True)
    w_ukv_v = w_ukv_d.rearrange("(k p) h c -> p k (h c)", p=128)
    for k in range(2):
        dma("pool", w_kv[:, k, :], w_ukv_v[:, k, :], "w1", writes=[TW1], group=True)
    dma("pool", w_pl, pool_w_d.rearrange("g c d -> c g d"), "w1", writes=[TW1], group=True)
    for k in range(8):
        ts("dve", w_f[:, k, 0:OFF_GA], w_f[:, k, 0:OFF_GA], gcol[:, k:k + 1], ALU.mult, [TWA, Tgin], [TWA])
        ts("dve", w_krs[:, k, 64:96], w_krs[:, k, 64:96], gcol[:, k:k + 1], ALU.mult, [TWA, Tgin], [TWA])

    def fold_wB():
        for k in range(8):
            ts("dve", w_f[:, k, OFF_GA:NFRONT], w_f[:, k, OFF_GA:NFRONT], gcol[:, k:k + 1], ALU.mult, [TWB, Tgin], [TWB])
    xt = [wmem.alloc(F32, [D]) for _ in range(4)]

    work.reset()
    NXS = 4
    Txt = [T("xt%d" % i) for i in range(NXS)]
    junk = [work.alloc(BF16, [D]) for _ in range(2)]
    Tjunk = [T("junk%d" % i) for i in range(2)]
    ssx = work.alloc(F32, [4]); Tssx = T("ssx")
    nty = work.alloc(F32, [4]); Tnty = T("nty")
    ntt = work.alloc(F32, [4]); Tntt = T("ntt")
    hn = [work.alloc(BF16, [D]) for _ in range(2)]
    Thn = [T("hn%d" % i) for i in range(2)]
    hnTb = [work.alloc(BF16, [8, G]) for _ in range(2)]
    ThnTb = [[T("hnT%d_%d" % (s_, i)) for i in range(4)] for s_ in range(2)]
    zs = work.alloc(F32, [5, G]); Tzs = [T("zs%d" % i) for i in range(5)]
    sq = work.alloc(BF16, [5, G]); Tsq = [T("sq%d" % i) for i in range(5)]
    ssn = work.alloc(F32, [2 * G]); Tssn = T("ssn")
    nry = work.alloc(F32, [2 * G]); Tnry = T("nry")
    nrt = work.alloc(F32, [2 * G]); Tnrt = T("nrt")
    cTb = [work.alloc(BF16, [5, G]) for _ in range(2)]
    TcTb = [[T("cT%d_%d" % (s_, i)) for i in range(5)] for s_ in range(2)]
    cs = [work.alloc(F32, [G]) for _ in range(2)]
    sn = [work.alloc(F32, [G]) for _ in range(2)]
    Tcs = [T("cs%d" % i) for i in range(2)]
    r1 = [work.alloc(BF16, [G]) for _ in range(2)]
    r2 = [work.alloc(BF16, [G]) for _ in range(2)]
    Tr1 = [T("r1_%d" % i) for i in range(2)]
    Tr2 = [T("r2_%d" % i) for i in range(2)]
    wu = work.alloc(BF16, [4, G]); Twu = [T("wu%d" % c) for c in range(4)]
    pfx = work.alloc(F32, [16]); Tpfx = T("pfx")
    qst = work.alloc(BF16, [H, G]); Tqst = [T("qst%d" % h) for h in range(H)]
    kst = work.alloc(BF16, [H, G]); Tkst = [T("kst%d" % h) for h in range(H)]; Tkr = T("kstrope")
    vst = work.alloc(BF16, [4, 512]); Tvst = [T("vst%d" % i) for i in range(4)]
    _th = work.alloc(BF16, [2, G]); _Tth = [T("th%d" % i) for i in range(2)]

    class _Rot:
        def __init__(self, a, n): self.a, self.n = a, n
        def __getitem__(self, i): return self.a[i % self.n]
    Tth = _Rot(_Tth, 2)

    class _RotAP:
        def __getitem__(self, key):
            p, c, f = key
            return _th[p, c % 2, f]
    th = _RotAP()
    sgst = work.alloc(BF16, [4, G]); Tsgst = [T("sgst%d" % i) for i in range(4)]
    sgp = work.alloc(BF16, [4, G]); Tsgp = [T("sgp%d" % i) for i in range(4)]
    ub = work.alloc(F32, [4, 16 + G])
    Tub = [T("ub_%d" % c) for c in range(4)]
    Tubh = [T("ubh_%d" % c) for c in range(4)]
    ubs = work.alloc(F32, [4, 16]); Tubs = [T("ubs%d" % c) for c in range(4)]
    pa = work.alloc(F32, [16 + G]); Tpa = T("pa")
    pbuf = work.alloc(F32, [16 + G]); Tpb = T("pbuf")
    dT = work.alloc(BF16, [4, G]); TdT = [T("dT%d" % i) for i in range(4)]
    ypst = work.alloc(BF16, [4, G]); Typst = [T("ypst%d" % i) for i in range(4)]

    Tqt_d = [[T("qtd%d_%d" % (h, g)) for g in range(NG)] for h in range(H)]
    Tkt_d = [[T("ktd%d_%d" % (h, g)) for g in range(NG)] for h in range(H)]
    Tv_d = [T("vd%d" % g) for g in range(NG)]
    Tsg_d = [T("sgd%d" % g) for g in range(NG)]
    Typ_d = [T("ypd%d" % g) for g in range(NG)]
    To_d = [[T("od%d_%d" % (h, g)) for g in range(NG)] for h in range(H)]
    Tl_d = [[T("ld%d_%d" % (h, g)) for g in range(NG)] for h in range(H)]

    for c in range(4):
        memset("pool", ubs[:, c, :], 0.0, [Tubs[c]])

    def load_x(t):
        xs = t % 4
        dma("sp", xt[xs], x_d[t * 128:(t + 1) * 128, :], "x%d" % xs, writes=[Txt[xs]])

    cur_g = [0]

    def zchunk(col0, ncols, b, lhs_src=None):
        hnT, ThnT = hnTb[cur_g[0] % 2], ThnTb[cur_g[0] % 2]
        for k in range(8):
            lhsT = w_f[:, k, col0:col0 + ncols] if lhs_src is None else lhs_src[:, k, :]
            mm(bank(b, G, ncols), lhsT, hnT[:, k, :], k == 0, k == 7,
               [TWA if (lhs_src is not None or col0 < OFF_GA) else TWB] + ThnT, [PB[b]])

    def P1a(g):
        for j in range(4):
            act(junk[j % 2], xt[j], AF.Square, [Txt[j]], [Tjunk[j % 2], Tssx], accum=ssx[:, j:j + 1])
        ts("dve", ssx, ssx, float(D * EPS), ALU.add, [Tssx], [Tssx])
        rsqrt_newton(ssx, nty, ntt, Tssx, Tnty, Tntt, 3)
        cp("dve", rstd_all[:, 4 * g:4 * g + 4], nty, [Tnty], [Trstd[g]])

    def P1b_scale(g, j):
        t = 4 * g + j
        hs = j % 2
        prog.op("act", lambda e, o=hn[hs], i=xt[j], sc=rstd_all[:, t:t + 1]: e.activation(out=o, in_=i, func=AF.Copy, scale=sc),
                reads=[Txt[j], Trstd[g]], writes=[Thn[hs]])
        if t + 4 < NT:
            load_x(t + 4)

    def P1b_tr(g, j):
        hs = j % 2
        hnT, ThnT = hnTb[g % 2], ThnTb[g % 2]
        b = next_bank()
        for k in range(8):
            prog.op("pe", lambda e, o=psum16[:, b * 1024 + k * 128:b * 1024 + (k + 1) * 128],
                    i=hn[hs][:, k * 128:(k + 1) * 128]: e.transpose(out=o, in_=i, identity=ident),
                    reads=[Thn[hs], Tident], writes=[PB[b]])
        cp("act", hnT[:, :, j * 128:(j + 1) * 128],
           psum16[:, b * 1024:(b + 1) * 1024].rearrange("p (k c) -> p k c", c=128),
           [PB[b]], [ThnT[j]])

    def P1b_steps(g):
        P1b_scale(g, 0)
        yield
        for j in range(4):
            if j + 1 < 4:
                P1b_scale(g, j + 1)
            P1b_tr(g, j)
            yield

    def drain(gen):
        if gen is not None:
            for _ in gen:
                pass

    def step(gen):
        if gen is not None:
            next(gen, None)

    def P2(g):
        cur_g[0] = g
        cT, TcT = cTb[g % 2], TcTb[g % 2]
        for c in range(5):
            b = next_bank()
            zchunk(c * 128, 128, b)
            act(sq[:, c, :], bank(b), AF.Square, [PB[b]], [Tsq[c]])
            cp("act", zs[:, c, :], bank(b), [PB[b]], [Tzs[c]])
        bq = next_bank()
        for c in range(3):
            mm(bank(bq), ones_bf, sq[:, c, :], c == 0, c == 2, [Tones, Tsq[c]], [PB[bq]])
        bk = next_bank()
        for c in range(2):
            mm(bank(bk), ones_bf, sq[:, 3 + c, :], c == 0, c == 1, [Tones, Tsq[3 + c]], [PB[bk]])
        ts("dve", ssn[:, 0:G], bank(bq), float(QL * EPS), ALU.add, [PB[bq]], [Tssn])
        ts("dve", ssn[:, G:2 * G], bank(bk), float(KVL * EPS), ALU.add, [PB[bk]], [Tssn])

    def P2b(g):
        cT, TcT = cTb[g % 2], TcTb[g % 2]
        rsqrt_newton(ssn, nry, nrt, Tssn, Tnry, Tnrt, 2)
        for c in range(3):
            stt(cT[:, c, :], zs[:, c, :], qn[:, c:c + 1], nry[:, 0:G], ALU.mult, ALU.mult,
                [Tzs[c], Tvec, Tnry], [TcT[c]])
        for c in range(2):
            stt(cT[:, 3 + c, :], zs[:, 3 + c, :], kvn[:, c:c + 1], nry[:, G:2 * G], ALU.mult, ALU.mult,
                [Tzs[3 + c], Tvec, Tnry], [TcT[3 + c]])

    def P3(g):
        cT, TcT = cTb[g % 2], TcTb[g % 2]
        cslot = g % 2
        for h in range(H):
            ba = next_bank()
            for k in range(3):
                mm(bank(ba, G, 96), w_q[:, k, h, :], cT[:, k, :], k == 0, k == 2, [TW1, TcT[k]], [PB[ba]])
            bb = next_bank()
            for k in range(3):
                mm(bank(bb, G, 96), w_qs[:, k, h, :], cT[:, k, :], k == 0, k == 2, [TW1, TcT[k]], [PB[bb]])
            cp("act", qst[0:64, h, :], psum[0:64, ba * 512:ba * 512 + G], [PB[ba]], [Tqst[h]])
            rs = h % 2
            tt("dve", r1[rs][64:96, :], psum[64:96, ba * 512:ba * 512 + G], cs[cslot][64:96, :], ALU.mult, [PB[ba], Tcs[cslot]], [Tr1[rs]])
            tt("dve", r2[rs][64:96, :], psum[64:96, bb * 512:bb * 512 + G], sn[cslot][64:96, :], ALU.mult, [PB[bb], Tcs[cslot]], [Tr2[rs]])
            tt("pool", qst[64:96, h, :], r1[rs][64:96, :], r2[rs][64:96, :], ALU.add, [Tr1[rs], Tr2[rs]], [Tqst[h]])
        dma("sp", qt_d[:, :, g * G:(g + 1) * G].rearrange("h r t -> r h t"), qst[0:96, :, :], "qst",
            reads=Tqst, writes=[Tqt_d[h][g] for h in range(H)])
        for h in range(H):
            b = next_bank()
            for k in range(2):
                mm(bank(b, G, 64), w_kv[:, k, h * 128:h * 128 + 64], cT[:, 3 + k, :], k == 0, k == 1,
                   [TW1, TcT[3 + k]], [PB[b]])
            cp("act", kst[0:64, h, :], psum[0:64, b * 512:b * 512 + G], [PB[b]], [Tkst[h]])
        dma("sp", kt_d[:, :, g * G:(g + 1) * G].rearrange("h r t -> r h t"), kst[0:96, :, :], "kst",
            reads=Tkst, writes=[Tkt_d[h][g] for h in range(H)])
        wv = w_kv.rearrange("p k (h c) -> p k h c", c=128)
        for j in range(4):
            b = next_bank()
            for k in range(2):
                mm(bank(b).rearrange("p (h c) -> p h c", c=64), cT[:, 3 + k, j * 128:(j + 1) * 128],
                   wv[:, k, :, 64:128], k == 0, k == 1, [TW1, TcT[3 + k]], [PB[b]])
            cp("act", vst[:, j, :], bank(b), [PB[b]], [Tvst[j]])
        dma("sp", v_d[g * G:(g + 1) * G, :].rearrange("(j p) c -> p j c", p=128), vst, "vst",
            reads=Tvst, writes=[Tv_d[g]])

    def P4a(g, nxt=None):
        cur_g[0] = g
        cslot = g % 2
        b1 = next_bank()
        zchunk(OFF_ZKR - 64, 96, b1)
        b2 = next_bank()
        zchunk(0, 96, b2, lhs_src=w_krs)
        tt("dve", r1[1][64:96, :], psum[64:96, b1 * 512:b1 * 512 + G], cs[cslot][64:96, :], ALU.mult, [PB[b1], Tcs[cslot]], [Tr1[1]])
        tt("dve", r2[1][64:96, :], psum[64:96, b2 * 512:b2 * 512 + G], sn[cslot][64:96, :], ALU.mult, [PB[b2], Tcs[cslot]], [Tr2[1]])
        tt("pool", kst[64:96, :, :], r1[1][64:96, :].unsqueeze(1).broadcast_to([32, H, G]),
           r2[1][64:96, :].unsqueeze(1).broadcast_to([32, H, G]), ALU.add, [Tr1[1], Tr2[1]], Tkst)
        step(nxt)
        for c in range(4):
            b = next_bank()
            zchunk(OFF_GA + c * 128, 128, b)
            act(th[:, c, :], bank(b), AF.Tanh, [PB[b]], [Tth[c]], scale=0.5)
            stt(sgst[:, c, :], th[:, c, :], 1.0, bank(b), ALU.add, ALU.mult, [Tth[c], PB[b]], [Tsgst[c]])
            if c % 2 == 1:
                step(nxt)
        dma("sp", sg_d[:, g * G:(g + 1) * G].rearrange("(c p) t -> p c t", p=128), sgst, "sgst",
            reads=Tsgst, writes=[Tsg_d[g]])

    def P4b(g, nxt=None):
        cur_g[0] = g
        for c in range(4):
            cp("pool", ub[:, c, 0:16], ubs[:, c, :], [Tubs[c]], [Tubh[c]])
            b = next_bank()
            zchunk(OFF_UP + c * 128, 128, b)
            cp("act", ub[:, c, 16:16 + G], bank(b), [PB[b]], [Tub[c]])
            act(wu[:, c, :], bank(b), AF.Copy, [PB[b]], [Twu[c]], scale=float(POOL_W[c]))
            if c % 2 == 1:
                step(nxt)
        for c in range(4):
            b = next_bank()
            zchunk(OFF_GP + c * 128, 128, b)
            act(th[:, 4 + c, :], bank(b), AF.Tanh, [PB[b]], [Tth[4 + c]], scale=0.5)
            stt(sgp[:, c, :], th[:, 4 + c, :], 1.0, bank(b), ALU.add, ALU.mult, [Tth[4 + c], PB[b]], [Tsgp[c]])

    def P4p(g):
        W = 16 + G
        for c in range(4):
            src, Tsrc = ub[:, c, :], Tub[c]
            cur, Tcur = src, Tsrc
            lo = 0
            dsts = [(pa, Tpa), (pbuf, Tpb)]
            for lvl in range(c + 1):
                sh = 1 << lvl
                dst, Tdst = dsts[lvl % 2]
                nlo = lo + sh
                tt("pool", dst[:, nlo:W], cur[:, nlo:W], cur[:, nlo - sh:W - sh], ALU.add,
                   [Tcur] + ([Tubh[c]] if lvl == 0 else []), [Tdst])
                cur, Tcur, lo = dst, Tdst, nlo
            tt("pool", dT[:, c, :], cur[:, 16:W], wu[:, c, :], ALU.subtract, [Tcur, Twu[c]], [TdT[c]])
            if g == 0:
                tt("pool", pfx, cur[:, 16:32], pcnt[:, c, :], ALU.mult, [Tcur, Tpcnt], [Tpfx])
                tt("pool", dT[:, c, 0:16], pfx, wu[:, c, 0:16], ALU.subtract, [Tpfx, Twu[c]], [TdT[c]])
            cp("pool", ubs[:, c, :], src[:, G:G + 16], [Tsrc], [Tubs[c]])

    def P4c(g):
        for c in range(4):
            b = next_bank()
            mm(bank(b), w_pl[:, c, :], dT[:, c, :], True, True, [TW1, TdT[c]], [PB[b]])
            stt(ypst[:, c, :], bank(b), psc[:, c:c + 1], sgp[:, c, :], ALU.mult, ALU.mult,
                [PB[b], Tvec, Tsgp[c]], [Typst[c]])
        dma("sp", yp_d[:, g * G:(g + 1) * G].rearrange("(c p) t -> p c t", p=128), ypst, "ypst",
            reads=Typst, writes=[Typ_d[g]])

    for t in range(min(4, NT)):
        load_x(t)
    for g in range(NG):
        cslot = g % 2
        dma("sp", cs[cslot][64:96, :], cos_d[:, g * G:(g + 1) * G], "cs%d" % cslot, writes=[Tcs[cslot]])
        dma("sp", sn[cslot][64:96, :], sin_d[:, g * G:(g + 1) * G], "cs%d" % cslot, writes=[Tcs[cslot]])
        if g == 0:
            P1a(0)
            drain(P1b_steps(0))
        P2(g)
        if g == 0:
            fold_wB()
        if g > 0:
            P3(g - 1)
            P4c(g - 1)
        nxt = None
        if g + 1 < NG:
            P1a(g + 1)
            nxt = P1b_steps(g + 1)
        P4a(g, nxt)
        P2b(g)
        P4b(g, nxt)
        P4p(g)
        drain(nxt)
    P3(NG - 1)
    P4c(NG - 1)

    if stop >= 2:
        prog.barrier()
        wmem.reset()
        w_m = wmem.alloc(BF16, [8, 2048])
        w_ba = wmem.alloc(BF16, [4, D])
        w_bp = wmem.alloc(BF16, [4, D])
        w_o = wmem.alloc(BF16, [8, D])
        TW3 = T("w3")
        work.reset()
        KTb = [work.alloc(BF16, [S]) for _ in range(2)]
        TKT = [T("KT%d" % i) for i in range(2)]
        VAb = [work.alloc(BF16, [NT, 65]) for _ in range(2)]
        TVA = [T("VA%d" % i) for i in range(2)]
        QTb = [work.alloc(BF16, [G]) for _ in range(3)]
        TQT = [T("QT%d" % i) for i in range(3)]
        NPT = 4
        PT = [work.alloc(BF16, [2 * G]) for _ in range(NPT)]
        TPT = [T("PT%d" % i) for i in range(NPT)]
        ost = [work.alloc(F32, [G]) for _ in range(2)]
        Tost = [T("ost%d" % i) for i in range(2)]
        for i in range(2):
            memset("pool", VAb[i][:, :, 64:65], 1.0, [TVA[i]])

        import os as _os4
        NDUMMY = int(_os4.environ.get("KDUMMY", "0"))
        SU = [(0, 1), (2, 3), (4, 5)]
        FILL_N = int(_os4.environ.get("KFILLN", "128"))
        su_rr = [0]
        pt_rr = [0]

        def load_head(h, s):
            dma("pool", KTb[s][0:96, :], kt_d[h, :, :], "KT%d" % s,
                reads=[Tkt_d[h][g] for g in range(NG)], writes=[TKT[s]])
            dma("pool", VAb[s][:, :, 0:64], v_d[:, h * 64:(h + 1) * 64].rearrange("(t p) c -> p t c", p=128),
                "VA%d" % s, reads=Tv_d, writes=[TVA[s]])

        def load_q(h, g, s):
            dma("sp", QTb[s][0:96, :], qt_d[h, :, g * G:(g + 1) * G], "QT%d" % s,
                reads=[Tqt_d[h][g]], writes=[TQT[s]])

        hq = [(h, g) for h in range(H) for g in range(NG)]
        allu = []
        for idx, (h, g) in enumerate(hq):
            units = []
            for p in range(2 * g):
                units.append(([2 * p, 2 * p + 1], [(0, 0), (G, 0)], False))
            units.append(([4 * g, 4 * g + 1], [(0, 0), (G, 128)], True))
            units.append(([4 * g + 2, 4 * g + 3], [(256, 256), (G, 384)], True))
            for ui, (kts, q0, pvq) in enumerate(units):
                allu.append((idx, kts, q0, ui == 0, ui == len(units) - 1, pvq))
        def emit_scores(i):
            idx, kts, q0, first, last, pvq = allu[i]
            h, g = hq[idx]
            hs, qs = h % 2, idx % 3
            if first:
                if idx >= 2:
                    w3_issue(1 if NG * H >= 40 else 3)
                if idx >= 1 and idx + 2 < len(hq):
                    load_q(hq[idx + 2][0], hq[idx + 2][1], (idx + 2) % 3)
            sb = SU[su_rr[0]]
            su_rr[0] = (su_rr[0] + 1) % len(SU)
            for k_, kt in enumerate(kts):
                b = sb[k_]
                rc, qk = q0[k_]
                mm(psum[:, sb[0] * 512 + rc:sb[0] * 512 + rc + G - qk], KTb[hs][0:96, kt * 128:(kt + 1) * 128],
                   QTb[qs][0:96, qk:G], True, True, [TKT[hs], TQT[qs]], [PB[b]])
            return sb

        def emit_exp_pv(i, sb):
            idx, kts, q0, first, last, pvq = allu[i]
            h, g = hq[idx]
            hs, qs = h % 2, idx % 3
            ob_ = 6 + idx % 2
            if first and g == 0 and h + 1 < H:
                load_head(h + 1, 1 - hs)
            pi = pt_rr[0]
            pt_rr[0] = (pi + 1) % NPT
            lo = q0[0][0]
            hi = q0[-1][0] + G - q0[-1][1]
            act(PT[pi][:, lo:hi], psum[:, sb[0] * 512 + lo:sb[0] * 512 + hi], AF.Exp,
                [PB[sb[0]], PB[sb[1]]], [TPT[pi]], scale=SM_SCALE)
            if pvq:
                for k_ in range(2):
                    memset("dve", PT[pi][64:128, q0[k_][0]:q0[k_][0] + 64], 0.0, [TPT[pi]])
            for k_, kt in enumerate(kts):
                rc, qk = q0[k_]
                mm(psum[0:65, ob_ * 512 + qk:ob_ * 512 + G], VAb[hs][:, kt, :], PT[pi][:, rc:rc + G - qk],
                   first and k_ == 0, last and k_ == len(kts) - 1, [TVA[hs], TPT[pi]], [PB[ob_]])
            for _ in range(NDUMMY):
                mm(bank(7, FILL_N), KTb[hs][0:96, 0:128], QTb[qs][0:96, 0:FILL_N], True, True, [TKT[hs], TQT[qs]], [PB[7]])
            if last:
                os_ = idx % 2
                cp("dve", ost[os_][0:65, :], psum[0:65, ob_ * 512:ob_ * 512 + G], [PB[ob_]], [Tost[os_]])
                dma("sp", o_d[h * 64:(h + 1) * 64, g * G:(g + 1) * G], ost[os_][0:64, :], "ost%d" % os_,
                    reads=[Tost[os_]], writes=[To_d[h][g]])
                dma("sp", l_d[h:h + 1, g * G:(g + 1) * G], ost[os_][64:65, :], "ost%d" % os_,
                    reads=[Tost[os_]], writes=[Tl_d[h][g]])
                if fold_todo and not w3_jobs and idx >= FOLD0:
                    k = fold_todo.pop(0)
                    ts("dve", w_m[:, k, :], w_m[:, k, :], gcol[:, k:k + 1], ALU.mult, [TW3, Tgin], [TW3])

        load_head(0, 0)
        load_q(0, 0, 0)
        for i_ in range(1, min(3, len(hq))):
            load_q(hq[i_][0], hq[i_][1], i_)
        w3_jobs = []
        for k in range(8):
            w3_jobs.append((w_m[:, k, :], w_in_v[:, k, OFF_GM:OFF_GM + 2048]))
        w3_jobs.append((w_ba, w_ba_d.rearrange("(k p) c -> p k c", p=128)))
        w3_jobs.append((w_bp, w_bp_d.rearrange("(k p) c -> p k c", p=128)))
        for k in range(8):
            w3_jobs.append((w_o[:, k, :], w_out_d.rearrange("(k p) c -> p k c", p=128)[:, k, :]))

        def w3_issue(n):
            for _ in range(n):
                if w3_jobs:
                    o_, i_ = w3_jobs.pop(0)
                    dma("pool", o_, i_, "w3", writes=[TW3], group=True)

        LA = 2
        fold_todo = list(range(8))
        FOLD0 = (len(hq) * 5) // 8
        pendq = [emit_scores(i) for i in range(min(LA, len(allu)))]
        for i in range(len(allu)):
            if i + LA < len(allu):
                pendq.append(emit_scores(i + LA))
            emit_exp_pv(i, pendq.pop(0))
        w3_issue(len(w3_jobs))
        for k in fold_todo:
            ts("dve", w_m[:, k, :], w_m[:, k, :], gcol[:, k:k + 1], ALU.mult, [TW3, Tgin], [TW3])

    if stop >= 3:
        prog.barrier()
        work.reset()
        NXS3 = 5
        xt = [work.alloc(F32, [D]) for _ in range(NXS3)]
        Txt = [T("x3t%d" % i) for i in range(NXS3)]
        hn = [work.alloc(BF16, [D]) for _ in range(2)]
        Thn = [T("h3n%d" % i) for i in range(2)]
        hnT = work.alloc(BF16, [8, G]); ThnT = [T("h3nT%d" % i) for i in range(4)]
        tg = work.alloc(BF16, [16, G]); Ttg = [T("tg%d" % i) for i in range(16)]
        oin = [work.alloc(F32, [4, G]) for _ in range(2)]; Toin = [T("oin%d" % i) for i in range(2)]
        lin = work.alloc(F32, [4, G]); Tlin = [T("lin%d" % i) for i in range(4)]
        sgin = [work.alloc(BF16, [4, G]) for _ in range(2)]; Tsgin = [T("sgin%d" % i) for i in range(2)]
        ypin = [work.alloc(BF16, [4, G]) for _ in range(2)]; Typin = [T("ypin%d" % i) for i in range(2)]
        ya = work.alloc(BF16, [4, G]); Tya = [T("ya%d" % i) for i in range(4)]
        m1 = [work.alloc(F32, [G]) for _ in range(2)]; Tm1 = [T("m1_%d" % i) for i in range(2)]
        m2 = [work.alloc(F32, [G]) for _ in range(2)]; Tm2 = [T("m2_%d" % i) for i in range(2)]
        mg = work.alloc(BF16, [8, G]); Tmg = [T("mg%d" % i) for i in range(8)]
        hh = [work.alloc(F32, [D]) for _ in range(4)]; Thh = [T("hh%d" % i) for i in range(4)]
        junk3 = [work.alloc(BF16, [D])] * 4; Tjunk3 = [T("j3")] * 4
        ss3 = work.alloc(F32, [4]); Tss3 = T("ss3")
        ny3 = work.alloc(F32, [4]); Tny3 = T("ny3")
        nt3 = work.alloc(F32, [4]); Tnt3 = T("nt3")

        import os as _os2
        if _os2.environ.get("KDBG"):
            print("DBG phase3 work used", work.off - work.lo, "of", work.hi - work.lo)

        def load_g3(g, s):
            dma("sp", oin[s], o_d[:, g * G:(g + 1) * G].rearrange("(c p) t -> p c t", p=128), "oin%d" % s,
                reads=[To_d[h][g] for h in range(H)], writes=[Toin[s]])
            dma("sp", sgin[s], sg_d[:, g * G:(g + 1) * G].rearrange("(c p) t -> p c t", p=128), "sgin%d" % s,
                reads=[Tsg_d[g]], writes=[Tsgin[s]])
            dma("sp", ypin[s], yp_d[:, g * G:(g + 1) * G].rearrange("(c p) t -> p c t", p=128), "ypin%d" % s,
                reads=[Typ_d[g]], writes=[Typin[s]])

        def load_x3(t):
            xs = t % NXS3
            dma("sp", xt[xs], x_d[t * 128:(t + 1) * 128, :], "x3_%d" % xs, writes=[Txt[xs]])

        def load_lin(g):
            for hh_ in range(H):
                dma("sp", lin[(hh_ % 2) * 64:(hh_ % 2) * 64 + 64, hh_ // 2, :],
                    l_d[hh_, g * G:(g + 1) * G].partition_broadcast(64), "lin%d" % (hh_ // 2),
                    reads=[Tl_d[hh_][g]], writes=[Tlin[hh_ // 2]])

        def stageA3(g, j):
            t = 4 * g + j
            xs = t % NXS3
            ts("dve", hn[j % 2], xt[xs], rstd_all[:, t:t + 1], ALU.mult, [Txt[xs], Trstd[g]], [Thn[j % 2]])
            b = next_bank()
            for k in range(8):
                prog.op("pe", lambda e, o=psum16[:, b * 1024 + k * 128:b * 1024 + (k + 1) * 128],
                        i=hn[j % 2][:, k * 128:(k + 1) * 128]: e.transpose(out=o, in_=i, identity=ident),
                        reads=[Thn[j % 2], Tident], writes=[PB[b]])
            cp("act", hnT[:, :, j * 128:(j + 1) * 128],
               psum16[:, b * 1024:(b + 1) * 1024].rearrange("p (k c) -> p k c", c=128),
               [PB[b]], [ThnT[j]])

        load_g3(0, 0)
        for t in range(min(NXS3, NT)):
            load_x3(t)
        for g in range(NG):
            s = g % 2
            if g + 1 < NG:
                load_g3(g + 1, 1 - s)
            if g == 0:
                for j in range(4):
                    stageA3(0, j)
            for c in range(16):
                b = next_bank()
                for k in range(8):
                    mm(bank(b), w_m[:, k, c * 128:(c + 1) * 128], hnT[:, k, :], k == 0, k == 7, [TW3] + ThnT, [PB[b]])
                act(tg[:, c, :], bank(b), AF.Tanh, [PB[b]], [Ttg[c]], scale=0.5)
            if g == 0:
                load_lin(0)
            for c in range(4):
                tt("pool", oin[s][:, c, :], oin[s][:, c, :], sgin[s][:, c, :], ALU.mult, [Toin[s], Tsgin[s]], [Toin[s]])
                prog.op("dve", lambda e, o=lin[:, c, :]: e.reciprocal(out=o, in_=o), reads=[Tlin[c]], writes=[Tlin[c]])
                stt(ya[:, c, :], oin[s][:, c, :], 0.5, lin[:, c, :], ALU.mult, ALU.mult, [Toin[s], Tlin[c]], [Tya[c]])
            if g + 1 < NG:
                load_lin(g + 1)
            for dc in range(8):
                ba = next_bank()
                for k in range(4):
                    mm(bank(ba), w_ba[:, k, dc * 128:(dc + 1) * 128], ya[:, k, :], k == 0, k == 3, [TW3, Tya[k]], [PB[ba]])
                bp = next_bank()
                for k in range(4):
                    mm(bank(bp), w_bp[:, k, dc * 128:(dc + 1) * 128], ypin[s][:, k, :], k == 0, k == 3, [TW3, Typin[s]], [PB[bp]])
                ms = dc % 2
                stt(m1[ms], tg[:, dc, :], 1.0, bank(ba), ALU.add, ALU.mult, [Ttg[dc], PB[ba]], [Tm1[ms]])
                stt(m2[ms], tg[:, 8 + dc, :], 1.0, bank(bp), ALU.add, ALU.mult, [Ttg[8 + dc], PB[bp]], [Tm2[ms]])
                tt("pool", mg[:, dc, :], m1[ms], m2[ms], ALU.add, [Tm1[ms], Tm2[ms]], [Tmg[dc]])
            for j in range(4):
                t = 4 * g + j
                xs = t % NXS3
                hs = j
                b0 = next_bank()
                b1 = next_bank()
                for half, b in ((0, b0), (1, b1)):
                    for k in range(8):
                        mm(bank(b), mg[:, k, j * 128:(j + 1) * 128], w_o[:, k, half * 512:(half + 1) * 512],
                           k == 0, k == 7, [TW3, Tmg[k]], [PB[b]])
                if g + 1 < NG and j >= 1:
                    stageA3(g + 1, j - 1)
                for half, b in ((0, b0), (1, b1)):
                    stt(hh[hs][:, half * 512:(half + 1) * 512], bank(b), 0.5, xt[xs][:, half * 512:(half + 1) * 512],
                        ALU.mult, ALU.add, [PB[b], Txt[xs]], [Thh[hs]])
                if t + NXS3 < NT:
                    load_x3(t + NXS3)
                act(junk3[hs], hh[hs], AF.Square, [Thh[hs]], [Tjunk3[hs], Tss3], accum=ss3[:, j:j + 1])
                tt("pool", hh[hs], hh[hs], gfin, ALU.mult, [Thh[hs], Tgfin], [Thh[hs]])
            if g + 1 < NG:
                stageA3(g + 1, 3)
            ts("dve", ss3, ss3, float(D * EPS), ALU.add, [Tss3], [Tss3])
            rsqrt_newton(ss3, ny3, nt3, Tss3, Tny3, Tnt3, 3)
            for j in range(4):
                t = 4 * g + j
                prog.op("act", lambda e, o=hh[j], sc=ny3[:, j:j + 1]: e.activation(out=o, in_=o, func=AF.Copy, scale=sc),
                        reads=[Thh[j], Tny3], writes=[Thh[j]])
                dma("sp", out_d[t * 128:(t + 1) * 128, :], hh[j], "ob%d" % j, reads=[Thh[j]], writes=[T("outd")])

    import os as _os
    _cut = int(_os.environ.get("KCUT", "0"))
    if _cut:
        prog.ins = prog.ins[:_cut]
    prog.ins.append(dict(eng="sp", fn=None, deps=set(range(len(prog.ins))), dma=None, signal=False))

    totals = prog.resolve()
    with ExitStack() as es:
        sems = {}
        for k in totals:
            sems[k] = es.enter_context(nc.semaphore("s_%s_%s" % (k[0], k[1])))
        es.enter_context(nc.allow_low_precision("bf16 matmul operands, fp32 accumulation (problem tolerance calibrated for bf16)"))
        block = es.enter_context(nc.Block())

        def run(engname):
            def f(e):
                for it in prog.ins:
                    if it["eng"] != engname:
                        continue
                    for k, v in it["waits"]:
                        e.wait_ge(sems[k], v)
                    if it["fn"] is None:
                        continue
                    ins = it["fn"](e)
                    if it["signal"]:
                        k, v = it["sig"]
                        ins.then_inc(sems[k], 16 if k[0] == "dma" else 1)
            return f

        block.sync(run("sp"))
        block.tensor(run("pe"))
        block.scalar(run("act"))
        block.vector(run("dve"))
        block.gpsimd(run("pool"))
    return nc


def host_consts(S):
    half = ROPE // 2
    inv_freq = (np.float32(10000.0) ** (-np.arange(half, dtype=np.float32) / np.float32(half))).astype(np.float32)
    ang = (np.arange(S, dtype=np.float32)[:, None] * inv_freq[None, :]).astype(np.float32)
    cos = np.cos(ang.astype(np.float64)).astype(np.float32).T
    sin = np.sin(ang.astype(np.float64)).astype(np.float32).T
    cosT = np.concatenate([cos, cos], axis=0)
    sinT = np.concatenate([-sin, sin], axis=0)
    pcnt = np.zeros((4, 16), np.float32)
    for gi, w in enumerate(POOL_W):
        pcnt[gi] = float(w) / np.minimum(np.arange(16) + 1, w)
    return {
        "c_ident": np.eye(128, dtype=np.float32),
        "c_cos": np.ascontiguousarray(cosT),
        "c_sin": np.ascontiguousarray(sinT),
        "c_pcnt": pcnt,
    }


_NC_CACHE = {}


def kernel(x, norm_in, w_in, q_norm, w_uq, kv_norm, w_ukv, pool_w, pool_scale,
           w_branch_attn, w_branch_pool, w_out, norm_final):
    x = np.asarray(x, dtype=np.float32)
    B, S, _ = x.shape
    if S not in _NC_CACHE:
        _NC_CACHE[S] = build_nc(S)
    nc = _NC_CACHE[S]
    consts = host_consts(S)
    shared = {
        "norm_in": np.asarray(norm_in, np.float32), "w_in": np.asarray(w_in, np.float32),
        "q_norm": np.asarray(q_norm, np.float32), "w_uq": np.asarray(w_uq, np.float32),
        "kv_norm": np.asarray(kv_norm, np.float32), "w_ukv": np.asarray(w_ukv, np.float32),
        "pool_w": np.asarray(pool_w, np.float32), "pool_scale": np.asarray(pool_scale, np.float32),
        "w_branch_attn": np.asarray(w_branch_attn, np.float32),
        "w_branch_pool": np.asarray(w_branch_pool, np.float32),
        "w_out": np.asarray(w_out, np.float32), "norm_final": np.asarray(norm_final, np.float32),
    }
    shared.update(consts)
    in_maps = []
    for b in range(B):
        m = dict(shared)
        m["x"] = np.ascontiguousarray(x[b])
        in_maps.append(m)
    res = run_bass_kernel_spmd(nc, in_maps, core_ids=list(range(B)))
    return np.stack([np.asarray(r["out"], dtype=np.float32) for r in res.results], axis=0)
```

```python
import numpy as np
import concourse.bass as bass
import concourse.mybir as mybir
from concourse.bass_utils import run_bass_kernel_spmd
from contextlib import ExitStack

F32 = mybir.dt.float32
BF16 = mybir.dt.bfloat16
I32 = mybir.dt.int32
ALU = mybir.AluOpType
AF = mybir.ActivationFunctionType

D = 1024
H = 8
NOPE = 64
ROPE = 32
QKD = 96
VD = 64
QL = 384
KVL = 256
NFRONT = 2208
OFF_ZQ, OFF_ZKV, OFF_ZKR, OFF_GA, OFF_UP, OFF_GP, OFF_GM = 0, 384, 640, 672, 1184, 1696, 2208
IN_TOTAL = 4256
G = 512
EPS = 1e-6
POOL_W = (2, 4, 8, 16)
MAGIC = 0x5F3759DF
SM_SCALE = float(QKD ** -0.5)
FULL_S = 4096
NCORES = 8


class T:
    __slots__ = ("name", "w", "r", "excl")

    def __init__(self, name, excl=False):
        self.name = name
        self.w = None
        self.r = {}
        self.excl = excl


class Prog:
    COMPUTE = ("pe", "act", "dve", "pool")

    def __init__(self):
        self.ins = []
        self.bar_gen = 0
        self.bar_deps = set()
        self.eng_gen = {}
        self.last = {}
        self.dmas_since_bar = []
        self.groups = {}

    def op(self, eng, fn, reads=(), writes=(), dma=None, group=False):
        i = len(self.ins)
        deps = set()
        for t in reads:
            if t.w is not None:
                deps.add(t.w)
            if t.excl:
                for k, rid in t.r.items():
                    if k != eng:
                        deps.add(rid)
        for t in writes:
            if t.w is not None:
                deps.add(t.w)
            for rid in t.r.values():
                deps.add(rid)
        if self.eng_gen.get(eng, 0) < self.bar_gen:
            deps |= self.bar_deps
            self.eng_gen[eng] = self.bar_gen
        deps.discard(i)
        if dma is not None:
            deps = {d for d in deps if self.ins[d]["dma"] != dma}
        self.ins.append(dict(eng=eng, fn=fn, deps=deps, dma=dma, signal=dma is not None))
        if dma is not None:
            self.groups[dma] = group
            self.dmas_since_bar.append(i)
        else:
            self.last[eng] = i
        rkey = eng if dma is None else ("dma", i)
        for t in reads:
            t.r[rkey] = i
        for t in writes:
            t.w = i
            t.r = {}
        return i

    def barrier(self):
        self.bar_gen += 1
        self.bar_deps = set(self.last.values()) | set(self.dmas_since_bar)
        self.dmas_since_bar = []

    def resolve(self):
        ins = self.ins
        for it in ins:
            for d in it["deps"]:
                dd = ins[d]
                if dd["dma"] is None and dd["eng"] == it["eng"] == "pe" and it["dma"] is None:
                    continue
                dd["signal"] = True
        cnt = {}
        for it in ins:
            if it["dma"] is not None:
                k = ("dma", it["dma"])
                cnt[k] = cnt.get(k, 0) + 16
                it["sig"] = (k, cnt[k])
            elif it["signal"]:
                k = ("eng", it["eng"])
                cnt[k] = cnt.get(k, 0) + 1
                it["sig"] = (k, cnt[k])
        totals = dict(cnt)
        waited = {}
        for it in ins:
            need = {}
            for d in it["deps"]:
                dd = ins[d]
                if dd["dma"] is None and dd["eng"] == it["eng"] == "pe" and it["dma"] is None:
                    continue
                k, v = dd["sig"]
                if k[0] == "dma" and self.groups[k[1]]:
                    v = totals[k]
                if v > need.get(k, 0):
                    need[k] = v
            w = waited.setdefault(it["eng"], {})
            out = []
            for k, v in need.items():
                if w.get(k, 0) < v:
                    w[k] = v
                    out.append((k, v))
            it["waits"] = out
        return totals


def build_nc(S, stop=3):
    NG = S // G
    NT = S // 128
    nc = bass.Bass("TRN2", target_bir_lowering=False)

    def din(name, shape, dt=F32):
        return nc.dram_tensor(name, list(shape), dt, kind="ExternalInput").ap()

    def dscr(name, shape, dt=BF16):
        return nc.dram_tensor(name, list(shape), dt, kind="Internal").ap()

    x_d = din("x", [S, D])
    norm_in_d = din("norm_in", [D])
    w_in_d = din("w_in", [D, IN_TOTAL])
    q_norm_d = din("q_norm", [QL])
    w_uq_d = din("w_uq", [QL, H, QKD])
    kv_norm_d = din("kv_norm", [KVL])
    w_ukv_d = din("w_ukv", [KVL, H, 128])
    pool_w_d = din("pool_w", [4, 128, 128])
    pool_scale_d = din("pool_scale", [512])
    w_ba_d = din("w_branch_attn", [512, D])
    w_bp_d = din("w_branch_pool", [512, D])
    w_out_d = din("w_out", [D, D])
    norm_final_d = din("norm_final", [D])
    ident_d = din("c_ident", [128, 128])
    cos_d = din("c_cos", [32, S])
    sin_d = din("c_sin", [32, S])
    pcnt_d = din("c_pcnt", [4, 16])
    out_d = nc.dram_tensor("out", [S, D], F32, kind="ExternalOutput").ap()

    qt_d = dscr("s_qt", [H, QKD, S])
    kt_d = dscr("s_kt", [H, QKD, S])
    v_d = dscr("s_v", [S, 512])
    sg_d = dscr("s_sg", [512, S])
    yp_d = dscr("s_yp", [512, S])
    o_d = dscr("s_o", [512, S], F32)
    l_d = dscr("s_l", [H, S], F32)

    ARENA = 207 * 1024
    arena = nc.alloc_sbuf_tensor("arena", [128, ARENA // 2], BF16)
    arena32 = arena.bitcast(F32)
    psum = nc.alloc_psum_tensor("psum", [128, 4096], F32)
    psum16 = psum.bitcast(BF16)

    class Bump:
        def __init__(self, lo, hi):
            self.lo, self.hi, self.off = lo, hi, lo

        def reset(self):
            self.off = self.lo

        def alloc(self, dt, shape):
            sz = 4 if dt in (F32, I32) else 2
            n = int(np.prod(shape))
            nb = (n * sz + 31) // 32 * 32
            assert self.off + nb <= self.hi, ("arena overflow", self.off, nb, self.hi)
            o = self.off
            self.off += nb
            if sz == 4:
                ap = arena32[:, o // 4:o // 4 + n]
            else:
                ap = arena[:, o // 2:o // 2 + n]
            if len(shape) == 2:
                ap = ap.rearrange("p (a b) -> p a b", b=shape[1])
            elif len(shape) == 3:
                ap = ap.rearrange("p (a b c) -> p a b c", b=shape[1], c=shape[2])
            return ap

    WREG = 68 * 1024
    wmem = Bump(0, WREG)
    cmem = Bump(WREG, WREG + 10 * 1024)
    work = Bump(WREG + 10 * 1024, ARENA)

    def bank(b, n=512, parts=128, off=0):
        return psum[0:parts, b * 512 + off:b * 512 + off + n]

    PB = [T("psum%d" % b, excl=True) for b in range(8)]
    prog = Prog()
    pb_rr = [0]

    def next_bank():
        b = pb_rr[0]
        pb_rr[0] = (b + 1) % 8
        return b

    def dma(q, out, in_, key, reads=(), writes=(), group=False):
        prog.op(q, lambda e, o=out, i=in_: e.dma_start(out=o, in_=i), reads=reads, writes=writes,
                dma=key, group=group)

    def mm(out, lhsT, rhs, start, stop, reads, writes):
        prog.op("pe", lambda e, o=out, l=lhsT, r=rhs, s=start, p=stop: e.matmul(o, lhsT=l, rhs=r, start=s, stop=p),
                reads=reads, writes=writes)

    def act(out, in_, func, reads, writes, scale=1.0, accum=None):
        if accum is None:
            prog.op("act", lambda e, o=out, i=in_, f=func, s=scale: e.activation(out=o, in_=i, func=f, scale=s),
                    reads=reads, writes=writes)
        else:
            prog.op("act", lambda e, o=out, i=in_, f=func, s=scale, a=accum: e.activation(out=o, in_=i, func=f, scale=s, accum_out=a),
                    reads=reads, writes=writes)

    def tt(eng, out, in0, in1, op, reads, writes):
        prog.op(eng, lambda e, o=out, a=in0, b=in1, p=op: e.tensor_tensor(out=o, in0=a, in1=b, op=p),
                reads=reads, writes=writes)

    def ts(eng, out, in0, s1, op0, reads, writes, s2=None, op1=None):
        if op1 is None:
            prog.op(eng, lambda e, o=out, a=in0, x=s1, p=op0: e.tensor_scalar(out=o, in0=a, scalar1=x, scalar2=None, op0=p),
                    reads=reads, writes=writes)
        else:
            prog.op(eng, lambda e, o=out, a=in0, x=s1, y=s2, p=op0, q=op1: e.tensor_scalar(out=o, in0=a, scalar1=x, scalar2=y, op0=p, op1=q),
                    reads=reads, writes=writes)

    def stt(out, in0, scalar, in1, op0, op1, reads, writes):
        prog.op("dve", lambda e, o=out, a=in0, s=scalar, b=in1, p=op0, q=op1: e.scalar_tensor_tensor(out=o, in0=a, scalar=s, in1=b, op0=p, op1=q),
                reads=reads, writes=writes)

    def cp(eng, out, in_, reads, writes):
        if eng == "act":
            prog.op("act", lambda e, o=out, i=in_: e.copy(out=o, in_=i), reads=reads, writes=writes)
        else:
            prog.op(eng, lambda e, o=out, i=in_: e.tensor_copy(out=o, in_=i), reads=reads, writes=writes)

    def memset(eng, ap, val, writes):
        prog.op(eng, lambda e, a=ap, v=val: e.memset(a, v), writes=writes)

    def rsqrt_newton(v_ap, y_ap, t_ap, Tv, Ty, Tt, iters):
        ts("dve", t_ap.bitcast(I32), v_ap.bitcast(I32), 1, ALU.logical_shift_right, [Tv], [Tt])
        ts("dve", y_ap.bitcast(I32), t_ap.bitcast(I32), -1, ALU.mult, [Tt], [Ty], s2=MAGIC, op1=ALU.add)
        for _ in range(iters):
            stt(t_ap, y_ap, -0.5, y_ap, ALU.mult, ALU.mult, [Ty], [Tt])
            tt("dve", t_ap, t_ap, v_ap, ALU.mult, [Tt, Tv], [Tt])
            stt(y_ap, t_ap, 1.5, y_ap, ALU.add, ALU.mult, [Tt, Ty], [Ty])

    ident = cmem.alloc(BF16, [128])
    Tident = T("ident")
    gcol = cmem.alloc(F32, [8])
    Tgin = T("gcol")
    gfin = cmem.alloc(F32, [D])
    Tgfin = T("gfin")
    qn = cmem.alloc(F32, [3])
    kvn = cmem.alloc(F32, [2])
    psc = cmem.alloc(F32, [4])
    Tvec = T("vecs")
    rstd_all = cmem.alloc(F32, [NT])
    Trstd = [T("rstd%d" % g) for g in range(NG)]
    ones_bf = cmem.alloc(BF16, [128])
    half_f = cmem.alloc(F32, [64])
    Tones = T("ones")
    pcnt = cmem.alloc(F32, [4, 16])
    Tpcnt = T("pcnt")

    dma("pool", ident, ident_d, "wI", writes=[Tident], group=True)
    for k in range(8):
        dma("sp", gcol[:, k:k + 1], norm_in_d[k * 128:(k + 1) * 128].rearrange("(p o) -> p o", o=1), "c0", writes=[Tgin], group=True)
    dma("sp", gfin, norm_final_d.partition_broadcast(128), "c0", writes=[Tgfin], group=True)
    for c in range(3):
        dma("sp", qn[:, c:c + 1], q_norm_d[c * 128:(c + 1) * 128].rearrange("(p o) -> p o", o=1), "c0", writes=[Tvec], group=True)
    for c in range(2):
        dma("sp", kvn[:, c:c + 1], kv_norm_d[c * 128:(c + 1) * 128].rearrange("(p o) -> p o", o=1), "c0", writes=[Tvec], group=True)
    for c in range(4):
        dma("sp", psc[:, c:c + 1], pool_scale_d[c * 128:(c + 1) * 128].rearrange("(p o) -> p o", o=1), "c0", writes=[Tvec], group=True)
    dma("sp", pcnt.rearrange("p a b -> p (a b)"), pcnt_d.rearrange("a b -> (a b)").partition_broadcast(128), "c0",
        writes=[Tpcnt], group=True)
    ts("dve", gcol, gcol, 32.0, ALU.mult, [Tgin], [Tgin])
    ts("dve", gfin, gfin, 32.0, ALU.mult, [Tgfin], [Tgfin])
    ts("dve", qn, qn, float(np.sqrt(QL)), ALU.mult, [Tvec], [Tvec])
    ts("dve", kvn, kvn, 16.0, ALU.mult, [Tvec], [Tvec])
    for gi, w in enumerate(POOL_W):
        ts("dve", psc[:, gi:gi + 1], psc[:, gi:gi + 1], 0.5 / w, ALU.mult, [Tvec], [Tvec])
    memset("dve", ones_bf, 1.0, [Tones])
    memset("dve", half_f, 0.5, [Tones])

    w_f = wmem.alloc(BF16, [8, NFRONT])
    w_krs = wmem.alloc(BF16, [8, 96])
    w_q = wmem.alloc(BF16, [3, H, QKD])
    w_qs = wmem.alloc(BF16, [3, H, QKD])
    w_kv = wmem.alloc(BF16, [2, 1024])
    w_pl = wmem.alloc(BF16, [4, 128])
    TW1 = T("w1")
    TWA = T("wA")
    TWB = T("wB")
    memset("pool", w_krs[:, :, 0:64], 0.0, [TWA])
    memset("pool", w_qs[:, :, :, 0:64], 0.0, [TW1])
    w_in_v = w_in_d.rearrange("(k p) c -> p k c", p=128)
    for k in range(8):
        dma("pool", w_f[:, k, 0:OFF_GA], w_in_v[:, k, 0:OFF_GA], "wA", writes=[TWA], group=True)
    dma("pool", w_krs[:, :, 64:80], w_in_v[:, :, OFF_ZKR + 16:OFF_ZKR + 32], "wA", writes=[TWA], group=True)
    dma("pool", w_krs[:, :, 80:96], w_in_v[:, :, OFF_ZKR:OFF_ZKR + 16], "wA", writes=[TWA], group=True)
    for k in range(8):
        dma("pool", w_f[:, k, OFF_GA:NFRONT], w_in_v[:, k, OFF_GA:NFRONT], "wB", writes=[TWB], group=True)
    w_uq_v = w_uq_d.rearrange("(k p) h c -> p k h c", p=128)
    for k in range(3):
        dma("pool", w_q[:, k, :, :], w_uq_v[:, k, :, :], "w1", writes=[TW1], group=True)
        dma("pool", w_qs[:, k, :, 64:80], w_uq_v[:, k, :, 80:96], "w1", writes=[TW1], group=True)
        dma("pool", w_qs[:, k, :, 80:96], w_uq_v[:, k, :, 64:80], "w1", writes=[TW1], group=True)
    w_ukv_v = w_ukv_d.rearrange("(k p) h c -> p k (h c)", p=128)
    for k in range(2):
        dma("pool", w_kv[:, k, :], w_ukv_v[:, k, :], "w1", writes=[TW1], group=True)
    dma("pool", w_pl, pool_w_d.rearrange("g c d -> c g d"), "w1", writes=[TW1], group=True)
    for k in range(8):
        ts("dve", w_f[:, k, 0:OFF_GA], w_f[:, k, 0:OFF_GA], gcol[:, k:k + 1], ALU.mult, [TWA, Tgin], [TWA])
        ts("dve", w_krs[:, k, 64:96], w_krs[:, k, 64:96], gcol[:, k:k + 1], ALU.mult, [TWA, Tgin], [TWA])

    def fold_wB():
        for k in range(8):
            ts("dve", w_f[:, k, OFF_GA:NFRONT], w_f[:, k, OFF_GA:NFRONT], gcol[:, k:k + 1], ALU.mult, [TWB, Tgin], [TWB])
    xt = [wmem.alloc(F32, [D]) for _ in range(4)]

    work.reset()
    NXS = 4
    Txt = [T("xt%d" % i) for i in range(NXS)]
    junk = [work.alloc(BF16, [D]) for _ in range(2)]
    Tjunk = [T("junk%d" % i) for i in range(2)]
    ssx = work.alloc(F32, [4]); Tssx = T("ssx")
    nty = work.alloc(F32, [4]); Tnty = T("nty")
    ntt = work.alloc(F32, [4]); Tntt = T("ntt")
    hn = [work.alloc(BF16, [D]) for _ in range(2)]
    Thn = [T("hn%d" % i) for i in range(2)]
    hnTb = [work.alloc(BF16, [8, G]) for _ in range(2)]
    ThnTb = [[T("hnT%d_%d" % (s_, i)) for i in range(4)] for s_ in range(2)]
    zs = work.alloc(F32, [5, G]); Tzs = [T("zs%d" % i) for i in range(5)]
    sq = work.alloc(BF16, [5, G]); Tsq = [T("sq%d" % i) for i in range(5)]
    ssn = work.alloc(F32, [2 * G]); Tssn = T("ssn")
    nry = work.alloc(F32, [2 * G]); Tnry = T("nry")
    nrt = work.alloc(F32, [2 * G]); Tnrt = T("nrt")
    cTb = [work.alloc(BF16, [5, G]) for _ in range(2)]
    TcTb = [[T("cT%d_%d" % (s_, i)) for i in range(5)] for s_ in range(2)]
    cs = [work.alloc(F32, [G]) for _ in range(2)]
    sn = [work.alloc(F32, [G]) for _ in range(2)]
    Tcs = [T("cs%d" % i) for i in range(2)]
    r1 = [work.alloc(BF16, [G]) for _ in range(2)]
    r2 = [work.alloc(BF16, [G]) for _ in range(2)]
    Tr1 = [T("r1_%d" % i) for i in range(2)]
    Tr2 = [T("r2_%d" % i) for i in range(2)]
    wu = work.alloc(BF16, [4, G]); Twu = [T("wu%d" % c) for c in range(4)]
    pfx = work.alloc(F32, [16]); Tpfx = T("pfx")
    qst = work.alloc(BF16, [H, G]); Tqst = [T("qst%d" % h) for h in range(H)]
    kst = work.alloc(BF16, [H, G]); Tkst = [T("kst%d" % h) for h in range(H)]; Tkr = T("kstrope")
    vst = work.alloc(BF16, [4, 512]); Tvst = [T("vst%d" % i) for i in range(4)]
    _th = work.alloc(BF16, [2, G]); _Tth = [T("th%d" % i) for i in range(2)]

    class _Rot:
        def __init__(self, a, n): self.a, self.n = a, n
        def __getitem__(self, i): return self.a[i % self.n]
    Tth = _Rot(_Tth, 2)

    class _RotAP:
        def __getitem__(self, key):
            p, c, f = key
            return _th[p, c % 2, f]
    th = _RotAP()
    sgst = work.alloc(BF16, [4, G]); Tsgst = [T("sgst%d" % i) for i in range(4)]
    sgp = work.alloc(BF16, [4, G]); Tsgp = [T("sgp%d" % i) for i in range(4)]
    ub = work.alloc(F32, [4, 16 + G])
    Tub = [T("ub_%d" % c) for c in range(4)]
    Tubh = [T("ubh_%d" % c) for c in range(4)]
    ubs = work.alloc(F32, [4, 16]); Tubs = [T("ubs%d" % c) for c in range(4)]
    pa = work.alloc(F32, [16 + G]); Tpa = T("pa")
    pbuf = work.alloc(F32, [16 + G]); Tpb = T("pbuf")
    dT = work.alloc(BF16, [4, G]); TdT = [T("dT%d" % i) for i in range(4)]
    ypst = work.alloc(BF16, [4, G]); Typst = [T("ypst%d" % i) for i in range(4)]

    Tqt_d = [[T("qtd%d_%d" % (h, g)) for g in range(NG)] for h in range(H)]
    Tkt_d = [[T("ktd%d_%d" % (h, g)) for g in range(NG)] for h in range(H)]
    Tv_d = [T("vd%d" % g) for g in range(NG)]
    Tsg_d = [T("sgd%d" % g) for g in range(NG)]
    Typ_d = [T("ypd%d" % g) for g in range(NG)]
    To_d = [[T("od%d_%d" % (h, g)) for g in range(NG)] for h in range(H)]
    Tl_d = [[T("ld%d_%d" % (h, g)) for g in range(NG)] for h in range(H)]

    for c in range(4):
        memset("pool", ubs[:, c, :], 0.0, [Tubs[c]])

    def load_x(t):
        xs = t % 4
        dma("sp", xt[xs], x_d[t * 128:(t + 1) * 128, :], "x%d" % xs, writes=[Txt[xs]])

    cur_g = [0]

    def zchunk(col0, ncols, b, lhs_src=None):
        hnT, ThnT = hnTb[cur_g[0] % 2], ThnTb[cur_g[0] % 2]
        for k in range(8):
            lhsT = w_f[:, k, col0:col0 + ncols] if lhs_src is None else lhs_src[:, k, :]
            mm(bank(b, G, ncols), lhsT, hnT[:, k, :], k == 0, k == 7,
               [TWA if (lhs_src is not None or col0 < OFF_GA) else TWB] + ThnT, [PB[b]])

    def P1a(g):
        for j in range(4):
            act(junk[j % 2], xt[j], AF.Square, [Txt[j]], [Tjunk[j % 2], Tssx], accum=ssx[:, j:j + 1])
        ts("dve", ssx, ssx, float(D * EPS), ALU.add, [Tssx], [Tssx])
        rsqrt_newton(ssx, nty, ntt, Tssx, Tnty, Tntt, 3)
        cp("dve", rstd_all[:, 4 * g:4 * g + 4], nty, [Tnty], [Trstd[g]])

    def P1b_scale(g, j):
        t = 4 * g + j
        hs = j % 2
        prog.op("act", lambda e, o=hn[hs], i=xt[j], sc=rstd_all[:, t:t + 1]: e.activation(out=o, in_=i, func=AF.Copy, scale=sc),
                reads=[Txt[j], Trstd[g]], writes=[Thn[hs]])
        if t + 4 < NT:
            load_x(t + 4)

    def P1b_tr(g, j):
        hs = j % 2
        hnT, ThnT = hnTb[g % 2], ThnTb[g % 2]
        b = next_bank()
        for k in range(8):
            prog.op("pe", lambda e, o=psum16[:, b * 1024 + k * 128:b * 1024 + (k + 1) * 128],
                    i=hn[hs][:, k * 128:(k + 1) * 128]: e.transpose(out=o, in_=i, identity=ident),
                    reads=[Thn[hs], Tident], writes=[PB[b]])
        cp("act", hnT[:, :, j * 128:(j + 1) * 128],
           psum16[:, b * 1024:(b + 1) * 1024].rearrange("p (k c) -> p k c", c=128),
           [PB[b]], [ThnT[j]])

    def P1b_steps(g):
        P1b_scale(g, 0)
        yield
        for j in range(4):
            if j + 1 < 4:
                P1b_scale(g, j + 1)
            P1b_tr(g, j)
            yield

    def drain(gen):
        if gen is not None:
            for _ in gen:
                pass

    def step(gen):
        if gen is not None:
            next(gen, None)

    def P2(g):
        cur_g[0] = g
        cT, TcT = cTb[g % 2], TcTb[g % 2]
        for c in range(5):
            b = next_bank()
            zchunk(c * 128, 128, b)
            act(sq[:, c, :], bank(b), AF.Square, [PB[b]], [Tsq[c]])
            cp("act", zs[:, c, :], bank(b), [PB[b]], [Tzs[c]])
        bq = next_bank()
        for c in range(3):
            mm(bank(bq), ones_bf, sq[:, c, :], c == 0, c == 2, [Tones, Tsq[c]], [PB[bq]])
        bk = next_bank()
        for c in range(2):
            mm(bank(bk), ones_bf, sq[:, 3 + c, :], c == 0, c == 1, [Tones, Tsq[3 + c]], [PB[bk]])
        ts("dve", ssn[:, 0:G], bank(bq), float(QL * EPS), ALU.add, [PB[bq]], [Tssn])
        ts("dve", ssn[:, G:2 * G], bank(bk), float(KVL * EPS), ALU.add, [PB[bk]], [Tssn])

    def P2b(g):
        cT, TcT = cTb[g % 2], TcTb[g % 2]
        rsqrt_newton(ssn, nry, nrt, Tssn, Tnry, Tnrt, 2)
        for c in range(3):
            stt(cT[:, c, :], zs[:, c, :], qn[:, c:c + 1], nry[:, 0:G], ALU.mult, ALU.mult,
                [Tzs[c], Tvec, Tnry], [TcT[c]])
        for c in range(2):
            stt(cT[:, 3 + c, :], zs[:, 3 + c, :], kvn[:, c:c + 1], nry[:, G:2 * G], ALU.mult, ALU.mult,
                [Tzs[3 + c], Tvec, Tnry], [TcT[3 + c]])

    def P3(g):
        cT, TcT = cTb[g % 2], TcTb[g % 2]
        cslot = g % 2
        for h in range(H):
            ba = next_bank()
            for k in range(3):
                mm(bank(ba, G, 96), w_q[:, k, h, :], cT[:, k, :], k == 0, k == 2, [TW1, TcT[k]], [PB[ba]])
            bb = next_bank()
            for k in range(3):
                mm(bank(bb, G, 96), w_qs[:, k, h, :], cT[:, k, :], k == 0, k == 2, [TW1, TcT[k]], [PB[bb]])
            cp("act", qst[0:64, h, :], psum[0:64, ba * 512:ba * 512 + G], [PB[ba]], [Tqst[h]])
            rs = h % 2
            tt("dve", r1[rs][64:96, :], psum[64:96, ba * 512:ba * 512 + G], cs[cslot][64:96, :], ALU.mult, [PB[ba], Tcs[cslot]], [Tr1[rs]])
            tt("dve", r2[rs][64:96, :], psum[64:96, bb * 512:bb * 512 + G], sn[cslot][64:96, :], ALU.mult, [PB[bb], Tcs[cslot]], [Tr2[rs]])
            tt("pool", qst[64:96, h, :], r1[rs][64:96, :], r2[rs][64:96, :], ALU.add, [Tr1[rs], Tr2[rs]], [Tqst[h]])
        dma("sp", qt_d[:, :, g * G:(g + 1) * G].rearrange("h r t -> r h t"), qst[0:96, :, :], "qst",
            reads=Tqst, writes=[Tqt_d[h][g] for h in range(H)])
        for h in range(H):
            b = next_bank()
            for k in range(2):
                mm(bank(b, G, 64), w_kv[:, k, h * 128:h * 128 + 64], cT[:, 3 + k, :], k == 0, k == 1,
                   [TW1, TcT[3 + k]], [PB[b]])
            cp("act", kst[0:64, h, :], psum[0:64, b * 512:b * 512 + G], [PB[b]], [Tkst[h]])
        dma("sp", kt_d[:, :, g * G:(g + 1) * G].rearrange("h r t -> r h t"), kst[0:96, :, :], "kst",
            reads=Tkst, writes=[Tkt_d[h][g] for h in range(H)])
        wv = w_kv.rearrange("p k (h c) -> p k h c", c=128)
        for j in range(4):
            b = next_bank()
            for k in range(2):
                mm(bank(b).rearrange("p (h c) -> p h c", c=64), cT[:, 3 + k, j * 128:(j + 1) * 128],
                   wv[:, k, :, 64:128], k == 0, k == 1, [TW1, TcT[3 + k]], [PB[b]])
            cp("act", vst[:, j, :], bank(b), [PB[b]], [Tvst[j]])
        dma("sp", v_d[g * G:(g + 1) * G, :].rearrange("(j p) c -> p j c", p=128), vst, "vst",
            reads=Tvst, writes=[Tv_d[g]])

    def P4a(g, nxt=None):
        cur_g[0] = g
        cslot = g % 2
        b1 = next_bank()
        zchunk(OFF_ZKR - 64, 96, b1)
        b2 = next_bank()
        zchunk(0, 96, b2, lhs_src=w_krs)
        tt("dve", r1[1][64:96, :], psum[64:96, b1 * 512:b1 * 512 + G], cs[cslot][64:96, :], ALU.mult, [PB[b1], Tcs[cslot]], [Tr1[1]])
        tt("dve", r2[1][64:96, :], psum[64:96, b2 * 512:b2 * 512 + G], sn[cslot][64:96, :], ALU.mult, [PB[b2], Tcs[cslot]], [Tr2[1]])
        tt("pool", kst[64:96, :, :], r1[1][64:96, :].unsqueeze(1).broadcast_to([32, H, G]),
           r2[1][64:96, :].unsqueeze(1).broadcast_to([32, H, G]), ALU.add, [Tr1[1], Tr2[1]], Tkst)
        step(nxt)
        for c in range(4):
            b = next_bank()
            zchunk(OFF_GA + c * 128, 128, b)
            act(th[:, c, :], bank(b), AF.Tanh, [PB[b]], [Tth[c]], scale=0.5)
            stt(sgst[:, c, :], th[:, c, :], 1.0, bank(b), ALU.add, ALU.mult, [Tth[c], PB[b]], [Tsgst[c]])
            if c % 2 == 1:
                step(nxt)
        dma("sp", sg_d[:, g * G:(g + 1) * G].rearrange("(c p) t -> p c t", p=128), sgst, "sgst",
            reads=Tsgst, writes=[Tsg_d[g]])

    def P4b(g, nxt=None):
        cur_g[0] = g
        for c in range(4):
            cp("pool", ub[:, c, 0:16], ubs[:, c, :], [Tubs[c]], [Tubh[c]])
            b = next_bank()
            zchunk(OFF_UP + c * 128, 128, b)
            cp("act", ub[:, c, 16:16 + G], bank(b), [PB[b]], [Tub[c]])
            act(wu[:, c, :], bank(b), AF.Copy, [PB[b]], [Twu[c]], scale=float(POOL_W[c]))
            if c % 2 == 1:
                step(nxt)
        for c in range(4):
            b = next_bank()
            zchunk(OFF_GP + c * 128, 128, b)
            act(th[:, 4 + c, :], bank(b), AF.Tanh, [PB[b]], [Tth[4 + c]], scale=0.5)
            stt(sgp[:, c, :], th[:, 4 + c, :], 1.0, bank(b), ALU.add, ALU.mult, [Tth[4 + c], PB[b]], [Tsgp[c]])

    def P4p(g):
        W = 16 + G
        for c in range(4):
            src, Tsrc = ub[:, c, :], Tub[c]
            cur, Tcur = src, Tsrc
            lo = 0
            dsts = [(pa, Tpa), (pbuf, Tpb)]
            for lvl in range(c + 1):
                sh = 1 << lvl
                dst, Tdst = dsts[lvl % 2]
                nlo = lo + sh
                tt("pool", dst[:, nlo:W], cur[:, nlo:W], cur[:, nlo - sh:W - sh], ALU.add,
                   [Tcur] + ([Tubh[c]] if lvl == 0 else []), [Tdst])
                cur, Tcur, lo = dst, Tdst, nlo
            tt("pool", dT[:, c, :], cur[:, 16:W], wu[:, c, :], ALU.subtract, [Tcur, Twu[c]], [TdT[c]])
            if g == 0:
                tt("pool", pfx, cur[:, 16:32], pcnt[:, c, :], ALU.mult, [Tcur, Tpcnt], [Tpfx])
                tt("pool", dT[:, c, 0:16], pfx, wu[:, c, 0:16], ALU.subtract, [Tpfx, Twu[c]], [TdT[c]])
            cp("pool", ubs[:, c, :], src[:, G:G + 16], [Tsrc], [Tubs[c]])

    def P4c(g):
        for c in range(4):
            b = next_bank()
            mm(bank(b), w_pl[:, c, :], dT[:, c, :], True, True, [TW1, TdT[c]], [PB[b]])
            stt(ypst[:, c, :], bank(b), psc[:, c:c + 1], sgp[:, c, :], ALU.mult, ALU.mult,
                [PB[b], Tvec, Tsgp[c]], [Typst[c]])
        dma("sp", yp_d[:, g * G:(g + 1) * G].rearrange("(c p) t -> p c t", p=128), ypst, "ypst",
            reads=Typst, writes=[Typ_d[g]])

    for t in range(min(4, NT)):
        load_x(t)
    for g in range(NG):
        cslot = g % 2
        dma("sp", cs[cslot][64:96, :], cos_d[:, g * G:(g + 1) * G], "cs%d" % cslot, writes=[Tcs[cslot]])
        dma("sp", sn[cslot][64:96, :], sin_d[:, g * G:(g + 1) * G], "cs%d" % cslot, writes=[Tcs[cslot]])
        if g == 0:
            P1a(0)
            drain(P1b_steps(0))
        P2(g)
        if g == 0:
            fold_wB()
        if g > 0:
            P3(g - 1)
            P4c(g - 1)
        nxt = None
        if g + 1 < NG:
            P1a(g + 1)
            nxt = P1b_steps(g + 1)
        P4a(g, nxt)
        P2b(g)
        P4b(g, nxt)
        P4p(g)
        drain(nxt)
    P3(NG - 1)
    P4c(NG - 1)

    if stop >= 2:
        prog.barrier()
        wmem.reset()
        w_m = wmem.alloc(BF16, [8, 2048])
        w_ba = wmem.alloc(BF16, [4, D])
        w_bp = wmem.alloc(BF16, [4, D])
        w_o = wmem.alloc(BF16, [8, D])
        TW3 = T("w3")
        work.reset()
        KTb = [work.alloc(BF16, [S]) for _ in range(2)]
        TKT = [T("KT%d" % i) for i in range(2)]
        VAb = [work.alloc(BF16, [NT, 65]) for _ in range(2)]
        TVA = [T("VA%d" % i) for i in range(2)]
        QTb = [work.alloc(BF16, [G]) for _ in range(3)]
        TQT = [T("QT%d" % i) for i in range(3)]
        NPT = 6
        PT = [work.alloc(BF16, [2 * G]) for _ in range(NPT)]
        TPT = [T("PT%d" % i) for i in range(NPT)]
        ost = [work.alloc(F32, [G]) for _ in range(2)]
        Tost = [T("ost%d" % i) for i in range(2)]
        for i in range(2):
            memset("pool", VAb[i][:, :, 64:65], 1.0, [TVA[i]])

        import os as _os4
        NDUMMY = int(_os4.environ.get("KDUMMY", "0"))
        SU = [(0, 1), (2, 3), (4, 5)]
        FILL_N = int(_os4.environ.get("KFILLN", "128"))
        su_rr = [0]
        pt_rr = [0]

        def load_head(h, s):
            dma("pool", KTb[s][0:96, :], kt_d[h, :, :], "KT%d" % s,
                reads=[Tkt_d[h][g] for g in range(NG)], writes=[TKT[s]])
            dma("pool", VAb[s][:, :, 0:64], v_d[:, h * 64:(h + 1) * 64].rearrange("(t p) c -> p t c", p=128),
                "VA%d" % s, reads=Tv_d, writes=[TVA[s]])

        def load_q(h, g, s):
            dma("sp", QTb[s][0:96, :], qt_d[h, :, g * G:(g + 1) * G], "QT%d" % s,
                reads=[Tqt_d[h][g]], writes=[TQT[s]])

        hq = [(h, g) for h in range(H) for g in range(NG)]
        allu = []
        for idx, (h, g) in enumerate(hq):
            units = []
            for p in range(2 * g):
                units.append(([2 * p, 2 * p + 1], [(0, 0), (G, 0)], False))
            units.append(([4 * g, 4 * g + 1], [(0, 0), (G, 128)], True))
            units.append(([4 * g + 2, 4 * g + 3], [(256, 256), (G, 384)], True))
            for ui, (kts, q0, pvq) in enumerate(units):
                allu.append((idx, kts, q0, ui == 0, ui == len(units) - 1, pvq))
        def emit_scores(i):
            idx, kts, q0, first, last, pvq = allu[i]
            h, g = hq[idx]
            hs, qs = h % 2, idx % 3
            if first:
                if idx >= 2:
                    w3_issue(1 if NG * H >= 40 else 3)
                if idx >= 1 and idx + 2 < len(hq):
                    load_q(hq[idx + 2][0], hq[idx + 2][1], (idx + 2) % 3)
            sb = SU[su_rr[0]]
            su_rr[0] = (su_rr[0] + 1) % len(SU)
            for k_, kt in enumerate(kts):
                b = sb[k_]
                rc, qk = q0[k_]
                mm(psum[:, sb[0] * 512 + rc:sb[0] * 512 + rc + G - qk], KTb[hs][0:96, kt * 128:(kt + 1) * 128],
                   QTb[qs][0:96, qk:G], True, True, [TKT[hs], TQT[qs]], [PB[b]])
            return sb

        def emit_exp_pv(i, sb):
            idx, kts, q0, first, last, pvq = allu[i]
            h, g = hq[idx]
            hs, qs = h % 2, idx % 3
            ob_ = 6 + idx % 2
            if first and g == 0 and h + 1 < H:
                load_head(h + 1, 1 - hs)
            pi = pt_rr[0]
            pt_rr[0] = (pi + 1) % NPT
            lo = q0[0][0]
            hi = q0[-1][0] + G - q0[-1][1]
            act(PT[pi][:, lo:hi], psum[:, sb[0] * 512 + lo:sb[0] * 512 + hi], AF.Exp,
                [PB[sb[0]], PB[sb[1]]], [TPT[pi]], scale=SM_SCALE)
            if pvq:
                for k_ in range(2):
                    memset("dve", PT[pi][64:128, q0[k_][0]:q0[k_][0] + 64], 0.0, [TPT[pi]])
            for k_, kt in enumerate(kts):
                rc, qk = q0[k_]
                mm(psum[0:65, ob_ * 512 + qk:ob_ * 512 + G], VAb[hs][:, kt, :], PT[pi][:, rc:rc + G - qk],
                   first and k_ == 0, last and k_ == len(kts) - 1, [TVA[hs], TPT[pi]], [PB[ob_]])
            for _ in range(NDUMMY):
                mm(bank(7, FILL_N), KTb[hs][0:96, 0:128], QTb[qs][0:96, 0:FILL_N], True, True, [TKT[hs], TQT[qs]], [PB[7]])
            if last:
                os_ = idx % 2
                cp("dve", ost[os_][0:65, :], psum[0:65, ob_ * 512:ob_ * 512 + G], [PB[ob_]], [Tost[os_]])
                dma("sp", o_d[h * 64:(h + 1) * 64, g * G:(g + 1) * G], ost[os_][0:64, :], "ost%d" % os_,
                    reads=[Tost[os_]], writes=[To_d[h][g]])
                dma("sp", l_d[h:h + 1, g * G:(g + 1) * G], ost[os_][64:65, :], "ost%d" % os_,
                    reads=[Tost[os_]], writes=[Tl_d[h][g]])
                if fold_todo and not w3_jobs and idx >= FOLD0:
                    k = fold_todo.pop(0)
                    ts("dve", w_m[:, k, :], w_m[:, k, :], gcol[:, k:k + 1], ALU.mult, [TW3, Tgin], [TW3])

        load_head(0, 0)
        load_q(0, 0, 0)
        for i_ in range(1, min(3, len(hq))):
            load_q(hq[i_][0], hq[i_][1], i_)
        w3_jobs = []
        for k in range(8):
            w3_jobs.append((w_m[:, k, :], w_in_v[:, k, OFF_GM:OFF_GM + 2048]))
        w3_jobs.append((w_ba, w_ba_d.rearrange("(k p) c -> p k c", p=128)))
        w3_jobs.append((w_bp, w_bp_d.rearrange("(k p) c -> p k c", p=128)))
        for k in range(8):
            w3_jobs.append((w_o[:, k, :], w_out_d.rearrange("(k p) c -> p k c", p=128)[:, k, :]))

        def w3_issue(n):
            for _ in range(n):
                if w3_jobs:
                    o_, i_ = w3_jobs.pop(0)
                    dma("pool", o_, i_, "w3", writes=[TW3], group=True)

        LA = 2
        fold_todo = list(range(8))
        FOLD0 = (len(hq) * 5) // 8
        pendq = [emit_scores(i) for i in range(min(LA, len(allu)))]
        for i in range(len(allu)):
            if i + LA < len(allu):
                pendq.append(emit_scores(i + LA))
            emit_exp_pv(i, pendq.pop(0))
        w3_issue(len(w3_jobs))
        for k in fold_todo:
            ts("dve", w_m[:, k, :], w_m[:, k, :], gcol[:, k:k + 1], ALU.mult, [TW3, Tgin], [TW3])

    if stop >= 3:
        prog.barrier()
        work.reset()
        NXS3 = 5
        xt = [work.alloc(F32, [D]) for _ in range(NXS3)]
        Txt = [T("x3t%d" % i) for i in range(NXS3)]
        hn = [work.alloc(BF16, [D]) for _ in range(2)]
        Thn = [T("h3n%d" % i) for i in range(2)]
        hnT = work.alloc(BF16, [8, G]); ThnT = [T("h3nT%d" % i) for i in range(4)]
        tg = work.alloc(BF16, [16, G]); Ttg = [T("tg%d" % i) for i in range(16)]
        oin = [work.alloc(F32, [4, G]) for _ in range(2)]; Toin = [T("oin%d" % i) for i in range(2)]
        lin = work.alloc(F32, [4, G]); Tlin = [T("lin%d" % i) for i in range(4)]
        sgin = [work.alloc(BF16, [4, G]) for _ in range(2)]; Tsgin = [T("sgin%d" % i) for i in range(2)]
        ypin = [work.alloc(BF16, [4, G]) for _ in range(2)]; Typin = [T("ypin%d" % i) for i in range(2)]
        ya = work.alloc(BF16, [4, G]); Tya = [T("ya%d" % i) for i in range(4)]
        m1 = [work.alloc(F32, [G]) for _ in range(2)]; Tm1 = [T("m1_%d" % i) for i in range(2)]
        m2 = [work.alloc(F32, [G]) for _ in range(2)]; Tm2 = [T("m2_%d" % i) for i in range(2)]
        mg = work.alloc(BF16, [8, G]); Tmg = [T("mg%d" % i) for i in range(8)]
        hh = [work.alloc(F32, [D]) for _ in range(4)]; Thh = [T("hh%d" % i) for i in range(4)]
        junk3 = [work.alloc(BF16, [D])] * 4; Tjunk3 = [T("j3")] * 4
        ss3 = work.alloc(F32, [4]); Tss3 = T("ss3")
        ny3 = work.alloc(F32, [4]); Tny3 = T("ny3")
        nt3 = work.alloc(F32, [4]); Tnt3 = T("nt3")

        import os as _os2
        if _os2.environ.get("KDBG"):
            print("DBG phase3 work used", work.off - work.lo, "of", work.hi - work.lo)

        def load_g3(g, s):
            dma("sp", oin[s], o_d[:, g * G:(g + 1) * G].rearrange("(c p) t -> p c t", p=128), "oin%d" % s,
                reads=[To_d[h][g] for h in range(H)], writes=[Toin[s]])
            dma("sp", sgin[s], sg_d[:, g * G:(g + 1) * G].rearrange("(c p) t -> p c t", p=128), "sgin%d" % s,
                reads=[Tsg_d[g]], writes=[Tsgin[s]])
            dma("sp", ypin[s], yp_d[:, g * G:(g + 1) * G].rearrange("(c p) t -> p c t", p=128), "ypin%d" % s,
                reads=[Typ_d[g]], writes=[Typin[s]])

        def load_x3(t):
            xs = t % NXS3
            dma("sp", xt[xs], x_d[t * 128:(t + 1) * 128, :], "x3_%d" % xs, writes=[Txt[xs]])

        def load_lin(g):
            for hh_ in range(H):
                dma("sp", lin[(hh_ % 2) * 64:(hh_ % 2) * 64 + 64, hh_ // 2, :],
                    l_d[hh_, g * G:(g + 1) * G].partition_broadcast(64), "lin%d" % (hh_ // 2),
                    reads=[Tl_d[hh_][g]], writes=[Tlin[hh_ // 2]])

        def stageA3(g, j):
            t = 4 * g + j
            xs = t % NXS3
            ts("dve", hn[j % 2], xt[xs], rstd_all[:, t:t + 1], ALU.mult, [Txt[xs], Trstd[g]], [Thn[j % 2]])
            b = next_bank()
            for k in range(8):
                prog.op("pe", lambda e, o=psum16[:, b * 1024 + k * 128:b * 1024 + (k + 1) * 128],
                        i=hn[j % 2][:, k * 128:(k + 1) * 128]: e.transpose(out=o, in_=i, identity=ident),
                        reads=[Thn[j % 2], Tident], writes=[PB[b]])
            cp("act", hnT[:, :, j * 128:(j + 1) * 128],
               psum16[:, b * 1024:(b + 1) * 1024].rearrange("p (k c) -> p k c", c=128),
               [PB[b]], [ThnT[j]])

        load_g3(0, 0)
        for t in range(min(NXS3, NT)):
            load_x3(t)
        for g in range(NG):
            s = g % 2
            if g + 1 < NG:
                load_g3(g + 1, 1 - s)
            if g == 0:
                for j in range(4):
                    stageA3(0, j)
            for c in range(16):
                b = next_bank()
                for k in range(8):
                    mm(bank(b), w_m[:, k, c * 128:(c + 1) * 128], hnT[:, k, :], k == 0, k == 7, [TW3] + ThnT, [PB[b]])
                act(tg[:, c, :], bank(b), AF.Tanh, [PB[b]], [Ttg[c]], scale=0.5)
            if g == 0:
                load_lin(0)
            for c in range(4):
                tt("pool", oin[s][:, c, :], oin[s][:, c, :], sgin[s][:, c, :], ALU.mult, [Toin[s], Tsgin[s]], [Toin[s]])
                prog.op("dve", lambda e, o=lin[:, c, :]: e.reciprocal(out=o, in_=o), reads=[Tlin[c]], writes=[Tlin[c]])
                stt(ya[:, c, :], oin[s][:, c, :], 0.5, lin[:, c, :], ALU.mult, ALU.mult, [Toin[s], Tlin[c]], [Tya[c]])
            if g + 1 < NG:
                load_lin(g + 1)
            for dc in range(8):
                ba = next_bank()
                for k in range(4):
                    mm(bank(ba), w_ba[:, k, dc * 128:(dc + 1) * 128], ya[:, k, :], k == 0, k == 3, [TW3, Tya[k]], [PB[ba]])
                bp = next_bank()
                for k in range(4):
                    mm(bank(bp), w_bp[:, k, dc * 128:(dc + 1) * 128], ypin[s][:, k, :], k == 0, k == 3, [TW3, Typin[s]], [PB[bp]])
                ms = dc % 2
                stt(m1[ms], tg[:, dc, :], 1.0, bank(ba), ALU.add, ALU.mult, [Ttg[dc], PB[ba]], [Tm1[ms]])
                stt(m2[ms], tg[:, 8 + dc, :], 1.0, bank(bp), ALU.add, ALU.mult, [Ttg[8 + dc], PB[bp]], [Tm2[ms]])
                tt("pool", mg[:, dc, :], m1[ms], m2[ms], ALU.add, [Tm1[ms], Tm2[ms]], [Tmg[dc]])
            for j in range(4):
                t = 4 * g + j
                xs = t % NXS3
                hs = j
                b0 = next_bank()
                b1 = next_bank()
                for half, b in ((0, b0), (1, b1)):
                    for k in range(8):
                        mm(bank(b), mg[:, k, j * 128:(j + 1) * 128], w_o[:, k, half * 512:(half + 1) * 512],
                           k == 0, k == 7, [TW3, Tmg[k]], [PB[b]])
                if g + 1 < NG and j >= 1:
                    stageA3(g + 1, j - 1)
                for half, b in ((0, b0), (1, b1)):
                    stt(hh[hs][:, half * 512:(half + 1) * 512], bank(b), 0.5, xt[xs][:, half * 512:(half + 1) * 512],
                        ALU.mult, ALU.add, [PB[b], Txt[xs]], [Thh[hs]])
                if t + NXS3 < NT:
                    load_x3(t + NXS3)
                act(junk3[hs], hh[hs], AF.Square, [Thh[hs]], [Tjunk3[hs], Tss3], accum=ss3[:, j:j + 1])
                tt("pool", hh[hs], hh[hs], gfin, ALU.mult, [Thh[hs], Tgfin], [Thh[hs]])
            if g + 1 < NG:
                stageA3(g + 1, 3)
            ts("dve", ss3, ss3, float(D * EPS), ALU.add, [Tss3], [Tss3])
            rsqrt_newton(ss3, ny3, nt3, Tss3, Tny3, Tnt3, 3)
            for j in range(4):
                t = 4 * g + j
                prog.op("act", lambda e, o=hh[j], sc=ny3[:, j:j + 1]: e.activation(out=o, in_=o, func=AF.Copy, scale=sc),
                        reads=[Thh[j], Tny3], writes=[Thh[j]])
                dma("sp", out_d[t * 128:(t + 1) * 128, :], hh[j], "ob%d" % j, reads=[Thh[j]], writes=[T("outd")])

    import os as _os
    _cut = int(_os.environ.get("KCUT", "0"))
    if _cut:
        prog.ins = prog.ins[:_cut]
    prog.ins.append(dict(eng="sp", fn=None, deps=set(range(len(prog.ins))), dma=None, signal=False))

    totals = prog.resolve()
    with ExitStack() as es:
        sems = {}
        for k in totals:
            sems[k] = es.enter_context(nc.semaphore("s_%s_%s" % (k[0], k[1])))
        es.enter_context(nc.allow_low_precision("bf16 matmul operands, fp32 accumulation (problem tolerance calibrated for bf16)"))
        block = es.enter_context(nc.Block())

        def run(engname):
            def f(e):
                for it in prog.ins:
                    if it["eng"] != engname:
                        continue
                    for k, v in it["waits"]:
                        e.wait_ge(sems[k], v)
                    if it["fn"] is None:
                        continue
                    ins = it["fn"](e)
                    if it["signal"]:
                        k, v = it["sig"]
                        ins.then_inc(sems[k], 16 if k[0] == "dma" else 1)
            return f

        block.sync(run("sp"))
        block.tensor(run("pe"))
        block.scalar(run("act"))
        block.vector(run("dve"))
        block.gpsimd(run("pool"))
    return nc


def host_consts(S):
    half = ROPE // 2
    inv_freq = (np.float32(10000.0) ** (-np.arange(half, dtype=np.float32) / np.float32(half))).astype(np.float32)
    ang = (np.arange(S, dtype=np.float32)[:, None] * inv_freq[None, :]).astype(np.float32)
    cos = np.cos(ang.astype(np.float64)).astype(np.float32).T
    sin = np.sin(ang.astype(np.float64)).astype(np.float32).T
    cosT = np.concatenate([cos, cos], axis=0)
    sinT = np.concatenate([-sin, sin], axis=0)
    pcnt = np.zeros((4, 16), np.float32)
    for gi, w in enumerate(POOL_W):
        pcnt[gi] = float(w) / np.minimum(np.arange(16) + 1, w)
    return {
        "c_ident": np.eye(128, dtype=np.float32),
        "c_cos": np.ascontiguousarray(cosT),
        "c_sin": np.ascontiguousarray(sinT),
        "c_pcnt": pcnt,
    }


_NC_CACHE = {}


def kernel(x, norm_in, w_in, q_norm, w_uq, kv_norm, w_ukv, pool_w, pool_scale,
           w_branch_attn, w_branch_pool, w_out, norm_final):
    x = np.asarray(x, dtype=np.float32)
    B, S, _ = x.shape
    if S not in _NC_CACHE:
        _NC_CACHE[S] = build_nc(S)
    nc = _NC_CACHE[S]
    consts = host_consts(S)
    shared = {
        "norm_in": np.asarray(norm_in, np.float32), "w_in": np.asarray(w_in, np.float32),
        "q_norm": np.asarray(q_norm, np.float32), "w_uq": np.asarray(w_uq, np.float32),
        "kv_norm": np.asarray(kv_norm, np.float32), "w_ukv": np.asarray(w_ukv, np.float32),
        "pool_w": np.asarray(pool_w, np.float32), "pool_scale": np.asarray(pool_scale, np.float32),
        "w_branch_attn": np.asarray(w_branch_attn, np.float32),
        "w_branch_pool": np.asarray(w_branch_pool, np.float32),
        "w_out": np.asarray(w_out, np.float32), "norm_final": np.asarray(norm_final, np.float32),
    }
    shared.update(consts)
    in_maps = []
    for b in range(B):
        m = dict(shared)
        m["x"] = np.ascontiguousarray(x[b])
        in_maps.append(m)
    res = run_bass_kernel_spmd(nc, in_maps, core_ids=list(range(B)))
    return np.stack([np.asarray(r["out"], dtype=np.float32) for r in res.results], axis=0)
```
